# Optimizing a Trainium2 kernel written in Bass

```python
import math
import jax, jax.numpy as jnp
from jax import lax
import numpy as np

D_MODEL = 1024
BATCH = 8
SEQ = 4096
DEPTH = 2

MEM_LEN = 256
D_FF = 2816
LN_EPS = 1e-5
RMS_EPS = 1e-6
ROPE_THETA = 10000.0
DEEPNORM_ALPHA = (2 * DEPTH) ** 0.25
DEEPNORM_BETA = (8 * DEPTH) ** -0.25
Q_BLOCK = 128
NEG_INF = -1e30
N_BRANCHES = 4

GMLP_CHUNK = 128
GMLP_GROUPS = 4
GMLP_WIDTH = 512
GMLP_GROUP_DIM = GMLP_WIDTH // GMLP_GROUPS

CONV_WIDTH = 512
CONV_K = 3

MLA_HEADS = 8
MLA_Q_RANK = 256
MLA_KV_RANK = 128
MLA_NOPE = 64
MLA_ROPE = 32
MLA_V = 64

NSA_HEADS = 8
NSA_KV_GROUPS = 2
NSA_HPG = NSA_HEADS // NSA_KV_GROUPS
NSA_DIM = 64
NSA_KV_WIDTH = NSA_KV_GROUPS * NSA_DIM
CMP_BLOCK = 32
CMP_STRIDE = 16
SLC_BLOCK = 64
SLC_TOPK = 8
WINDOW = 512

XATTN_HEADS = 4
XATTN_DIM = 128

IN_SPLITS = (GMLP_WIDTH, GMLP_WIDTH,
             CONV_WIDTH, CONV_WIDTH, CONV_WIDTH,
             MLA_Q_RANK, MLA_KV_RANK, MLA_ROPE,
             NSA_HEADS * NSA_DIM) + (NSA_KV_WIDTH,) * 6 + (NSA_HEADS * 3,) + (D_MODEL,) * N_BRANCHES
IN_WIDTH = sum(IN_SPLITS)

kernel_name = 'hybrid_gated_parallel_mixer_deepnorm'


def _layer_norm(x, g, b):
    xf = x.astype(jnp.float32)
    mu = jnp.mean(xf, -1, keepdims=True)
    var = jnp.mean(jnp.square(xf - mu), -1, keepdims=True)
    return ((xf - mu) * lax.rsqrt(var + LN_EPS) * g + b).astype(x.dtype)


def _rms_norm(x, g):
    xf = x.astype(jnp.float32)
    return (xf * lax.rsqrt(jnp.mean(xf * xf, -1, keepdims=True) + RMS_EPS) * g).astype(x.dtype)


def _rope_tables(pos, dim):
    inv = ROPE_THETA ** (-(jnp.arange(0, dim, 2, dtype=jnp.float32) / dim))
    ang = pos[:, None] * inv[None, :]
    return jnp.cos(ang), jnp.sin(ang)


def _apply_rope(x, cos, sin):
    x1, x2 = jnp.split(x.astype(jnp.float32), 2, axis=-1)
    return jnp.concatenate([x1 * cos - x2 * sin, x1 * sin + x2 * cos], -1).astype(x.dtype)


def _swiglu(x, w1, w3, w2):
    return (jax.nn.silu(x @ w1) * (x @ w3)) @ w2


def _blocked_causal_attention(q, k, v, scale):
    Bsz, S, H, dk = q.shape
    nq = S // Q_BLOCK
    qb = q.reshape(Bsz, nq, Q_BLOCK, H, dk).transpose(1, 0, 2, 3, 4)
    kpos = jnp.arange(S)

    def one(args):
        qi, blk = args
        qpos = blk * Q_BLOCK + jnp.arange(Q_BLOCK)
        s = jnp.einsum('bthd,bshd->bhts', qi, k).astype(jnp.float32) * scale
        s = jnp.where(kpos[None, :] <= qpos[:, None], s, NEG_INF)
        p = jax.nn.softmax(s, axis=-1).astype(v.dtype)
        return jnp.einsum('bhts,bshd->bthd', p, v)

    o = lax.map(one, (qb, jnp.arange(nq)))
    return o.transpose(1, 0, 2, 3, 4).reshape(Bsz, S, H, v.shape[-1])


def _gmlp_branch(u, v, ln_g, ln_b, w_s, b_s, w_out):
    Bsz, S, _ = u.shape
    n_chunk = S // GMLP_CHUNK
    v = _layer_norm(v, ln_g, ln_b).reshape(Bsz, n_chunk, GMLP_CHUNK, GMLP_GROUPS, GMLP_GROUP_DIM)
    causal = jnp.tril(jnp.ones((GMLP_CHUNK, GMLP_CHUNK), dtype=bool))
    w = jnp.where(causal, w_s, 0)
    s = jnp.einsum('gts,bcsgd->bctgd', w, v) + b_s.T[:, :, None]
    return (u * s.reshape(Bsz, S, GMLP_WIDTH)) @ w_out


def _short_conv_branch(b_gate, c_gate, h, conv_w, w_out):
    S = h.shape[1]
    zp = jnp.pad(c_gate * h, ((0, 0), (CONV_K - 1, 0), (0, 0)))
    y = zp[:, 0:S] * conv_w[0]
    for k in range(1, CONV_K):
        y = y + zp[:, k:k + S] * conv_w[k]
    return (b_gate * y) @ w_out


def _mla_branch(q_lat, kv_lat, k_rope, qn_g, kvn_g, w_uq, w_ukv, w_out, pos):
    Bsz, S, _ = q_lat.shape
    cos, sin = _rope_tables(pos, MLA_ROPE)
    q = (_rms_norm(q_lat, qn_g) @ w_uq).reshape(Bsz, S, MLA_HEADS, MLA_NOPE + MLA_ROPE)
    q = jnp.concatenate([q[..., :MLA_NOPE], _apply_rope(q[..., MLA_NOPE:], cos[:, None], sin[:, None])], -1)
    kv = (_rms_norm(kv_lat, kvn_g) @ w_ukv).reshape(Bsz, S, MLA_HEADS, MLA_NOPE + MLA_V)
    k_pe = _apply_rope(k_rope, cos, sin)[:, :, None, :]
    k = jnp.concatenate([kv[..., :MLA_NOPE], jnp.broadcast_to(k_pe, (Bsz, S, MLA_HEADS, MLA_ROPE))], -1)
    v = kv[..., MLA_NOPE:]
    o = _blocked_causal_attention(q, k, v, (MLA_NOPE + MLA_ROPE) ** -0.5)
    return o.reshape(Bsz, S, MLA_HEADS * MLA_V) @ w_out


def _nsa_branch(q, k_c, v_c, k_s, v_s, k_w, v_w, gate, pe_k, pe_v, wcmp_k, wcmp_v, w_out, pos):
    Bsz, S, _ = q.shape
    G, Hg, d = NSA_KV_GROUPS, NSA_HPG, NSA_DIM
    cos, sin = _rope_tables(pos, d)
    q = _apply_rope(q.reshape(Bsz, S, NSA_HEADS, d), cos[:, None], sin[:, None])
    k_c, v_c, k_s, v_s, k_w, v_w = [t.reshape(Bsz, S, G, d) for t in (k_c, v_c, k_s, v_s, k_w, v_w)]
    k_s = _apply_rope(k_s, cos[:, None], sin[:, None])
    k_w = _apply_rope(k_w, cos[:, None], sin[:, None])
    gate = jax.nn.sigmoid(gate).reshape(Bsz, S, NSA_HEADS, 3)

    n_cmp = (S - CMP_BLOCK) // CMP_STRIDE + 1
    cmp_start = jnp.arange(n_cmp) * CMP_STRIDE
    cmp_end = cmp_start + CMP_BLOCK - 1
    cmp_idx = cmp_start[:, None] + jnp.arange(CMP_BLOCK)[None, :]
    k_cmp = jnp.einsum('bnlgd,lde->bnge', k_c[:, cmp_idx] + pe_k[:, None, :], wcmp_k)
    v_cmp = jnp.einsum('bnlgd,lde->bnge', v_c[:, cmp_idx] + pe_v[:, None, :], wcmp_v)
    ccos, csin = _rope_tables(cmp_end.astype(jnp.float32), d)
    k_cmp = _apply_rope(k_cmp, ccos[:, None], csin[:, None])

    n_slc = S // SLC_BLOCK
    slc_start = jnp.arange(n_slc) * SLC_BLOCK
    ov = (jnp.minimum(cmp_start[:, None] + CMP_BLOCK, slc_start[None, :] + SLC_BLOCK)
          - jnp.maximum(cmp_start[:, None], slc_start[None, :]))
    overlap = jnp.clip(ov, 0).astype(jnp.float32) / CMP_BLOCK
    top_k = min(SLC_TOPK, n_slc)

    ks_blk = k_s.transpose(0, 2, 1, 3).reshape(Bsz, G, n_slc, SLC_BLOCK, d)
    vs_blk = v_s.transpose(0, 2, 1, 3).reshape(Bsz, G, n_slc, SLC_BLOCK, d)
    kw_pad = jnp.pad(k_w, ((0, 0), (WINDOW, 0), (0, 0), (0, 0)))
    vw_pad = jnp.pad(v_w, ((0, 0), (WINDOW, 0), (0, 0), (0, 0)))
    b_idx = jnp.arange(Bsz)[:, None, None, None]
    g_idx = jnp.arange(G)[None, :, None, None]
    scale = d ** -0.5
    jr = jnp.arange(n_slc)

    nq = S // Q_BLOCK
    qb = q.reshape(Bsz, nq, Q_BLOCK, G, Hg, d).transpose(1, 0, 2, 3, 4, 5)
    gb = gate.reshape(Bsz, nq, Q_BLOCK, G, Hg, 3).transpose(1, 0, 2, 3, 4, 5)

    def one(args):
        qi, g_blk, blk = args
        qpos = blk * Q_BLOCK + jnp.arange(Q_BLOCK)
        s = jnp.einsum('btghd,bngd->bghtn', qi, k_cmp).astype(jnp.float32) * scale
        valid = cmp_end[None, :] <= qpos[:, None]
        p_cmp = jnp.where(valid, jax.nn.softmax(jnp.where(valid, s, NEG_INF), axis=-1), 0.0)
        o_cmp = jnp.einsum('bghtn,bngd->btghd', p_cmp.astype(v_cmp.dtype), v_cmp)
        imp = jnp.einsum('bghtn,nj->bgtj', p_cmp, overlap)
        j_q = qpos // SLC_BLOCK
        forced = (jr[None, :] == 0) | (jr[None, :] == j_q[:, None]) | (jr[None, :] == j_q[:, None] - 1)
        imp = jnp.where(forced, 1e9, imp)
        imp = jnp.where(jr[None, :] <= j_q[:, None], imp, -1.0)
        top_s, top_i = lax.top_k(imp, top_k)
        k_sel = ks_blk[b_idx, g_idx, top_i].reshape(Bsz, G, Q_BLOCK, top_k * SLC_BLOCK, d)
        v_sel = vs_blk[b_idx, g_idx, top_i].reshape(Bsz, G, Q_BLOCK, top_k * SLC_BLOCK, d)
        tok = (top_i[..., None] * SLC_BLOCK + jnp.arange(SLC_BLOCK)).reshape(Bsz, G, Q_BLOCK, top_k * SLC_BLOCK)
        ok = (tok <= qpos[:, None]) & jnp.repeat(top_s >= 0, SLC_BLOCK, axis=-1)
        s = jnp.einsum('btghd,bgtnd->bghtn', qi, k_sel).astype(jnp.float32) * scale
        s = jnp.where(ok[:, :, None], s, NEG_INF)
        o_slc = jnp.einsum('bghtn,bgtnd->btghd', jax.nn.softmax(s, axis=-1).astype(v_sel.dtype), v_sel)
        k_win = lax.dynamic_slice_in_dim(kw_pad, blk * Q_BLOCK, WINDOW + Q_BLOCK, axis=1)
        v_win = lax.dynamic_slice_in_dim(vw_pad, blk * Q_BLOCK, WINDOW + Q_BLOCK, axis=1)
        kpos = blk * Q_BLOCK - WINDOW + jnp.arange(WINDOW + Q_BLOCK)
        dist = qpos[:, None] - kpos[None, :]
        okw = (dist >= 0) & (dist < WINDOW) & (kpos[None, :] >= 0)
        s = jnp.einsum('btghd,bsgd->bghts', qi, k_win).astype(jnp.float32) * scale
        s = jnp.where(okw, s, NEG_INF)
        o_win = jnp.einsum('bghts,bsgd->btghd', jax.nn.softmax(s, axis=-1).astype(v_win.dtype), v_win)
        return g_blk[..., 0:1] * o_cmp + g_blk[..., 1:2] * o_slc + g_blk[..., 2:3] * o_win

    o = lax.map(one, (qb, gb, jnp.arange(nq)))
    o = o.transpose(1, 0, 2, 3, 4, 5).reshape(Bsz, S, NSA_HEADS * d)
    return o @ w_out


def _token_mixing(h, w_in, b_in, gmlp_ln_g, gmlp_ln_b, gmlp_ws, gmlp_bs, gmlp_wout,
                  conv_w, conv_wout, mla_qnorm_g, mla_kvnorm_g, mla_wuq, mla_wukv, mla_wout,
                  nsa_pe_k, nsa_pe_v, nsa_wcmp_k, nsa_wcmp_v, nsa_wout, w_o):
    S = h.shape[1]
    z = h @ w_in + b_in
    offs = np.cumsum(IN_SPLITS)[:-1].tolist()
    (u, v, cb, cc, ch, q_lat, kv_lat, k_rope, nq, nkc, nvc, nks, nvs, nkw, nvw, ngate,
     ga, gb, gc, gd) = jnp.split(z, offs, axis=-1)
    pos = jnp.arange(S, dtype=jnp.float32)
    y_a = _gmlp_branch(u, v, gmlp_ln_g, gmlp_ln_b, gmlp_ws, gmlp_bs, gmlp_wout)
    y_b = _short_conv_branch(cb, cc, ch, conv_w, conv_wout)
    y_c = _mla_branch(q_lat, kv_lat, k_rope, mla_qnorm_g, mla_kvnorm_g, mla_wuq, mla_wukv, mla_wout, pos)
    y_d = _nsa_branch(nq, nkc, nvc, nks, nvs, nkw, nvw, ngate, nsa_pe_k, nsa_pe_v,
                      nsa_wcmp_k, nsa_wcmp_v, nsa_wout, pos)
    merged = (jax.nn.sigmoid(ga) * y_a + jax.nn.sigmoid(gb) * y_b
              + jax.nn.sigmoid(gc) * y_c + jax.nn.sigmoid(gd) * y_d)
    return merged @ w_o


def _cross_attention(x, mem, wq, wk, wv, wo):
    Bsz, S, _ = x.shape
    M = mem.shape[1]
    q = (x @ wq).reshape(Bsz, S, XATTN_HEADS, XATTN_DIM)
    k = (mem @ wk).reshape(Bsz, M, XATTN_HEADS, XATTN_DIM)
    v = (mem @ wv).reshape(Bsz, M, XATTN_HEADS, XATTN_DIM)
    s = jnp.einsum('bthd,bmhd->bhtm', q, k).astype(jnp.float32) * XATTN_DIM ** -0.5
    p = jax.nn.softmax(s, axis=-1).astype(v.dtype)
    o = jnp.einsum('bhtm,bmhd->bthd', p, v).reshape(Bsz, S, XATTN_HEADS * XATTN_DIM)
    return o @ wo


def setup_inputs(seed: int = 0) -> dict:
    key = jax.random.key(seed)
    keys = iter(jax.random.split(key, 64))

    def dense(shape, fan_in, scale=1.0):
        return jax.random.normal(next(keys), (DEPTH,) + shape, jnp.float32) * (scale * fan_in ** -0.5)

    def gain(shape):
        return 1.0 + 0.02 * jax.random.normal(next(keys), (DEPTH,) + shape, jnp.float32)

    def small(shape, s=0.01):
        return s * jax.random.normal(next(keys), (DEPTH,) + shape, jnp.float32)

    D, F = D_MODEL, D_FF
    return {
        'x': jax.random.normal(next(keys), (BATCH, SEQ, D), jnp.float32),
        'mem': jax.random.normal(next(keys), (BATCH, MEM_LEN, D), jnp.float32),
        'ffn1_w1': dense((D, F), D),
        'ffn1_w3': dense((D, F), D),
        'ffn1_w2': dense((F, D), F, DEEPNORM_BETA),
        'ln1_g': gain((D,)),
        'ln1_b': small((D,)),
        'w_in': dense((D, IN_WIDTH), D),
        'b_in': small((IN_WIDTH,)),
        'gmlp_ln_g': gain((GMLP_WIDTH,)),
        'gmlp_ln_b': small((GMLP_WIDTH,)),
        'gmlp_ws': dense((GMLP_GROUPS, GMLP_CHUNK, GMLP_CHUNK), GMLP_CHUNK),
        'gmlp_bs': gain((GMLP_GROUPS, GMLP_CHUNK)),
        'gmlp_wout': dense((GMLP_WIDTH, D), GMLP_WIDTH),
        'conv_w': dense((CONV_K, CONV_WIDTH), CONV_K),
        'conv_wout': dense((CONV_WIDTH, D), CONV_WIDTH),
        'mla_qnorm_g': gain((MLA_Q_RANK,)),
        'mla_kvnorm_g': gain((MLA_KV_RANK,)),
        'mla_wuq': dense((MLA_Q_RANK, MLA_HEADS * (MLA_NOPE + MLA_ROPE)), MLA_Q_RANK),
        'mla_wukv': dense((MLA_KV_RANK, MLA_HEADS * (MLA_NOPE + MLA_V)), MLA_KV_RANK),
        'mla_wout': dense((MLA_HEADS * MLA_V, D), MLA_HEADS * MLA_V),
        'nsa_pe_k': small((CMP_BLOCK, NSA_DIM), 0.02),
        'nsa_pe_v': small((CMP_BLOCK, NSA_DIM), 0.02),
        'nsa_wcmp_k': dense((CMP_BLOCK, NSA_DIM, NSA_DIM), CMP_BLOCK * NSA_DIM),
        'nsa_wcmp_v': dense((CMP_BLOCK, NSA_DIM, NSA_DIM), CMP_BLOCK * NSA_DIM),
        'nsa_wout': dense((NSA_HEADS * NSA_DIM, D), NSA_HEADS * NSA_DIM),
        'w_o': dense((D, D), D, DEEPNORM_BETA),
        'ln2_g': gain((D,)),
        'ln2_b': small((D,)),
        'xattn_wq': dense((D, XATTN_HEADS * XATTN_DIM), D),
        'xattn_wk': dense((D, XATTN_HEADS * XATTN_DIM), D),
        'xattn_wv': dense((D, XATTN_HEADS * XATTN_DIM), D),
        'xattn_wo': dense((XATTN_HEADS * XATTN_DIM, D), XATTN_HEADS * XATTN_DIM, DEEPNORM_BETA),
        'ln3_g': gain((D,)),
        'ln3_b': small((D,)),
        'ffn2_w1': dense((D, F), D),
        'ffn2_w3': dense((D, F), D),
        'ffn2_w2': dense((F, D), F, DEEPNORM_BETA),
        'ln4_g': gain((D,)),
        'ln4_b': small((D,)),
    }


def reference(x, mem, ffn1_w1, ffn1_w3, ffn1_w2, ln1_g, ln1_b, w_in, b_in,
              gmlp_ln_g, gmlp_ln_b, gmlp_ws, gmlp_bs, gmlp_wout, conv_w, conv_wout,
              mla_qnorm_g, mla_kvnorm_g, mla_wuq, mla_wukv, mla_wout,
              nsa_pe_k, nsa_pe_v, nsa_wcmp_k, nsa_wcmp_v, nsa_wout,
              w_o, ln2_g, ln2_b, xattn_wq, xattn_wk, xattn_wv, xattn_wo, ln3_g, ln3_b,
              ffn2_w1, ffn2_w3, ffn2_w2, ln4_g, ln4_b):
    a = DEEPNORM_ALPHA
    for l in range(DEPTH):
        x = _layer_norm(a * x + 0.5 * _swiglu(x, ffn1_w1[l], ffn1_w3[l], ffn1_w2[l]), ln1_g[l], ln1_b[l])
        mix = _token_mixing(x, w_in[l], b_in[l], gmlp_ln_g[l], gmlp_ln_b[l], gmlp_ws[l], gmlp_bs[l],
                            gmlp_wout[l], conv_w[l], conv_wout[l], mla_qnorm_g[l], mla_kvnorm_g[l],
                            mla_wuq[l], mla_wukv[l], mla_wout[l], nsa_pe_k[l], nsa_pe_v[l],
                            nsa_wcmp_k[l], nsa_wcmp_v[l], nsa_wout[l], w_o[l])
        x = _layer_norm(a * x + mix, ln2_g[l], ln2_b[l])
        x = _layer_norm(a * x + _cross_attention(x, mem, xattn_wq[l], xattn_wk[l], xattn_wv[l], xattn_wo[l]),
                        ln3_g[l], ln3_b[l])
        x = _layer_norm(a * x + 0.5 * _swiglu(x, ffn2_w1[l], ffn2_w3[l], ffn2_w2[l]), ln4_g[l], ln4_b[l])
    return x
```

```python
import numpy as np
import concourse.bass as bass
import concourse.mybir as mybir
from concourse.bass_utils import run_bass_kernel_spmd
from contextlib import ExitStack

F32 = mybir.dt.float32
BF16 = mybir.dt.bfloat16
AF = mybir.ActivationFunctionType
ALU = mybir.AluOpType

S = 4096; D = 1024; FF = 2816; NT = 32; L = 2; MEM = 256
ALPHA = float(4 ** 0.25)
NEGV = -30000.0
LN_EPS = 1e-5; RMS_EPS = 1e-6
IN_SPLITS = (512, 512, 512, 512, 512, 256, 128, 32, 512) + (128,) * 6 + (24,) + (1024,) * 4
IN_NAMES = ['u', 'v', 'cb', 'cc', 'ch', 'qlat', 'kvlat', 'krope', 'nq', 'nkc', 'nvc', 'nks', 'nvs', 'nkw', 'nvw',
            'ngate', 'ga', 'gb', 'gc', 'gd']


def _offsets(spec):
    off = {}; o = 0
    for n, shp in spec:
        sz = int(np.prod(shp))
        off[n] = (o, sz, tuple(shp)); o += sz
    return off, o


def wspec():
    sp = [('ident', (128,)), ('tril', (128,)), ('negc', (128,)), ('nega', (128,)), ('eexp', (4096,)),
          ('negcmp', (2, 4096)), ('ovl', (2, 64))]
    for l in range(L):
        sp += [(f'{l}f1w1', (8, FF)), (f'{l}f1w3', (8, FF)), (f'{l}f1w2', (22, D)),
               (f'{l}wA', (8, 1024)), (f'{l}wB', (8, 1536)), (f'{l}wC1', (8, 384)), (f'{l}wC2', (8, 64)),
               (f'{l}wDq', (8, 1024)), (f'{l}wDk', (8, 512)), (f'{l}wDc', (8, 256)), (f'{l}wDv', (8, 280)),
               (f'{l}wG', (8, 4096)),
               (f'{l}WsT', (4, 128)), (f'{l}woA', (4, D)), (f'{l}woB', (4, D)), (f'{l}woC', (4, D)), (f'{l}woD', (4, D)),
               (f'{l}wuq', (2, 768)), (f'{l}wuqs', (2, 768)), (f'{l}wukv', (1024,)),
               (f'{l}wck', (32, 64)), (f'{l}wcks', (32, 64)), (f'{l}wcv', (32, 64)),
               (f'{l}wo', (8, D)), (f'{l}xq', (8, 512)), (f'{l}xk', (8, 512)), (f'{l}xv', (8, 512)), (f'{l}xo', (4, D)),
               (f'{l}f2w1', (8, FF)), (f'{l}f2w3', (8, FF)), (f'{l}f2w2', (22, D))]
    return sp


CSPEC = [('bAu', 4), ('bB', 12), ('cw', 12), ('bC2', 2), ('bDq', 16), ('bDk', 8), ('bDc', 4), ('bG', 32),
         ('pek', 32), ('pev', 32)]
RSPEC = [('bAv', 512), ('glng', 512), ('glnb', 512), ('bs', 512), ('bC1', 384), ('qg', 256), ('kvg', 128),
         ('bDv', 280)] + [(f'ln{i}{c}', 1024) for i in (1, 2, 3, 4) for c in 'gb']
FSPEC = [('C64', 4096), ('S64', 4096), ('C96', 4096), ('S96', 4096), ('Ccmp', 256), ('Scmp', 256),
         ('CM', 2048), ('ADD', 2048), ('id32', 128)]

WOFF, NWB = _offsets(wspec())
COFF, NCOL = _offsets([(n, (s,)) for n, s in CSPEC])
ROFF, NROW = _offsets([(n, (s,)) for n, s in RSPEC])
FOFF, NCF = _offsets([(n, (s,)) for n, s in FSPEC])


def _pk(w):
    k, n = w.shape
    return np.ascontiguousarray(w.reshape(k // 128, 128, n).transpose(1, 0, 2))


def _rope_tab(pos, dim):
    inv = (np.float32(10000.0) ** (-(np.arange(0, dim, 2, dtype=np.float32) / np.float32(dim)))).astype(np.float32)
    ang = pos.astype(np.float32)[:, None] * inv[None, :]
    return np.cos(ang).astype(np.float32), np.sin(ang).astype(np.float32)


def _consts_bf():
    j = np.arange(128)[:, None]; i = np.arange(128)[None, :]
    c = {}
    c['ident'] = (j == i).astype(np.float32)
    c['tril'] = (j <= i).astype(np.float32)
    c['negc'] = np.where(j <= i, 0.0, NEGV).astype(np.float32)
    c['nega'] = np.where(j > i, 0.0, NEGV).astype(np.float32)
    ee = np.zeros((128, 4096), np.float32)
    key = np.arange(4096)
    for jj in range(64):
        ee[jj, key // 64 == jj] = -NEGV
    c['eexp'] = ee
    idx = (np.arange(2)[None, :, None] * 128 + np.arange(128)[:, None, None])
    q = np.arange(4096)[None, None, :]
    c['negcmp'] = np.where((idx < 255) & (16 * idx + 31 <= q), 0.0, NEGV).astype(np.float32)
    jj = np.arange(64)[None, None, :]
    ov = np.minimum(16 * idx + 32, 64 * jj + 64) - np.maximum(16 * idx, 64 * jj)
    ov = np.clip(ov, 0, None).astype(np.float32) / 32.0
    c['ovl'] = np.where(idx < 255, ov, 0.0).astype(np.float32)
    return c


def _consts_f32():
    cf = np.zeros((128, NCF), np.float32)
    pos = np.arange(S, dtype=np.float32)
    c64, s64 = _rope_tab(pos, 64)
    c32, s32 = _rope_tab(pos, 32)
    r = np.arange(128)
    C64 = c64.T[r % 32]
    S64 = np.where(((r % 64) < 32)[:, None], -s64.T[r % 32], s64.T[r % 32])
    C96 = np.zeros((128, S), np.float32); S96 = np.zeros((128, S), np.float32)
    C96[0:64] = 1.0
    rr = np.arange(32)
    C96[64:96] = c32.T[rr % 16]
    S96[64:96] = np.where((rr < 16)[:, None], -s32.T[rr % 16], s32.T[rr % 16])
    pc = (np.arange(255) * 16 + 31).astype(np.float32)
    cc, sc = _rope_tab(pc, 64)
    Cc = np.zeros((128, 256), np.float32); Sc = np.zeros((128, 256), np.float32)
    Cc[:, :255] = cc.T[r % 32]
    Sc[:, :255] = np.where(((r % 64) < 32)[:, None], -sc.T[r % 32], sc.T[r % 32])
    qq = (np.arange(32)[None, :, None] * 128 + np.arange(128)[:, None, None])
    jq = qq // 64
    jj = np.arange(64)[None, None, :]
    forced = (jj == 0) | (jj == jq) | (jj == jq - 1)
    causal = jj <= jq
    CM = (causal & ~forced).astype(np.float32)
    ADD = np.where(forced, 1e9, np.where(causal, 0.0, -1.0)).astype(np.float32)
    for n, a in (('C64', C64), ('S64', S64), ('C96', C96), ('S96', S96), ('Ccmp', Cc), ('Scmp', Sc),
                 ('CM', CM.reshape(128, -1)), ('ADD', ADD.reshape(128, -1)), ('id32', np.eye(128, dtype=np.float32))):
        o, sz, _ = FOFF[n]
        cf[:, o:o + sz] = a
    return cf


def _pack(inp):
    wb = np.zeros((128, NWB), np.float32)
    col = np.zeros((L, 128, NCOL), np.float32)
    row = np.zeros((L, 1, NROW), np.float32)

    def put(name, a):
        o, sz, shp = WOFF[name]
        assert a.shape == (128,) + shp, (name, a.shape, shp)
        wb[:, o:o + sz] = a.reshape(128, sz)

    def putc(l, name, a):
        o, sz, _ = COFF[name]
        col[l, :, o:o + sz] = a

    def putr(l, name, a):
        o, sz, _ = ROFF[name]
        row[l, 0, o:o + sz] = a

    offs = np.cumsum((0,) + IN_SPLITS)
    sw32 = np.concatenate([np.arange(16, 32), np.arange(0, 16)])
    sw64 = np.concatenate([np.arange(32, 64), np.arange(0, 32)])
    swq = np.concatenate([h * 64 + sw64 for h in range(8)])
    fm = lambda v: np.ascontiguousarray(v.reshape(-1, 128).T)
    for l in range(L):
        g = lambda n: np.asarray(inp[n][l], dtype=np.float32)
        win = g('w_in'); bin_ = g('b_in')
        cs = {n: win[:, offs[i]:offs[i + 1]] for i, n in enumerate(IN_NAMES)}
        bs_ = {n: bin_[offs[i]:offs[i + 1]] for i, n in enumerate(IN_NAMES)}
        for f, pre in (('f1', 'ffn1'), ('f2', 'ffn2')):
            put(f'{l}{f}w1', _pk(g(pre + '_w1'))); put(f'{l}{f}w3', _pk(g(pre + '_w3'))); put(f'{l}{f}w2', _pk(g(pre + '_w2')))
        put(f'{l}wA', _pk(np.concatenate([cs['u'], cs['v']], 1)))
        put(f'{l}wB', _pk(np.concatenate([cs['cb'], cs['cc'], cs['ch']], 1)))
        put(f'{l}wC1', _pk(np.concatenate([cs['qlat'], cs['kvlat']], 1)))
        put(f'{l}wC2', _pk(np.concatenate([cs['krope'], cs['krope'][:, sw32]], 1)))
        put(f'{l}wDq', _pk(np.concatenate([cs['nq'], cs['nq'][:, swq]], 1)))
        kcols = []; kb = []
        for nm in ('nks', 'nkw'):
            for sw in (False, True):
                idx = np.concatenate([gg * 64 + (sw64 if sw else np.arange(64)) for gg in range(2)])
                kcols.append(cs[nm][:, idx]); kb.append(bs_[nm][idx])
        put(f'{l}wDk', _pk(np.concatenate(kcols, 1)))
        put(f'{l}wDc', _pk(np.concatenate([cs['nkc'], cs['nvc']], 1)))
        put(f'{l}wDv', _pk(np.concatenate([cs['nvs'], cs['nvw'], cs['ngate']], 1)))
        put(f'{l}wG', _pk(np.concatenate([cs['ga'], cs['gb'], cs['gc'], cs['gd']], 1)))
        put(f'{l}WsT', np.ascontiguousarray(g('gmlp_ws').transpose(2, 0, 1)))
        put(f'{l}woA', _pk(g('gmlp_wout'))); put(f'{l}woB', _pk(g('conv_wout')))
        put(f'{l}woC', _pk(g('mla_wout'))); put(f'{l}woD', _pk(g('nsa_wout')))
        wuq = g('mla_wuq')
        swu = np.concatenate([np.concatenate([h * 96 + np.arange(64), h * 96 + 64 + sw32]) for h in range(8)])
        put(f'{l}wuq', _pk(wuq)); put(f'{l}wuqs', _pk(wuq[:, swu]))
        put(f'{l}wukv', g('mla_wukv'))
        wck = g('nsa_wcmp_k').transpose(1, 0, 2)
        wcv = g('nsa_wcmp_v').transpose(1, 0, 2)
        d2 = lambda a: np.concatenate([a, np.zeros_like(a)], 0)
        put(f'{l}wck', d2(wck))
        put(f'{l}wcks', d2(wck[:, :, sw64]))
        put(f'{l}wcv', d2(wcv))
        put(f'{l}wo', _pk(g('w_o')))
        put(f'{l}xq', _pk(g('xattn_wq'))); put(f'{l}xk', _pk(g('xattn_wk'))); put(f'{l}xv', _pk(g('xattn_wv')))
        put(f'{l}xo', _pk(g('xattn_wo')))
        putc(l, 'bAu', fm(bs_['u']))
        putc(l, 'bB', np.concatenate([fm(bs_['cb']), fm(bs_['cc']), fm(bs_['ch'])], 1))
        putc(l, 'cw', np.concatenate([fm(g('conv_w')[k]) for k in range(3)], 1))
        b2 = np.zeros((128, 2), np.float32); b2[64:96, 0] = bs_['krope']; b2[64:96, 1] = bs_['krope'][sw32]
        putc(l, 'bC2', b2)
        fm64 = lambda v: np.concatenate([np.ascontiguousarray(v.reshape(-1, 64).T), np.zeros((64, v.size // 64), np.float32)], 0)
        putc(l, 'bDq', np.concatenate([fm64(bs_['nq']), fm64(bs_['nq'][swq])], 1))
        putc(l, 'bDk', fm64(np.concatenate(kb)))
        putc(l, 'bDc', np.concatenate([fm64(bs_['nkc']), fm64(bs_['nvc'])], 1))
        putc(l, 'bG', np.concatenate([fm(bs_[n]) for n in ('ga', 'gb', 'gc', 'gd')], 1))
        putc(l, 'pek', d2(g('nsa_pe_k').T)); putc(l, 'pev', d2(g('nsa_pe_v').T))
        putr(l, 'bAv', bs_['v']); putr(l, 'glng', g('gmlp_ln_g')); putr(l, 'glnb', g('gmlp_ln_b'))
        putr(l, 'bs', g('gmlp_bs').reshape(-1))
        putr(l, 'bC1', np.concatenate([bs_['qlat'], bs_['kvlat']]))
        putr(l, 'qg', g('mla_qnorm_g')); putr(l, 'kvg', g('mla_kvnorm_g'))
        putr(l, 'bDv', np.concatenate([bs_['nvs'], bs_['nvw'], bs_['ngate']]))
        for i in (1, 2, 3, 4):
            putr(l, f'ln{i}g', g(f'ln{i}_g')); putr(l, f'ln{i}b', g(f'ln{i}_b'))
    for n, a in _consts_bf().items():
        put(n, a)
    return wb, col, row, _consts_f32()


class T:
    __slots__ = ('h', 'w', 'r')

    def __init__(s, h):
        s.h = h; s.w = None; s.r = {}

    def __getitem__(s, i):
        return s.h[i]


def cap(base, *dims):
    return bass.AP(tensor=base.tensor, offset=base.offset, ap=[list(base.ap[0])] + [list(d) for d in dims])


class K:
    def __init__(s, nc):
        s.nc = nc; s.es = ExitStack()
        s.eng = {'pe': nc.tensor, 'act': nc.scalar, 'dve': nc.vector, 'pool': nc.gpsimd, 'sp': nc.sync}
        s.esem = {n: s.es.enter_context(nc.semaphore('s_' + n)) for n in s.eng}
        s.cnt = {n: 0 for n in s.eng}; s.seen = {n: {} for n in s.eng}
        s.ND = 24
        s.dsem = [s.es.enter_context(nc.semaphore(f'd{i}')) for i in range(s.ND)]
        s.dval = [0] * s.ND; s.dnext = 0
        s.ps = [T(s.es.enter_context(nc.psum_tensor(f'ps{i}', [128, 512], F32))) for i in range(8)]
        s.uid = 0
        s.npe = 0; s.marks = []

    def alloc(s, es, shape, dt, name='t'):
        s.uid += 1
        return T(es.enter_context(s.nc.sbuf_tensor(f'{name}_{s.uid}', list(shape), dt)))

    def _need(s, e, ev, raw):
        kind, key, val = ev
        if kind == 'e' and key == e and (e == 'pe' or not raw):
            return
        kk = (kind, key)
        if s.seen[e].get(kk, 0) >= val:
            return
        s.seen[e][kk] = val
        s.eng[e].wait_ge(s.esem[key] if kind == 'e' else s.dsem[key], val)

    def deps(s, e, R, W):
        for t in R:
            if t.w is not None: s._need(e, t.w, True)
        for t in W:
            if t.w is not None: s._need(e, t.w, True)
            for ev in t.r.values(): s._need(e, ev, False)

    def op(s, e, fn, R=(), W=(), inc=True):
        s.deps(e, R, W)
        if e == 'pe': s.npe += 1
        ins = fn(s.eng[e])
        c = s.cnt[e] + 1
        if inc:
            ins.then_inc(s.esem[e], 1); s.cnt[e] = c
        ev = ('e', e, c)
        for t in R: t.r[e] = ev
        for t in W:
            t.w = ev; t.r = {}
        return ins

    def dma(s, out, in_, R=(), W=(), q='sp'):
        s.deps(q, R, W)
        i = s.dnext; s.dnext = (i + 1) % s.ND
        if s.dval[i] > 0: s._need(q, ('d', i, s.dval[i]), True)
        s.dval[i] += 16
        ev = ('d', i, s.dval[i])
        s.eng[q].dma_start(out=out, in_=in_).then_inc(s.dsem[i], 16)
        for t in R: t.r[('d', i)] = ev
        for t in W:
            t.w = ev; t.r = {}

    def barrier(s):
        for e in s.eng:
            for x in s.eng:
                if x != e and s.cnt[x] > 0: s._need(e, ('e', x, s.cnt[x]), True)
            for i in range(s.ND):
                if s.dval[i] > 0: s._need(e, ('d', i, s.dval[i]), True)

    def mm(s, pt, out, lhsT, rhs, R, start=True, stop=True, skip=False, inc=True):
        return s.op('pe', lambda e: e.matmul(out, lhsT=lhsT, rhs=rhs, start=start, stop=stop, skip_group_check=skip),
                    R=R, W=[pt], inc=inc)

    def tr(s, pt, out, in_, ident, R, inc=True):
        return s.op('pe', lambda e: e.transpose(out, in_, ident), R=R, W=[pt], inc=inc)

    def act(s, out, in_, func, R, W, bias=None, scale=None):
        kw = {}
        if bias is not None: kw['bias'] = bias
        if scale is not None: kw['scale'] = scale
        return s.op('act', lambda e: e.activation(out=out, in_=in_, func=func, **kw), R=R, W=W)

    def tt(s, e, out, in0, in1, op, R, W):
        return s.op(e, lambda g: g.tensor_tensor(out=out, in0=in0, in1=in1, op=op), R=R, W=W)

    def ts(s, e, out, in0, s1, s2, op0, op1, R, W):
        if op1 is None:
            return s.op(e, lambda g: g.tensor_scalar(out=out, in0=in0, scalar1=s1, scalar2=None, op0=op0), R=R, W=W)
        return s.op(e, lambda g: g.tensor_scalar(out=out, in0=in0, scalar1=s1, scalar2=s2, op0=op0, op1=op1), R=R, W=W)

    def stt(s, out, in0, sc, in1, op0, op1, R, W):
        return s.op('dve', lambda g: g.scalar_tensor_tensor(out=out, in0=in0, scalar=sc, in1=in1, op0=op0, op1=op1),
                    R=R, W=W)

    def cp(s, e, out, in_, R, W):
        if e == 'act':
            return s.op('act', lambda g: g.copy(out=out, in_=in_), R=R, W=W)
        return s.op(e, lambda g: g.tensor_copy(out=out, in_=in_), R=R, W=W)

    def memset(s, e, ap, v, W):
        return s.op(e, lambda g: g.memset(ap, v), W=W)


class U:
    __slots__ = ('S', 'X', 'P', 'sb', 'eb')

    def __init__(s, S, X, P):
        s.S = S; s.X = X; s.P = P; s.sb = None; s.eb = None


def run_items(k, items, sbanks, ebufs, ctr):
    def emitS(u):
        u.sb = k.ps[sbanks[ctr[0] % len(sbanks)]]; u.eb = ebufs[ctr[0] % len(ebufs)]; ctr[0] += 1
        u.S(u.sb)
    n = len(items)
    for i, it in enumerate(items):
        if isinstance(it, U):
            if it.sb is None: emitS(it)
            j = i + 1
            while j < n and (not isinstance(items[j], U)) and items[j][0] == 'soft': j += 1
            if j < n and isinstance(items[j], U) and items[j].sb is None: emitS(items[j])
            it.X(it.sb, it.eb); it.P(it.eb)
        else:
            it[1]()


class Prog:
    def __init__(s, nc, dbg=False, stop_after=None):
        s.nc = nc; s.k = K(nc); s.dbg = dbg; s.stop_after = stop_after; s.dq = []
        kind_s = "ExternalOutput" if dbg else "Internal"
        dt = nc.dram_tensor
        s.x_in = dt("x", [S, D], F32, kind="ExternalInput").ap()
        s.mem_in = dt("mem", [MEM, D], F32, kind="ExternalInput").ap()
        s.wbig = dt("wbig", [128, NWB], F32, kind="ExternalInput").ap()
        s.col = dt("col", [L, 128, NCOL], F32, kind="ExternalInput").ap()
        s.row = dt("row", [L, 1, NROW], F32, kind="ExternalInput").ap()
        s.cf = dt("cf", [128, NCF], F32, kind="ExternalInput").ap()
        s.out = dt("out", [S, D], F32, kind="ExternalOutput").ap()
        s.WB = dt("WB", [128, NWB], BF16, kind="Internal").ap()
        s.XR = dt("XR", [S, D], F32, kind=kind_s).ap()
        s.XT = dt("XT", [128, 8, S], BF16, kind=kind_s).ap()
        s.PA = dt("PA", [128, 4, S], BF16, kind=kind_s).ap()
        s.PB = dt("PB", [128, 4, S], BF16, kind=kind_s).ap()
        s.PC = dt("PC", [128, 4, S], BF16, kind=kind_s).ap()
        s.PD = dt("PD", [128, 4, S], BF16, kind=kind_s).ap()
        s.QM = dt("QM", [96, 8, S], BF16, kind="Internal").ap()
        s.KM = dt("KM", [96, 8, S], BF16, kind="Internal").ap()
        s.VM = dt("VM", [128, 8, NT, 65], BF16, kind="Internal").ap()
        s.QN = dt("QN", [64, 8, S], BF16, kind="Internal").ap()
        s.KK = dt("KK", [64, 4, S], BF16, kind="Internal").ap()
        s.KCV = dt("KCV", [64, 4, S], BF16, kind="Internal").ap()
        s.VSW = dt("VSW", [128, 4, NT, 65], BF16, kind="Internal").ap()
        s.GATE = dt("GATE", [128, NT, 24], F32, kind="Internal").ap()
        s.YP = dt("YP", [S, D], F32, kind="Internal").ap()
        s.MT = dt("MT", [128, 8, S], BF16, kind="Internal").ap()

    def wsl(s, name):
        o, sz, shp = WOFF[name]
        a = s.WB[:, o:o + sz]
        if len(shp) == 2:
            a = a.rearrange("p (a b) -> p a b", a=shp[0])
        return a

    def loadw(s, es, name):
        o, sz, shp = WOFF[name]
        t = s.k.alloc(es, (128,) + shp, BF16, 'w')
        s.k.dma(t[:], s.wsl(name), W=[t])
        return t

    def loadrow(s, es, l, name, n=None):
        o, sz, _ = ROFF[name]
        n = n or sz
        t = s.k.alloc(es, (128, n), F32, 'r')
        src = s.row[l, 0:1, o:o + n]
        s.k.dma(t[:], bass.AP(tensor=src.tensor, offset=src.offset, ap=[[0, 128], [1, n]]), W=[t])
        return t

    def loadcf(s, es, name, c0=0, n=None, parts=128):
        o, sz, _ = FOFF[name]
        n = n or sz
        t = s.k.alloc(es, (128, n), F32, 'c')
        s.k.dma(t[0:parts, :], s.cf[0:parts, o + c0:o + c0 + n], W=[t])
        return t

    def phase_end(s, name):
        s.k.barrier()
        s.k.marks.append((name, s.k.npe))
        return s.stop_after == name

    def prep(s):
        k = s.k
        CH = 4096
        end0 = WOFF['0f1w2'][0] + WOFF['0f1w2'][1]
        with ExitStack() as es:
            fb = [k.alloc(es, (128, CH), F32, 'pf') for _ in range(4)]
            bb = [k.alloc(es, (128, CH), BF16, 'pb') for _ in range(4)]
            engs = ['dve', 'act', 'dve', 'pool']
            n = 0
            for c0 in range(0, end0, CH):
                w = min(CH, end0 - c0)
                f = fb[n % 4]; b = bb[n % 4]
                k.dma(f[:, 0:w], s.wbig[:, c0:c0 + w], W=[f])
                k.cp(engs[n % 4], b[:, 0:w], f[:, 0:w], R=[f], W=[b])
                k.dma(s.WB[:, c0:c0 + w], b[:, 0:w], R=[b])
                n += 1
        s.k.barrier()
        s.BCH = 1024
        s.bgf = [k.alloc(k.es, (128, s.BCH), F32, 'bgf') for _ in range(2)]
        s.bgb = [k.alloc(k.es, (128, s.BCH), BF16, 'bgb') for _ in range(2)]
        s.bg_pos = end0; s.bg_n = 0

    def bg(s, n=1, act_ok=False):
        k = s.k
        for _ in range(n):
            if s.bg_pos >= NWB: return
            c0 = s.bg_pos; w = min(s.BCH, NWB - c0)
            f = s.bgf[s.bg_n % 2]; b = s.bgb[s.bg_n % 2]
            e = 'pool'
            s.bg_n += 1
            k.dma(f[:, 0:w], s.wbig[:, c0:c0 + w], W=[f])
            k.cp(e, b[:, 0:w], f[:, 0:w], R=[f], W=[b])
            k.dma(s.WB[:, c0:c0 + w], b[:, 0:w], R=[b], q=e)
            s.bg_pos += w

    def bg_until(s, col):
        while s.bg_pos < min(col, NWB):
            s.bg(1)

    def to_fm(s, es_bufs, src_t, src_ap_fn, n, dst_ap, psb, ident):
        k = s.k
        stg = es_bufs
        pv = k.ps[psb][:].bitcast(BF16)
        for j in range(n):
            k.tr(k.ps[psb], pv[:, j * 128:(j + 1) * 128], src_ap_fn(j), ident[:], R=[src_t, ident], inc=(j == n - 1))
        k.cp('act', stg[:, 0:n, :], pv[:, 0:n * 128].rearrange("p (a b) -> p a b", a=n), R=[], W=[stg, k.ps[psb]])
        k.dma(dst_ap, stg[:, 0:n, :], R=[stg])

    def ln_setup(s, es, l, i):
        k = s.k
        o = type('o', (), {})()
        o.g = s.loadrow(es, l, f'ln{i}g'); o.b = s.loadrow(es, l, f'ln{i}b')
        o.st = [k.alloc(es, (128, 2, 6), F32, 'st') for _ in range(4)]
        o.mv = [k.alloc(es, (128, 2), F32, 'mv') for _ in range(4)]
        o.rs = [k.alloc(es, (128, 2), F32, 'rs') for _ in range(4)]
        o.xb = [k.alloc(es, (128, D), BF16, 'xb') for _ in range(2)]
        o.stg = [k.alloc(es, (128, 8, 128), BF16, 'sg') for _ in range(2)]
        o.mh = k.alloc(es, (128, 1), F32, 'mh')
        k.memset('dve', o.mh[:], -0.5, W=[o.mh])
        o.n = 0
        return o

    def ln_tile(s, o, pre, tt, dst, psb, ident, write_xt=True):
        k = s.k
        i = o.n % 2; i4 = o.n % 4; o.n += 1
        st, mv, rs, xb, stg = o.st[i4], o.mv[i4], o.rs[i4], o.xb[i], o.stg[i]
        for hh in range(2):
            k.op('dve', lambda g, hh=hh: g.bn_stats(out=st[:, hh, :], in_=pre[:, hh * 512:(hh + 1) * 512]), R=[pre], W=[st])
        k.op('dve', lambda g: g.bn_aggr(out=mv[:], in_=st[:].rearrange("p a b -> p (a b)")), R=[st], W=[mv])
        k.ts('dve', rs[:, 0:1], mv[:, 1:2], LN_EPS, None, ALU.add, None, R=[mv], W=[rs])
        k.tt('pool', rs[:, 1:2], rs[:, 0:1], o.mh[:], ALU.pow, R=[rs, o.mh], W=[rs])
        k.ts('dve', pre[:], pre[:], mv[:, 0:1], rs[:, 1:2], ALU.subtract, ALU.mult, R=[pre, mv, rs], W=[pre])
        k.tt('pool', pre[:], pre[:], o.g[:], ALU.mult, R=[pre, o.g], W=[pre])
        k.tt('pool', pre[:], pre[:], o.b[:], ALU.add, R=[pre, o.b], W=[pre])
        k.dma(dst[tt * 128:(tt + 1) * 128, :], pre[:], R=[pre])
        if write_xt:
            s.flush(1)
            k.cp('act', xb[:], pre[:], R=[pre], W=[xb])
            s.dq.append(lambda: s.to_fm(stg, xb, lambda j: xb[:, j * 128:(j + 1) * 128], 8,
                                        s.XT[:, :, tt * 128:(tt + 1) * 128], psb, ident))

    def flush(s, keep=0):
        while len(s.dq) > keep:
            s.dq.pop(0)()

    def xt0(s):
        k = s.k
        with ExitStack() as es:
            ident = s.loadw(es, 'ident')
            xf = [k.alloc(es, (128, D), F32, 'xf') for _ in range(2)]
            xb = [k.alloc(es, (128, D), BF16, 'xb') for _ in range(2)]
            stg = [k.alloc(es, (128, 8, 128), BF16, 'sg') for _ in range(2)]
            for tt in range(NT):
                f = xf[tt % 2]; b = xb[tt % 2]
                k.dma(f[:], s.x_in[tt * 128:(tt + 1) * 128, :], W=[f])
                k.cp('dve', b[:], f[:], R=[f], W=[b])
                s.to_fm(stg[tt % 2], b, lambda j, b=b: b[:, j * 128:(j + 1) * 128], 8,
                        s.XT[:, :, tt * 128:(tt + 1) * 128], tt % 2, ident)
        s.k.barrier()

    def ffn(s, l, f, lni, src, dst, write_xt=True, bgn=0):
        k = s.k
        GT = 256; NG = S // GT; HF = FF // 2; NFC = 11
        for half in range(2):
            with ExitStack() as es:
                w1 = k.alloc(es, (128, 8, HF), BF16, 'w1'); w3 = k.alloc(es, (128, 8, HF), BF16, 'w3')
                w2 = k.alloc(es, (128, NFC, D), BF16, 'w2')
                k.dma(w1[:], s.wsl(f'{l}{f}w1')[:, :, half * HF:(half + 1) * HF], W=[w1])
                k.dma(w3[:], s.wsl(f'{l}{f}w3')[:, :, half * HF:(half + 1) * HF], W=[w3])
                k.dma(w2[:], s.wsl(f'{l}{f}w2')[:, half * NFC:(half + 1) * NFC, :], W=[w2])
                ident = s.loadw(es, 'ident')
                ln = s.ln_setup(es, l, lni) if half == 1 else None
                xT = [k.alloc(es, (128, 8, GT), BF16, 'xT') for _ in range(3)]
                gT = [k.alloc(es, (128, NFC, GT), BF16, 'gT') for _ in range(2)]
                sl = [k.alloc(es, (128, GT), F32, 'sl') for _ in range(2)]
                xr = [k.alloc(es, (128, D), F32, 'xr') for _ in range(4)]
                pre = [k.alloc(es, (128, D), F32, 'pre') for _ in range(4)]

                def LDT(gi):
                    k.dma(xT[gi % 3][:], s.XT[:, :, gi * GT:(gi + 1) * GT], W=[xT[gi % 3]])

                def LDX(gi):
                    for t2 in range(2):
                        tt = gi * 2 + t2
                        sr = src if half == 0 else s.YP
                        k.dma(xr[tt % 4][:], sr[tt * 128:(tt + 1) * 128, :], W=[xr[tt % 4]])

                def H(gi):
                    x = xT[gi % 3]; g = gT[gi % 2]
                    for fc in range(NFC):
                        p1 = k.ps[(2 * fc) % 3]; p3 = k.ps[(2 * fc + 1) % 3]
                        for kk in range(8):
                            k.mm(p1, p1[:, 0:GT], w1[:, kk, fc * 128:(fc + 1) * 128], x[:, kk, :], R=[w1, x],
                                 start=(kk == 0), stop=(kk == 7), inc=(kk == 7))
                        for kk in range(8):
                            k.mm(p3, p3[:, 0:GT], w3[:, kk, fc * 128:(fc + 1) * 128], x[:, kk, :], R=[w3, x],
                                 start=(kk == 0), stop=(kk == 7), inc=(kk == 7))
                        sb = sl[fc % 2]
                        k.act(sb[:], p1[:, 0:GT], AF.Silu, R=[], W=[sb, p1])
                        k.tt('dve', g[:, fc, :], sb[:], p3[:, 0:GT], ALU.mult, R=[sb], W=[g, p3])

                def Y(gi):
                    g = gT[gi % 2]
                    for t2 in range(2):
                        tt = gi * 2 + t2
                        xx = xr[tt % 4]; pp = pre[tt % 4]
                        for dh in range(2):
                            pb = k.ps[4 + t2 * 2 + dh]
                            for fc in range(NFC):
                                k.mm(pb, pb[:, :], g[:, fc, t2 * 128:(t2 + 1) * 128], w2[:, fc, dh * 512:(dh + 1) * 512],
                                     R=[g, w2], start=(fc == 0), stop=(fc == NFC - 1), inc=(fc == NFC - 1))
                            if half == 0:
                                k.act(pp[:, dh * 512:(dh + 1) * 512], pb[:, :], AF.Copy, R=[], W=[pp, pb], scale=0.5)
                            else:
                                k.stt(pp[:, dh * 512:(dh + 1) * 512], pb[:, :], 0.5, xx[:, dh * 512:(dh + 1) * 512], ALU.mult, ALU.add,
                                      R=[xx], W=[pp, pb])
                        if half == 0:
                            k.stt(pp[:], xx[:], ALPHA, pp[:], ALU.mult, ALU.add, R=[xx, pp], W=[pp])
                            k.dma(s.YP[tt * 128:(tt + 1) * 128, :], pp[:], R=[pp])
                        else:
                            s.ln_tile(ln, pp, tt, dst, 3, ident, write_xt)

                LDT(0); LDT(1); LDX(0)
                H(0)
                for gi in range(NG):
                    if gi + 2 < NG: LDT(gi + 2)
                    if gi + 1 < NG: LDX(gi + 1)
                    s.bg((bgn * 3 + 1) // 2 if half == 0 else bgn // 2)
                    if gi + 1 < NG: H(gi + 1)
                    Y(gi)
                s.flush(0)
                if half == 1 and getattr(s, '_bg_until_col', None):
                    s.bg_until(s._bg_until_col)
            s.k.barrier()
        return s.phase_end(f'{l}{f}')

    def mixA(s, l):
        k = s.k
        with ExitStack() as es:
            wA = s.loadw(es, f'{l}wA'); ws = s.loadw(es, f'{l}WsT'); tril = s.loadw(es, 'tril')
            bv = s.loadrow(es, l, 'bAv'); lg = s.loadrow(es, l, 'glng'); lb = s.loadrow(es, l, 'glnb')
            bsb = s.loadrow(es, l, 'bs')
            colt = k.alloc(es, (128, NCOL), F32, 'col'); k.dma(colt[:], s.col[l], W=[colt])
            cA = COFF['bAu'][0]
            wsm = k.alloc(es, (128, 4, 128), BF16, 'wsm')
            k.tt('dve', wsm[:], ws[:], cap(tril[:, 0:1], [0, 4], [1, 128]), ALU.mult, R=[ws, tril], W=[wsm])
            mh = k.alloc(es, (128, 1), F32, 'mh'); k.memset('dve', mh[:], -0.5, W=[mh])
            hT = [k.alloc(es, (128, 8, 512), BF16, 'hT') for _ in range(2)]
            uT = [k.alloc(es, (128, 4, 512), F32, 'uT') for _ in range(2)]
            pa = [k.alloc(es, (128, 4, 512), BF16, 'pa') for _ in range(2)]
            vb = [k.alloc(es, (128, 512), F32, 'vb') for _ in range(2)]
            vn = [k.alloc(es, (128, 512), F32, 'vn') for _ in range(2)]
            vl = [k.alloc(es, (128, 512), BF16, 'vl') for _ in range(2)]
            t1 = [k.alloc(es, (128, 512), F32, 't1') for _ in range(2)]
            st = [k.alloc(es, (128, 6), F32, 'st') for _ in range(2)]
            mv = [k.alloc(es, (128, 2), F32, 'mv') for _ in range(2)]
            rs = [k.alloc(es, (128, 2), F32, 'rs') for _ in range(2)]

            def load(G):
                h = hT[G % 2]
                k.dma(h[:], s.XT[:, :, G * 512:(G + 1) * 512], W=[h])

            def Uproj(G):
                h = hT[G % 2]; u = uT[G % 2]
                for fc in range(4):
                    pb = k.ps[fc % 2]
                    for kk in range(8):
                        k.mm(pb, pb[:, :], wA[:, kk, fc * 128:(fc + 1) * 128], h[:, kk, :], R=[wA, h],
                             start=(kk == 0), stop=(kk == 7), inc=(kk == 7))
                    k.act(u[:, fc, :], pb[:, :], AF.Identity, R=[colt], W=[u, pb], bias=colt[:, cA + fc:cA + fc + 1])

            def V(cc):
                G, c = divmod(cc, 4); i = cc % 2
                h = hT[G % 2]
                pv = k.ps[2 + i]
                for kk in range(8):
                    k.mm(pv, pv[:, :], h[:, kk, c * 128:(c + 1) * 128], wA[:, kk, 512:1024], R=[wA, h],
                         start=(kk == 0), stop=(kk == 7), inc=(kk == 7))
                k.tt('dve', vb[i][:], pv[:, :], bv[:], ALU.add, R=[bv], W=[vb[i], pv])
                k.op('dve', lambda g: g.bn_stats(out=st[i][:], in_=vb[i][:]), R=[vb[i]], W=[st[i]])
                k.op('dve', lambda g: g.bn_aggr(out=mv[i][:], in_=st[i][:]), R=[st[i]], W=[mv[i]])
                k.ts('dve', rs[i][:, 0:1], mv[i][:, 1:2], LN_EPS, None, ALU.add, None, R=[mv[i]], W=[rs[i]])
                k.tt('pool', rs[i][:, 1:2], rs[i][:, 0:1], mh[:], ALU.pow, R=[rs[i], mh], W=[rs[i]])
                k.ts('dve', vn[i][:], vb[i][:], mv[i][:, 0:1], rs[i][:, 1:2], ALU.subtract, ALU.mult,
                     R=[vb[i], mv[i], rs[i]], W=[vn[i]])
                k.tt('pool', vn[i][:], vn[i][:], lg[:], ALU.mult, R=[vn[i], lg], W=[vn[i]])
                k.tt('pool', vl[i][:], vn[i][:], lb[:], ALU.add, R=[vn[i], lb], W=[vl[i]])

            def S2(cc):
                G, c = divmod(cc, 4); i = cc % 2
                u = uT[G % 2]; p = pa[G % 2]
                p2 = k.ps[4 + i]
                for g in range(4):
                    k.mm(p2, p2[:, g * 128:(g + 1) * 128], vl[i][:, g * 128:(g + 1) * 128], wsm[:, g, :],
                         R=[vl[i], wsm], start=True, stop=True, inc=(g == 3))
                k.tt('dve', t1[i][:], p2[:, :], bsb[:], ALU.add, R=[bsb], W=[t1[i], p2])
                k.tt('pool', p[:, :, c * 128:(c + 1) * 128], t1[i][:].rearrange("p (a b) -> p a b", a=4),
                     u[:, :, c * 128:(c + 1) * 128], ALU.mult, R=[t1[i], u], W=[p])
                if c == 3:
                    k.dma(s.PA[:, :, G * 512:(G + 1) * 512], p[:], R=[p])

            load(0); load(1); Uproj(0); V(0)
            for cc in range(32):
                G, c = divmod(cc, 4)
                if cc + 1 < 32:
                    if (cc + 1) % 4 == 0: Uproj((cc + 1) // 4)
                    V(cc + 1)
                S2(cc)
                if c == 3 and G + 2 < 8: load(G + 2)
        return s.phase_end(f'{l}A')

    def mixB(s, l):
        k = s.k
        with ExitStack() as es:
            wB = s.loadw(es, f'{l}wB')
            colt = k.alloc(es, (128, NCOL), F32, 'col'); k.dma(colt[:], s.col[l], W=[colt])
            cb0 = COFF['bB'][0]; cw0 = COFF['cw'][0]
            P = k.alloc(es, (128, 4, 514), F32, 'P')
            k.memset('dve', P[:, :, 0:2], 0.0, W=[P])
            hT = [k.alloc(es, (128, 8, 512), BF16, 'hT') for _ in range(2)]
            pbuf = [k.alloc(es, (128, 4, 512), BF16, 'pb') for _ in range(2)]
            ccs = [k.alloc(es, (128, 512), F32, 'cc') for _ in range(2)]
            y1 = [k.alloc(es, (128, 512), F32, 'y1') for _ in range(2)]
            y2 = [k.alloc(es, (128, 512), F32, 'y2') for _ in range(2)]

            def load(G):
                k.dma(hT[G % 2][:], s.XT[:, :, G * 512:(G + 1) * 512], W=[hT[G % 2]])
            load(0)
            n = 0
            for G in range(8):
                if G + 1 < 8: load(G + 1)
                h = hT[G % 2]; pb = pbuf[G % 2]
                for fc in range(4):
                    i = n % 2; n += 1
                    pa_, pb_, pc_ = k.ps[i], k.ps[2 + i], k.ps[4 + i]
                    for which, pp in ((1, pa_), (2, pb_), (0, pc_)):
                        for kk in range(8):
                            c0 = which * 512 + fc * 128
                            k.mm(pp, pp[:, :], wB[:, kk, c0:c0 + 128], h[:, kk, :], R=[wB, h],
                                 start=(kk == 0), stop=(kk == 7), inc=(kk == 7))
                    bcol = lambda which: colt[:, cb0 + which * 4 + fc:cb0 + which * 4 + fc + 1]
                    wcol = lambda kk: colt[:, cw0 + kk * 4 + fc:cw0 + kk * 4 + fc + 1]
                    k.act(ccs[i][:], pa_[:, :], AF.Identity, R=[colt], W=[ccs[i], pa_], bias=bcol(1))
                    k.stt(P[:, fc, 2:514], pb_[:, :], bcol(2), ccs[i][:], ALU.add, ALU.mult, R=[ccs[i], colt], W=[P, pb_])
                    k.ts('dve', y1[i][:], P[:, fc, 2:514], wcol(2), None, ALU.mult, None, R=[P, colt], W=[y1[i]])
                    k.stt(y2[i][:], P[:, fc, 1:513], wcol(1), y1[i][:], ALU.mult, ALU.add, R=[P, y1[i], colt], W=[y2[i]])
                    k.stt(y1[i][:], P[:, fc, 0:512], wcol(0), y2[i][:], ALU.mult, ALU.add, R=[P, y2[i], colt], W=[y1[i]])
                    k.stt(pb[:, fc, :], pc_[:, :], bcol(0), y1[i][:], ALU.add, ALU.mult, R=[y1[i], colt], W=[pb, pc_])
                    k.cp('pool', P[:, fc, 0:2], P[:, fc, 512:514], R=[], W=[P])
                k.dma(s.PB[:, :, G * 512:(G + 1) * 512], pb[:], R=[pb])
        return s.phase_end(f'{l}B')

    def mixC1(s, l):
        k = s.k
        with ExitStack() as es:
            wC1 = s.loadw(es, f'{l}wC1'); wC2 = s.loadw(es, f'{l}wC2')
            wuq = s.loadw(es, f'{l}wuq'); wuqs = s.loadw(es, f'{l}wuqs'); wukv = s.loadw(es, f'{l}wukv')
            ident = s.loadw(es, 'ident')
            bC1 = s.loadrow(es, l, 'bC1'); qg = s.loadrow(es, l, 'qg'); kvg = s.loadrow(es, l, 'kvg')
            colt = k.alloc(es, (128, NCOL), F32, 'col'); k.dma(colt[:], s.col[l], W=[colt])
            c2 = COFF['bC2'][0]
            mh = k.alloc(es, (128, 1), F32, 'mh'); k.memset('dve', mh[:], -0.5, W=[mh])
            hT = [k.alloc(es, (128, 8, 512), BF16, 'hT') for _ in range(2)]
            Ct = [k.alloc(es, (128, 512), F32, 'Ct') for _ in range(2)]
            St = [k.alloc(es, (128, 512), F32, 'St') for _ in range(2)]
            zb = [k.alloc(es, (128, 384), F32, 'zb') for _ in range(2)]
            st = [k.alloc(es, (128, 2, 6), F32, 'st') for _ in range(2)]
            mv = [k.alloc(es, (128, 4), F32, 'mv') for _ in range(2)]
            rs = [k.alloc(es, (128, 4), F32, 'rs') for _ in range(2)]
            lat = [k.alloc(es, (128, 384), BF16, 'lat') for _ in range(2)]
            latT = [k.alloc(es, (128, 3, 512), BF16, 'latT') for _ in range(2)]
            qT = [k.alloc(es, (128, 8, 512), BF16, 'qT') for _ in range(2)]
            kT = [k.alloc(es, (128, 8, 512), BF16, 'kT') for _ in range(2)]
            va = [k.alloc(es, (128, 8, 4, 65), BF16, 'va') for _ in range(2)]
            ta = [k.alloc(es, (128, 512), F32, 'ta') for _ in range(2)]
            tb = [k.alloc(es, (128, 512), F32, 'tb') for _ in range(2)]
            kpe = k.alloc(es, (128, 512), BF16, 'kpe')
            for v in va:
                k.memset('pool', v[:, :, :, 64:65], 1.0, W=[v])
            o96 = FOFF['C96'][0]; s96 = FOFF['S96'][0]

            def load(G):
                i = G % 2
                k.dma(hT[i][:], s.XT[:, :, G * 512:(G + 1) * 512], W=[hT[i]])
                k.dma(Ct[i][0:96, :], s.cf[0:96, o96 + G * 512:o96 + (G + 1) * 512], W=[Ct[i]])
                k.dma(St[i][0:96, :], s.cf[0:96, s96 + G * 512:s96 + (G + 1) * 512], W=[St[i]])
            def Z(cc):
                G, c = divmod(cc, 4); i = cc % 2
                h = hT[G % 2]
                pz = k.ps[i]
                for kk in range(8):
                    k.mm(pz, pz[:, 0:384], h[:, kk, c * 128:(c + 1) * 128], wC1[:, kk, :], R=[h, wC1],
                         start=(kk == 0), stop=(kk == 7), inc=(kk == 7))
                k.tt('dve', zb[i][:], pz[:, 0:384], bC1[:], ALU.add, R=[bC1], W=[zb[i], pz])
                k.op('dve', lambda g, i=i: g.bn_stats(out=st[i][:, 0, :], in_=zb[i][:, 0:256]), R=[zb[i]], W=[st[i]])
                k.op('dve', lambda g, i=i: g.bn_stats(out=st[i][:, 1, :], in_=zb[i][:, 256:384]), R=[zb[i]], W=[st[i]])
                k.op('dve', lambda g, i=i: g.bn_aggr(out=mv[i][:, 0:2], in_=st[i][:, 0, :]), R=[st[i]], W=[mv[i]])
                k.op('dve', lambda g, i=i: g.bn_aggr(out=mv[i][:, 2:4], in_=st[i][:, 1, :]), R=[st[i]], W=[mv[i]])
                for j in range(2):
                    k.stt(rs[i][:, j:j + 1], mv[i][:, 2 * j:2 * j + 1], mv[i][:, 2 * j:2 * j + 1], mv[i][:, 2 * j + 1:2 * j + 2],
                          ALU.mult, ALU.add, R=[mv[i]], W=[rs[i]])
                k.ts('dve', rs[i][:, 2:4], rs[i][:, 0:2], RMS_EPS, None, ALU.add, None, R=[rs[i]], W=[rs[i]])
                k.tt('pool', rs[i][:, 0:2], rs[i][:, 2:4], cap(mh[:, 0:1], [0, 2]), ALU.pow, R=[rs[i], mh], W=[rs[i]])
                k.stt(lat[i][:, 0:256], zb[i][:, 0:256], rs[i][:, 0:1], qg[:], ALU.mult, ALU.mult, R=[zb[i], rs[i], qg], W=[lat[i]])
                k.stt(lat[i][:, 256:384], zb[i][:, 256:384], rs[i][:, 1:2], kvg[:], ALU.mult, ALU.mult, R=[zb[i], rs[i], kvg], W=[lat[i]])

            def Tt(cc):
                G, c = divmod(cc, 4); i = cc % 2
                lt = latT[G % 2]; v = va[G % 2]
                pt = k.ps[2 + i]; ptv = pt[:].bitcast(BF16)
                for j in range(3):
                    k.tr(pt, ptv[:, j * 128:(j + 1) * 128], lat[i][:, j * 128:(j + 1) * 128], ident[:], R=[lat[i], ident], inc=(j == 2))
                k.cp('act', lt[:, :, c * 128:(c + 1) * 128], ptv[:, 0:384].rearrange("p (a b) -> p a b", a=3), R=[], W=[lt, pt])
                pv = k.ps[4 + i]
                vcols = cap(wukv[:, 64:65], [128, 8], [1, 64])
                k.mm(pv, pv[:, :].rearrange("p (a b) -> p a b", a=8), lt[:, 2, c * 128:(c + 1) * 128], vcols, R=[lt, wukv])
                k.cp('act', v[:, :, c, 0:64], pv[:, :].rearrange("p (a b) -> p a b", a=8), R=[], W=[v, pv])

            def UP(G):
                gi = G % 2
                h = hT[gi]; C = Ct[gi]; Sn = St[gi]; lt = latT[gi]; q = qT[gi]; kt_ = kT[gi]; v = va[gi]
                pe1 = k.ps[6]; pe2 = k.ps[7]
                for kk in range(8):
                    k.mm(pe1, pe1[64:96, :], wC2[:, kk, 0:32], h[:, kk, :], R=[wC2, h], start=(kk == 0), stop=(kk == 7), inc=(kk == 7))
                for kk in range(8):
                    k.mm(pe2, pe2[64:96, :], wC2[:, kk, 32:64], h[:, kk, :], R=[wC2, h], start=(kk == 0), stop=(kk == 7), inc=(kk == 7))
                k.stt(ta[0][64:96, :], pe1[64:96, :], colt[64:96, c2:c2 + 1], C[64:96, :], ALU.add, ALU.mult, R=[colt, C], W=[ta[0], pe1])
                k.stt(tb[0][64:96, :], pe2[64:96, :], colt[64:96, c2 + 1:c2 + 2], Sn[64:96, :], ALU.add, ALU.mult, R=[colt, Sn], W=[tb[0], pe2])
                k.tt('pool', kpe[64:96, :], ta[0][64:96, :], tb[0][64:96, :], ALU.add, R=[ta[0], tb[0]], W=[kpe])
                k.cp('pool', kt_[64:96, :, :], cap(kpe[64:96, 0:1], [0, 8], [1, 512]), R=[kpe], W=[kt_])
                for hh in range(8):
                    i = hh % 2
                    px = k.ps[i]; py = k.ps[2 + i]; pk_ = k.ps[4 + i]
                    for kc in range(2):
                        k.mm(px, px[0:96, :], wuq[:, kc, hh * 96:(hh + 1) * 96], lt[:, kc, :], R=[wuq, lt], start=(kc == 0), stop=(kc == 1), inc=(kc == 1))
                    for kc in range(2):
                        k.mm(py, py[0:96, :], wuqs[:, kc, hh * 96:(hh + 1) * 96], lt[:, kc, :], R=[wuqs, lt], start=(kc == 0), stop=(kc == 1), inc=(kc == 1))
                    k.tt('dve', ta[1][0:96, :], px[0:96, :], C[0:96, :], ALU.mult, R=[C], W=[ta[1], px])
                    k.tt('dve', tb[1][0:96, :], py[0:96, :], Sn[0:96, :], ALU.mult, R=[Sn], W=[tb[1], py])
                    k.tt('pool', q[0:96, hh, :], ta[1][0:96, :], tb[1][0:96, :], ALU.add, R=[ta[1], tb[1]], W=[q])
                    k.mm(pk_, pk_[0:64, :], wukv[:, hh * 128:hh * 128 + 64], lt[:, 2, :], R=[wukv, lt])
                    k.cp('act', kt_[0:64, hh, :], pk_[0:64, :], R=[], W=[kt_, pk_])
                k.dma(s.QM[:, :, G * 512:(G + 1) * 512], q[0:96, :, :], R=[q])
                k.dma(s.KM[:, :, G * 512:(G + 1) * 512], kt_[0:96, :, :], R=[kt_])
                k.dma(s.VM[:, :, G * 4:(G + 1) * 4, :], v[:], R=[v])

            load(0); load(1); Z(0)
            for cc in range(32):
                G, c = divmod(cc, 4)
                if cc + 1 < 32: Z(cc + 1)
                Tt(cc)
                if c == 3:
                    UP(G)
                    if G + 2 < 8: load(G + 2)
        return s.phase_end(f'{l}C1')

    def otm_to_fm(s, es, otm, dst, ident, psb0):
        k = s.k
        stg = [k.alloc(es, (128, 4, 512), BF16, 'osg') for _ in range(2)]
        for qg_ in range(8):
            sg = stg[qg_ % 2]
            for q4 in range(4):
                qt = qg_ * 4 + q4
                pt = k.ps[psb0 + qt % 2]; ptv = pt[:].bitcast(BF16)
                for j in range(4):
                    k.tr(pt, ptv[:, j * 128:(j + 1) * 128], otm[:, qt, j * 128:(j + 1) * 128], ident[:], R=[otm, ident], inc=(j == 3))
                k.cp('act' if qt % 2 else 'dve', sg[:, :, q4 * 128:(q4 + 1) * 128], ptv[:, 0:512].rearrange("p (a b) -> p a b", a=4), R=[], W=[sg, pt])
            k.dma(dst[:, :, qg_ * 512:(qg_ + 1) * 512], sg[:], R=[sg])

    def mixC2(s, l, bgn=0):
        k = s.k
        sc = float(96 ** -0.5)
        with ExitStack() as es:
            ident = s.loadw(es, 'ident'); negc = s.loadw(es, 'negc')
            otm = k.alloc(es, (128, NT, 512), BF16, 'otm')
            kT = [k.alloc(es, (128, S), BF16, 'kT') for _ in range(2)]
            qT = [k.alloc(es, (128, S), BF16, 'qT') for _ in range(2)]
            V = [k.alloc(es, (128, NT, 65), BF16, 'V') for _ in range(2)]
            E = [k.alloc(es, (128, 512), BF16, 'E') for _ in range(3)]
            rc = [k.alloc(es, (128, 4), F32, 'rc') for _ in range(2)]

            for t_ in kT + qT:
                k.memset('pool', t_[:, :], 0.0, W=[t_])

            def load(hh):
                i = hh % 2
                k.dma(kT[i][0:96, :], s.KM[:, hh, :], W=[kT[i]])
                k.dma(qT[i][0:96, :], s.QM[:, hh, :], W=[qT[i]])
                k.dma(V[i][:], s.VM[:, hh, :, :], W=[V[i]])
            load(0)
            no = 0; ctr = [0]
            for hh in range(8):
                if hh + 1 < 8: load(hh + 1)
                kt_, q, v = kT[hh % 2], qT[hh % 2], V[hh % 2]
                items = []
                for G in range(8):
                    O = k.ps[4 + no % 2]; r = rc[no % 2]; no += 1
                    items.append(('soft', lambda O=O: (s.bg(bgn), k.memset('dve', O[:, 0:260], 0.0, W=[O]))))
                    for kt in range(4 * G + 4):
                        q0 = max(kt * 128, G * 512) - G * 512
                        nc_ = 512 - q0
                        diag = kt >= 4 * G

                        def fS(Sb, kt=kt, q0=q0, nc_=nc_, diag=diag, G=G):
                            k.mm(Sb, Sb[:, 0:nc_], kt_[:, kt * 128:(kt + 1) * 128], q[:, G * 512 + q0:(G + 1) * 512],
                                 R=[kt_, q], start=True, stop=(not diag), inc=(not diag))
                            if diag:
                                k.mm(Sb, Sb[:, 0:128], ident[:], negc[:], R=[ident, negc], start=False, stop=True)

                        def fX(Sb, Eb, nc_=nc_):
                            k.act(Eb[:, 0:nc_], Sb[:, 0:nc_], AF.Exp, R=[], W=[Eb, Sb], scale=sc)

                        def fP(Eb, kt=kt, q0=q0, O=O):
                            for qs in range(q0 // 128, 4):
                                k.mm(O, O[:, qs * 65:(qs + 1) * 65], Eb[:, (qs * 128 - q0):(qs * 128 - q0) + 128], v[:, kt, :],
                                     R=[Eb, v], start=False, stop=True, skip=True, inc=(qs == 3))
                        items.append(U(fS, fX, fP))

                    def fin(O=O, r=r, G=G, hh=hh):
                        Ov = O[:, 0:260].rearrange("p (a b) -> p a b", a=4)
                        k.op('dve', lambda g: g.reciprocal(out=r[:], in_=Ov[:, :, 64]), R=[], W=[r, O])
                        k.tt('dve', otm[:, G * 4:(G + 1) * 4, hh * 64:(hh + 1) * 64], Ov[:, :, 0:64],
                             cap(r[:, 0:1], [1, 4], [0, 64]), ALU.mult, R=[r], W=[otm, O])
                    items.append(('soft', fin))
                run_items(k, items, [0, 1, 2], E, ctr)
            s.otm_to_fm(es, otm, s.PC, ident, 6)
        return s.phase_end(f'{l}C2')

    def mixD1(s, l):
        k = s.k
        with ExitStack() as es:
            wDq = s.loadw(es, f'{l}wDq'); wDk = s.loadw(es, f'{l}wDk'); wDc = s.loadw(es, f'{l}wDc'); wDv = s.loadw(es, f'{l}wDv')
            bDv = s.loadrow(es, l, 'bDv')
            colt = k.alloc(es, (128, NCOL), F32, 'col'); k.dma(colt[:], s.col[l], W=[colt])
            cq = COFF['bDq'][0]; ck = COFF['bDk'][0]; cc_ = COFF['bDc'][0]
            hT = [k.alloc(es, (128, 8, 512), BF16, 'hT') for _ in range(2)]
            Ct = [k.alloc(es, (128, 512), F32, 'Ct') for _ in range(2)]
            St = [k.alloc(es, (128, 512), F32, 'St') for _ in range(2)]
            qn = [k.alloc(es, (128, 8, 512), BF16, 'qn') for _ in range(2)]
            kk_ = [k.alloc(es, (128, 4, 512), BF16, 'kk') for _ in range(2)]
            kcv = [k.alloc(es, (128, 4, 512), BF16, 'kcv') for _ in range(2)]
            vsw = [k.alloc(es, (128, 4, 4, 65), BF16, 'vsw') for _ in range(2)]
            gate = [k.alloc(es, (128, 4, 24), F32, 'gate') for _ in range(2)]
            ta = [k.alloc(es, (128, 512), F32, 'ta') for _ in range(2)]
            tb = [k.alloc(es, (128, 512), F32, 'tb') for _ in range(2)]
            vb = [k.alloc(es, (128, 280), F32, 'vb') for _ in range(2)]
            for v in vsw:
                k.memset('pool', v[:, :, :, 64:65], 1.0, W=[v])
            o64 = FOFF['C64'][0]; s64 = FOFF['S64'][0]

            def load(G):
                i = G % 2
                k.dma(hT[i][:], s.XT[:, :, G * 512:(G + 1) * 512], W=[hT[i]])
                k.dma(Ct[i][0:64, :], s.cf[0:64, o64 + G * 512:o64 + (G + 1) * 512], W=[Ct[i]])
                k.dma(St[i][0:64, :], s.cf[0:64, s64 + G * 512:s64 + (G + 1) * 512], W=[St[i]])
            load(0)
            n = 0
            for G in range(8):
                if G + 1 < 8: load(G + 1)
                gi = G % 2
                h = hT[gi]; C = Ct[gi]; Sn = St[gi]

                def roped(w, x0, s0, bx, bs_, dst_t, dst_ap):
                    nonlocal n
                    i = n % 2; n += 1
                    px = k.ps[i]; py = k.ps[2 + i]
                    for kk in range(8):
                        k.mm(px, px[0:64, :], w[:, kk, x0:x0 + 64], h[:, kk, :], R=[w, h], start=(kk == 0), stop=(kk == 7), inc=(kk == 7))
                    for kk in range(8):
                        k.mm(py, py[0:64, :], w[:, kk, s0:s0 + 64], h[:, kk, :], R=[w, h], start=(kk == 0), stop=(kk == 7), inc=(kk == 7))
                    k.stt(ta[i][0:64, :], px[0:64, :], colt[0:64, bx:bx + 1], C[0:64, :], ALU.add, ALU.mult, R=[colt, C], W=[ta[i], px])
                    k.stt(tb[i][0:64, :], py[0:64, :], colt[0:64, bs_:bs_ + 1], Sn[0:64, :], ALU.add, ALU.mult, R=[colt, Sn], W=[tb[i], py])
                    k.tt('pool', dst_ap, ta[i][0:64, :], tb[i][0:64, :], ALU.add, R=[ta[i], tb[i]], W=[dst_t])
                for hh in range(8):
                    roped(wDq, hh * 64, 512 + hh * 64, cq + hh, cq + 8 + hh, qn[gi], qn[gi][0:64, hh, :])
                for nm in range(2):
                    for g in range(2):
                        x0 = (nm * 2) * 128 + g * 64
                        roped(wDk, x0, x0 + 128, ck + (nm * 2) * 2 + g, ck + (nm * 2 + 1) * 2 + g, kk_[gi], kk_[gi][0:64, nm * 2 + g, :])
                for j in range(4):
                    pb = k.ps[4 + j % 2]
                    for kk in range(8):
                        k.mm(pb, pb[0:64, :], wDc[:, kk, j * 64:(j + 1) * 64], h[:, kk, :], R=[wDc, h], start=(kk == 0), stop=(kk == 7), inc=(kk == 7))
                    k.act(kcv[gi][0:64, j, :], pb[0:64, :], AF.Identity, R=[colt], W=[kcv[gi], pb], bias=colt[0:64, cc_ + j:cc_ + j + 1])
                for c in range(4):
                    i = c % 2
                    pb = k.ps[6 + i]
                    for kk in range(8):
                        k.mm(pb, pb[:, 0:280], h[:, kk, c * 128:(c + 1) * 128], wDv[:, kk, :], R=[wDv, h], start=(kk == 0), stop=(kk == 7), inc=(kk == 7))
                    k.tt('dve', vb[i][:], pb[:, 0:280], bDv[:], ALU.add, R=[bDv], W=[vb[i], pb])
                    k.cp('pool', vsw[gi][:, :, c, 0:64], vb[i][:, 0:256].rearrange("p (a b) -> p a b", a=4), R=[vb[i]], W=[vsw[gi]])
                    k.act(gate[gi][:, c, :], vb[i][:, 256:280], AF.Sigmoid, R=[vb[i]], W=[gate[gi]])
                k.dma(s.QN[:, :, G * 512:(G + 1) * 512], qn[gi][0:64, :, :], R=[qn[gi]])
                k.dma(s.KK[:, :, G * 512:(G + 1) * 512], kk_[gi][0:64, :, :], R=[kk_[gi]])
                k.dma(s.KCV[:, :, G * 512:(G + 1) * 512], kcv[gi][0:64, :, :], R=[kcv[gi]])
                k.dma(s.VSW[:, :, G * 4:(G + 1) * 4, :], vsw[gi][:], R=[vsw[gi]])
                k.dma(s.GATE[:, G * 4:(G + 1) * 4, :], gate[gi][:], R=[gate[gi]])
        return s.phase_end(f'{l}D1')

    def mixD2(s, l, bgn=0):
        k = s.k
        with ExitStack() as es:
            ident = s.loadw(es, 'ident'); negc = s.loadw(es, 'negc'); nega = s.loadw(es, 'nega')
            negcmp = s.loadw(es, 'negcmp'); ovl = s.loadw(es, 'ovl')
            id32 = s.loadcf(es, 'id32')
            CM = s.loadcf(es, 'CM'); ADD = s.loadcf(es, 'ADD')
            kcmp = k.alloc(es, (128, 2, 256), BF16, 'kcmp')
            vcmp = k.alloc(es, (128, 2, 2, 65), BF16, 'vcmp')
            onsa = k.alloc(es, (128, NT, 512), BF16, 'onsa')
            gate = k.alloc(es, (128, NT, 24), F32, 'gate')
            k.dma(gate[:], s.GATE[:], W=[gate])
            k.memset('dve', kcmp[:], 0.0, W=[kcmp])
            k.memset('dve', vcmp[:], 0.0, W=[vcmp])
            k.memset('dve', vcmp[:, :, :, 64:65], 1.0, W=[vcmp])
            with ExitStack() as e2:
                wck = s.loadw(e2, f'{l}wck'); wcks = s.loadw(e2, f'{l}wcks'); wcv = s.loadw(e2, f'{l}wcv')
                kcv = k.alloc(e2, (128, 4, S), BF16, 'kcv'); k.dma(kcv[0:64, :, :], s.KCV[:], W=[kcv])
                colt = k.alloc(e2, (128, NCOL), F32, 'col'); k.dma(colt[:], s.col[l], W=[colt])
                Cc = s.loadcf(e2, 'Ccmp'); Sc = s.loadcf(e2, 'Scmp')
                pebk = k.alloc(e2, (128, 32, 128), BF16, 'pebk'); pebv = k.alloc(e2, (128, 32, 128), BF16, 'pebv')
                ok_ = COFF['pek'][0]; ov_ = COFF['pev'][0]
                k.cp('dve', pebk[0:64, :, :], cap(colt[0:64, ok_:ok_ + 1], [1, 32], [0, 128]), R=[colt], W=[pebk])
                k.cp('dve', pebv[0:64, :, :], cap(colt[0:64, ov_:ov_ + 1], [1, 32], [0, 128]), R=[colt], W=[pebv])
                ta = k.alloc(e2, (128, 128), F32, 'ta'); tb = k.alloc(e2, (128, 128), F32, 'tb')
                for g in range(2):
                    for nt in range(2):
                        nn = 128 if nt == 0 else 127
                        n0 = nt * 128
                        px = k.ps[0]; py = k.ps[1]; pv = k.ps[2]
                        for (pp, w) in ((px, wck), (py, wcks)):
                            for ll in range(32):
                                rhs = cap(kcv[0:64, g, n0 * 16 + ll:n0 * 16 + ll + 1], [16, nn])
                                k.mm(pp, pp[0:64, 0:nn], w[0:64, ll, :], rhs, R=[w, kcv], start=(ll == 0), stop=False, inc=False)
                            for ll in range(32):
                                k.mm(pp, pp[0:64, 0:nn], w[0:64, ll, :], pebk[0:64, ll, 0:nn], R=[w, pebk],
                                     start=False, stop=(ll == 31), inc=(ll == 31))
                        k.tt('dve', ta[0:64, 0:nn], px[0:64, 0:nn], Cc[0:64, n0:n0 + nn], ALU.mult, R=[Cc], W=[ta, px])
                        k.tt('dve', tb[0:64, 0:nn], py[0:64, 0:nn], Sc[0:64, n0:n0 + nn], ALU.mult, R=[Sc], W=[tb, py])
                        k.tt('pool', kcmp[0:64, g, n0:n0 + nn], ta[0:64, 0:nn], tb[0:64, 0:nn], ALU.add, R=[ta, tb], W=[kcmp])
                        for ll in range(32):
                            lhs = cap(kcv[0:64, 2 + g, n0 * 16 + ll:n0 * 16 + ll + 1], [16, nn])
                            k.mm(pv, pv[0:nn, 0:64], lhs, wcv[0:64, ll, :], R=[wcv, kcv], start=(ll == 0), stop=False, inc=False)
                        for ll in range(32):
                            k.mm(pv, pv[0:nn, 0:64], pebv[0:64, ll, 0:nn], wcv[0:64, ll, :], R=[wcv, pebv],
                                 start=False, stop=(ll == 31), inc=(ll == 31))
                        k.cp('act', vcmp[0:nn, nt, g, 0:64], pv[0:nn, 0:64], R=[], W=[vcmp, pv])
                k.barrier()
            qa = k.alloc(es, (128, 4, S), BF16, 'qa')
            ksa = k.alloc(es, (128, S), BF16, 'ksa')
            kw = k.alloc(es, (128, S), BF16, 'kw')
            vs = k.alloc(es, (128, NT, 65), BF16, 'vs'); vw = k.alloc(es, (128, NT, 65), BF16, 'vw')
            E = [k.alloc(es, (128, 512), BF16, 'E') for _ in range(3)]
            qas = T(qa.h)
            k.dma(ksa[64:128, :], s.wsl('eexp')[0:64, :], W=[ksa])
            k.memset('dve', qa[64:128, :, :], 0.0, W=[qas])
            sm = [type('o', (), {})() for _ in range(2)]
            for o in sm:
                o.dn = k.alloc(es, (128, 4), F32, 'dn'); o.rc = k.alloc(es, (128, 12), F32, 'rc')
                o.imp = k.alloc(es, (128, 64), F32, 'imp'); o.i2 = k.alloc(es, (128, 64), F32, 'i2')
                o.m8 = k.alloc(es, (128, 8), F32, 'm8'); o.thr = k.alloc(es, (128, 1), F32, 'thr')
                o.sel = k.alloc(es, (128, 128), F32, 'sel')
                k.memset('dve', o.sel[:], 0.0, W=[o.sel])
                o.coef = k.alloc(es, (128, 3, 4), F32, 'coef')
                o.o1 = k.alloc(es, (128, 4, 64), F32, 'o1'); o.o2 = k.alloc(es, (128, 4, 64), F32, 'o2'); o.o3 = k.alloc(es, (128, 4, 64), F32, 'o3')
            ctr = [0]
            Oc = k.ps[3]; IMP = k.ps[4]; Os = k.ps[5]; Ow = k.ps[6]; pT = k.ps[7]
            bc4 = lambda t: cap(t[:, 0:1], [0, 4], [1, 128])
            v4 = lambda Sb: Sb[:, :].rearrange("p (a b) -> p a b", a=4)
            for g in range(2):
                k.dma(qa[0:64, :, :], s.QN[:, 4 * g:4 * g + 4, :], W=[qa])
                k.dma(ksa[0:64, :], s.KK[:, g, :], W=[ksa]); k.dma(kw[0:64, :], s.KK[:, 2 + g, :], W=[kw])
                k.dma(vs[:], s.VSW[:, g, :, :], W=[vs]); k.dma(vw[:], s.VSW[:, 2 + g, :, :], W=[vw])
                items = []
                O4 = lambda Ob: Ob[:, 0:260].rearrange("p (a b) -> p a b", a=4)
                gv = lambda qt, b: cap(gate[:, qt, g * 12 + b:g * 12 + b + 1], [3, 4])

                def xexp(Sb, Eb):
                    k.act(Eb[:], Sb[:, :], AF.Exp, R=[], W=[Eb, Sb], scale=0.125)

                def zero_all():
                    k.memset('dve', Oc[:, 0:260], 0.0, W=[Oc]); k.memset('dve', IMP[:, 0:256], 0.0, W=[IMP])
                    k.memset('dve', Os[:, 0:260], 0.0, W=[Os]); k.memset('dve', Ow[:, 0:260], 0.0, W=[Ow])

                def cmp_items(qt):
                    qsl = slice(qt * 128, (qt + 1) * 128)
                    o = sm[qt % 2]
                    out = []
                    for nt in ([0] if qt < 16 else [0, 1]):
                        def fS(Sb, nt=nt):
                            k.mm(Sb, v4(Sb), kcmp[0:64, g, nt * 128:(nt + 1) * 128], qa[0:64, :, qsl], R=[kcmp, qa], start=True, stop=False, inc=False)
                            k.mm(Sb, v4(Sb), ident[:], bc4(negcmp[:, nt, qsl]), R=[ident, negcmp], start=False, stop=True)

                        def fP(Eb, nt=nt):
                            for hh in range(4):
                                k.mm(Oc, Oc[:, hh * 65:(hh + 1) * 65], Eb[:, hh * 128:(hh + 1) * 128], vcmp[:, nt, g, :],
                                     R=[Eb, vcmp], start=False, stop=True, skip=True, inc=(hh == 3))
                            for hh in range(4):
                                k.mm(IMP, IMP[:, hh * 64:(hh + 1) * 64], Eb[:, hh * 128:(hh + 1) * 128], ovl[:, nt, :],
                                     R=[Eb, ovl], start=False, stop=True, skip=True, inc=(hh == 3))
                        out.append(U(fS, xexp, fP))

                    def sel1():
                        s.bg(bgn)
                        Ocv = O4(Oc)
                        k.ts('dve', o.dn[:], Ocv[:, :, 64], 1e-30, None, ALU.max, None, R=[], W=[o.dn, Oc])
                        k.op('dve', lambda g_: g_.reciprocal(out=o.rc[:, 0:4], in_=o.dn[:]), R=[o.dn], W=[o.rc])
                        k.ts('dve', o.imp[:], IMP[:, 0:64], o.rc[:, 0:1], None, ALU.mult, None, R=[o.rc], W=[o.imp, IMP])
                        for hh in range(1, 4):
                            dstb, srcb = (o.i2, o.imp) if hh % 2 == 1 else (o.imp, o.i2)
                            k.stt(dstb[:], IMP[:, hh * 64:(hh + 1) * 64], o.rc[:, hh:hh + 1], srcb[:], ALU.mult, ALU.add,
                                  R=[o.rc, srcb], W=[dstb, IMP])
                        k.tt('dve', o.imp[:], o.i2[:], CM[:, qt * 64:(qt + 1) * 64], ALU.mult, R=[o.i2, CM], W=[o.imp])
                        k.tt('dve', o.i2[:], o.imp[:], ADD[:, qt * 64:(qt + 1) * 64], ALU.add, R=[o.imp, ADD], W=[o.i2])
                        k.op('dve', lambda g_: g_.max(out=o.m8[:], in_=o.i2[:]), R=[o.i2], W=[o.m8])
                        k.ts('dve', o.thr[:], o.m8[:, 7:8], 0.0, None, ALU.max, None, R=[o.m8], W=[o.thr])
                        k.ts('dve', o.sel[:, 64:128], o.i2[:], o.thr[:, 0:1], 1.0, ALU.is_ge, ALU.subtract, R=[o.i2, o.thr], W=[o.sel])
                        k.tt('dve', o.coef[:, 0, :], gv(qt, 0), o.rc[:, 0:4], ALU.mult, R=[gate, o.rc], W=[o.coef])
                        k.tt('dve', o.o1[:], Ocv[:, :, 0:64], cap(o.coef[:, 0, 0:1], [1, 4], [0, 64]), ALU.mult, R=[o.coef], W=[o.o1, Oc])
                        k.memset('dve', Oc[:, 0:260], 0.0, W=[Oc]); k.memset('dve', IMP[:, 0:256], 0.0, W=[IMP])
                    out.append(('soft', sel1))
                    return out

                items.append(('soft', zero_all))
                items += cmp_items(0)
                for qt in range(NT):
                    o = sm[qt % 2]
                    qsl = slice(qt * 128, (qt + 1) * 128)
                    def sel2(o=o, qsl=qsl):
                        k.tr(pT, pT[:, 0:128], o.sel[:], id32[:], R=[o.sel, id32])
                        k.cp('dve', qa[64:128, :, qsl], cap(pT[64:128, 0:1], [0, 4], [1, 128]), R=[], W=[qas, pT])
                    items.append(('hard', sel2))
                    for kt in range(max(0, qt - 4), qt + 1):
                        msk = negc if kt == qt else (nega if kt == qt - 4 else None)

                        def fS(Sb, kt=kt, msk=msk, qsl=qsl):
                            k.mm(Sb, v4(Sb), kw[0:64, kt * 128:(kt + 1) * 128], qa[0:64, :, qsl], R=[kw, qa],
                                 start=True, stop=(msk is None), inc=(msk is None))
                            if msk is not None:
                                k.mm(Sb, v4(Sb), ident[:], bc4(msk), R=[ident, msk], start=False, stop=True)

                        def fP(Eb, kt=kt):
                            for hh in range(4):
                                k.mm(Ow, Ow[:, hh * 65:(hh + 1) * 65], Eb[:, hh * 128:(hh + 1) * 128], vw[:, kt, :],
                                     R=[Eb, vw], start=False, stop=True, skip=True, inc=(hh == 3))
                        items.append(U(fS, xexp, fP))

                    def combW(o=o, qt=qt):
                        Owv = O4(Ow)
                        k.op('dve', lambda g_: g_.reciprocal(out=o.rc[:, 8:12], in_=Owv[:, :, 64]), R=[], W=[o.rc, Ow])
                        k.tt('dve', o.coef[:, 2, :], gv(qt, 2), o.rc[:, 8:12], ALU.mult, R=[gate, o.rc], W=[o.coef])
                        k.tt('dve', o.o3[:], Owv[:, :, 0:64], cap(o.coef[:, 2, 0:1], [1, 4], [0, 64]), ALU.mult, R=[o.coef], W=[o.o3, Ow])
                        k.memset('dve', Ow[:, 0:260], 0.0, W=[Ow])
                    items.append(('soft', combW))

                    for kt in range(qt + 1):
                        def fS(Sb, kt=kt, qt=qt, qsl=qsl):
                            k.mm(Sb, v4(Sb), ksa[:, kt * 128:(kt + 1) * 128], qa[:, :, qsl], R=[ksa, qa, qas],
                                 start=True, stop=(kt != qt), inc=(kt != qt))
                            if kt == qt:
                                k.mm(Sb, v4(Sb), ident[:], bc4(negc), R=[ident, negc], start=False, stop=True)

                        def fP(Eb, kt=kt):
                            for hh in range(4):
                                k.mm(Os, Os[:, hh * 65:(hh + 1) * 65], Eb[:, hh * 128:(hh + 1) * 128], vs[:, kt, :],
                                     R=[Eb, vs], start=False, stop=True, skip=True, inc=(hh == 3))
                        items.append(U(fS, xexp, fP))
                        if kt == 0 and qt + 1 < NT:
                            items += cmp_items(qt + 1)

                    def combS(o=o, qt=qt):
                        Osv = O4(Os)
                        k.op('dve', lambda g_: g_.reciprocal(out=o.rc[:, 4:8], in_=Osv[:, :, 64]), R=[], W=[o.rc, Os])
                        k.tt('dve', o.coef[:, 1, :], gv(qt, 1), o.rc[:, 4:8], ALU.mult, R=[gate, o.rc], W=[o.coef])
                        k.tt('dve', o.o2[:], Osv[:, :, 0:64], cap(o.coef[:, 1, 0:1], [1, 4], [0, 64]), ALU.mult, R=[o.coef], W=[o.o2, Os])
                        k.memset('dve', Os[:, 0:260], 0.0, W=[Os])
                        k.tt('pool', o.o1[:], o.o1[:], o.o2[:], ALU.add, R=[o.o1, o.o2], W=[o.o1])
                        k.tt('pool', onsa[:, qt, g * 256:(g + 1) * 256].rearrange("p (a b) -> p a b", a=4), o.o1[:], o.o3[:], ALU.add,
                             R=[o.o1, o.o3], W=[onsa])
                    items.append(('soft', combS))
                run_items(k, items, [0, 1, 2], E, ctr)
            s.otm_to_fm(es, onsa, s.PD, ident, 0)
        return s.phase_end(f'{l}D2')

    def merge1(s, l):
        k = s.k
        PS = [s.PA, s.PB, s.PC, s.PD]
        for half in range(2):
            with ExitStack() as es:
                wG = []; wos = []
                for b, c in enumerate('ABCD'):
                    t = k.alloc(es, (128, 8, 512), BF16, 'wG')
                    k.dma(t[:], s.wsl(f'{l}wG')[:, :, b * 1024 + half * 512:b * 1024 + (half + 1) * 512], W=[t]); wG.append(t)
                    t = k.alloc(es, (128, 4, 512), BF16, 'wos')
                    k.dma(t[:], s.wsl(f'{l}wo{c}')[:, :, half * 512:(half + 1) * 512], W=[t]); wos.append(t)
                colt = k.alloc(es, (128, NCOL), F32, 'col'); k.dma(colt[:], s.col[l], W=[colt])
                cg = COFF['bG'][0]
                hT = [k.alloc(es, (128, 8, 512), BF16, 'hT') for _ in range(2)]
                Pin = [[k.alloc(es, (128, 4, 512), BF16, 'Pin') for _ in range(4)] for _ in range(2)]
                mT = [k.alloc(es, (128, 4, 512), BF16, 'mT') for _ in range(2)]
                sig = [k.alloc(es, (128, 512), F32, 'sig') for _ in range(2)]
                acc = [k.alloc(es, (128, 512), F32, 'acc') for _ in range(2)]
                tm = [k.alloc(es, (128, 512), F32, 'tm') for _ in range(2)]

                def load(G):
                    i = G % 2
                    k.dma(hT[i][:], s.XT[:, :, G * 512:(G + 1) * 512], W=[hT[i]])
                    for b in range(4):
                        k.dma(Pin[i][b][:], PS[b][:, :, G * 512:(G + 1) * 512], W=[Pin[i][b]])
                load(0)
                n = 0
                for G in range(8):
                    if G + 1 < 8: load(G + 1)
                    gi = G % 2
                    h = hT[gi]; m = mT[gi]
                    for dcl in range(4):
                        dc = half * 4 + dcl
                        a = acc[dcl % 2]
                        for b in range(4):
                            i = n % 2; n += 1
                            pg = k.ps[i]; py = k.ps[2 + i]
                            for kk in range(8):
                                k.mm(pg, pg[:, :], wG[b][:, kk, dcl * 128:(dcl + 1) * 128], h[:, kk, :], R=[wG[b], h], start=(kk == 0), stop=(kk == 7), inc=(kk == 7))
                            for fc in range(4):
                                k.mm(py, py[:, :], wos[b][:, fc, dcl * 128:(dcl + 1) * 128], Pin[gi][b][:, fc, :], R=[wos[b], Pin[gi][b]],
                                     start=(fc == 0), stop=(fc == 3), inc=(fc == 3))
                            k.act(sig[i][:], pg[:, :], AF.Sigmoid, R=[colt], W=[sig[i], pg], bias=colt[:, cg + b * 8 + dc:cg + b * 8 + dc + 1])
                            if b == 0:
                                k.tt('dve', a[:], sig[i][:], py[:, :], ALU.mult, R=[sig[i]], W=[a, py])
                            else:
                                k.tt('dve', tm[i][:], sig[i][:], py[:, :], ALU.mult, R=[sig[i]], W=[tm[i], py])
                                if b < 3:
                                    k.tt('pool', a[:], a[:], tm[i][:], ALU.add, R=[a, tm[i]], W=[a])
                                else:
                                    k.tt('pool', m[:, dcl, :], a[:], tm[i][:], ALU.add, R=[a, tm[i]], W=[m])
                    k.dma(s.MT[:, half * 4:(half + 1) * 4, G * 512:(G + 1) * 512], m[:], R=[m])
            s.k.barrier()
        return s.phase_end(f'{l}M1')

    def merge2(s, l):
        k = s.k
        with ExitStack() as es:
            wo = s.loadw(es, f'{l}wo'); ident = s.loadw(es, 'ident')
            ln = s.ln_setup(es, l, 2)
            mT = [k.alloc(es, (128, 8, 512), BF16, 'mT') for _ in range(2)]
            xr = [k.alloc(es, (128, D), F32, 'xr') for _ in range(4)]
            pre = [k.alloc(es, (128, D), F32, 'pre') for _ in range(4)]

            def load(G):
                k.dma(mT[G % 2][:], s.MT[:, :, G * 512:(G + 1) * 512], W=[mT[G % 2]])

            def ldx(tt):
                if tt < NT: k.dma(xr[tt % 4][:], s.XR[tt * 128:(tt + 1) * 128, :], W=[xr[tt % 4]])
            load(0); ldx(0); ldx(1)
            for G in range(8):
                if G + 1 < 8: load(G + 1)
                m = mT[G % 2]
                for t4 in range(4):
                    tt = G * 4 + t4
                    xx = xr[tt % 4]; pp = pre[tt % 4]
                    ldx(tt + 2)
                    for dh in range(2):
                        pb = k.ps[(tt % 2) * 2 + dh]
                        for kk in range(8):
                            k.mm(pb, pb[:, :], m[:, kk, t4 * 128:(t4 + 1) * 128], wo[:, kk, dh * 512:(dh + 1) * 512], R=[m, wo],
                                 start=(kk == 0), stop=(kk == 7), inc=(kk == 7))
                        k.stt(pp[:, dh * 512:(dh + 1) * 512], xx[:, dh * 512:(dh + 1) * 512], ALPHA, pb[:, :], ALU.mult, ALU.add,
                              R=[xx], W=[pp, pb])
                    s.ln_tile(ln, pp, tt, s.XR, 4 + tt % 2, ident, True)
            s.flush(0)
        return s.phase_end(f'{l}M2')

    def xattn(s, l):
        k = s.k
        sc = float(128 ** -0.5)
        GT = 256
        with ExitStack() as es:
            xq = s.loadw(es, f'{l}xq'); xk = s.loadw(es, f'{l}xk'); xv = s.loadw(es, f'{l}xv'); xo = s.loadw(es, f'{l}xo')
            ident = s.loadw(es, 'ident')
            ln = s.ln_setup(es, l, 3)
            memT = k.alloc(es, (128, 8, MEM), BF16, 'memT')
            kT = k.alloc(es, (128, 4, MEM), BF16, 'kT')
            va = k.alloc(es, (128, 2, 4, 129), BF16, 'va')
            k.memset('dve', va[:, :, :, 128:129], 1.0, W=[va])
            with ExitStack() as e2:
                mf = k.alloc(e2, (128, D), F32, 'mf'); mb = k.alloc(e2, (128, D), BF16, 'mb')
                for mt in range(2):
                    k.dma(mf[:], s.mem_in[mt * 128:(mt + 1) * 128, :], W=[mf])
                    k.cp('dve', mb[:], mf[:], R=[mf], W=[mb])
                    pt = k.ps[mt]; ptv = pt[:].bitcast(BF16)
                    for j in range(8):
                        k.tr(pt, ptv[:, j * 128:(j + 1) * 128], mb[:, j * 128:(j + 1) * 128], ident[:], R=[mb, ident], inc=(j == 7))
                    k.cp('act', memT[:, :, mt * 128:(mt + 1) * 128], ptv[:, :].rearrange("p (a b) -> p a b", a=8), R=[], W=[memT, pt])
                for hh in range(4):
                    pb = k.ps[2 + hh % 2]
                    for kk in range(8):
                        k.mm(pb, pb[:, 0:MEM], xk[:, kk, hh * 128:(hh + 1) * 128], memT[:, kk, :], R=[xk, memT], start=(kk == 0), stop=(kk == 7), inc=(kk == 7))
                    k.cp('act', kT[:, hh, :], pb[:, 0:MEM], R=[], W=[kT, pb])
                for mt in range(2):
                    pb = k.ps[4 + mt]
                    for kk in range(8):
                        k.mm(pb, pb[:, :], memT[:, kk, mt * 128:(mt + 1) * 128], xv[:, kk, :], R=[xv, memT], start=(kk == 0), stop=(kk == 7), inc=(kk == 7))
                    k.cp('act', va[:, mt, :, 0:128], pb[:, :].rearrange("p (a b) -> p a b", a=4), R=[], W=[va, pb])
                k.barrier()
            xT = [k.alloc(es, (128, 8, GT), BF16, 'xT') for _ in range(2)]
            qT = [k.alloc(es, (128, 4, GT), BF16, 'qT') for _ in range(2)]
            E = [k.alloc(es, (128, GT), BF16, 'E') for _ in range(3)]
            otm = [k.alloc(es, (128, 2, 512), BF16, 'otm') for _ in range(2)]
            oT = [k.alloc(es, (128, 4, GT), BF16, 'oT') for _ in range(2)]
            rc = [k.alloc(es, (128, 2), F32, 'rc') for _ in range(2)]
            xr = [k.alloc(es, (128, D), F32, 'xr') for _ in range(4)]
            pre = [k.alloc(es, (128, D), F32, 'pre') for _ in range(4)]

            def load(G):
                k.dma(xT[G % 2][:], s.XT[:, :, G * GT:(G + 1) * GT], W=[xT[G % 2]])

            def ldx(tt):
                if tt < NT: k.dma(xr[tt % 4][:], s.XR[tt * 128:(tt + 1) * 128, :], W=[xr[tt % 4]])
            cn = {'ne': 0, 'nh': 0}; xctr = [0]
            NGX = S // GT

            def stage1(G):
                gi = G % 2
                x = xT[gi]; q = qT[gi]; ot = otm[gi]
                for hh in range(4):
                    pb = k.ps[hh % 2]
                    for kk in range(8):
                        k.mm(pb, pb[:, 0:GT], xq[:, kk, hh * 128:(hh + 1) * 128], x[:, kk, :], R=[xq, x], start=(kk == 0), stop=(kk == 7), inc=(kk == 7))
                    k.cp('dve', q[:, hh, :], pb[:, 0:GT], R=[], W=[q, pb])
                items = []
                for hh in range(4):
                    O = k.ps[2 + cn['nh'] % 2]; r = rc[cn['nh'] % 2]; cn['nh'] += 1
                    items.append(('soft', lambda O=O: k.memset('dve', O[:, 0:258], 0.0, W=[O])))
                    for mt in range(2):
                        def fS(Sb, hh=hh, mt=mt):
                            k.mm(Sb, Sb[:, 0:GT], kT[:, hh, mt * 128:(mt + 1) * 128], q[:, hh, :], R=[kT, q])

                        def fX(Sb, Eb):
                            k.act(Eb[:], Sb[:, 0:GT], AF.Exp, R=[], W=[Eb, Sb], scale=sc)

                        def fP(Eb, hh=hh, mt=mt, O=O):
                            for qs in range(2):
                                k.mm(O, O[:, qs * 129:(qs + 1) * 129], Eb[:, qs * 128:(qs + 1) * 128], va[:, mt, hh, :], R=[Eb, va],
                                     start=False, stop=True, skip=True, inc=(qs == 1))
                        items.append(U(fS, fX, fP))

                    def fin(O=O, r=r, hh=hh):
                        Ov = O[:, 0:258].rearrange("p (a b) -> p a b", a=2)
                        k.op('dve', lambda g_: g_.reciprocal(out=r[:], in_=Ov[:, :, 128]), R=[], W=[r, O])
                        k.tt('dve', ot[:, :, hh * 128:(hh + 1) * 128], Ov[:, :, 0:128], cap(r[:, 0:1], [1, 2], [0, 128]), ALU.mult, R=[r], W=[ot, O])
                    items.append(('soft', fin))
                run_items(k, items, [4, 5], E, xctr)

            def stage2(G):
                gi = G % 2
                ot = otm[gi]; o_T = oT[gi]
                for qs in range(2):
                    pt = k.ps[6]; ptv = pt[:].bitcast(BF16)
                    for j in range(4):
                        k.tr(pt, ptv[:, j * 128:(j + 1) * 128], ot[:, qs, j * 128:(j + 1) * 128], ident[:], R=[ot, ident], inc=(j == 3))
                    k.cp('act', o_T[:, :, qs * 128:(qs + 1) * 128], ptv[:, 0:512].rearrange("p (a b) -> p a b", a=4), R=[], W=[o_T, pt])
                for qs in range(2):
                    tt = G * 2 + qs
                    xx = xr[tt % 4]; pp = pre[tt % 4]
                    ldx(tt + 2)
                    for dh in range(2):
                        pb = k.ps[dh]
                        for kk in range(4):
                            k.mm(pb, pb[:, :], o_T[:, kk, qs * 128:(qs + 1) * 128], xo[:, kk, dh * 512:(dh + 1) * 512], R=[o_T, xo],
                                 start=(kk == 0), stop=(kk == 3), inc=(kk == 3))
                        k.stt(pp[:, dh * 512:(dh + 1) * 512], xx[:, dh * 512:(dh + 1) * 512], ALPHA, pb[:, :], ALU.mult, ALU.add,
                              R=[xx], W=[pp, pb])
                    s.ln_tile(ln, pp, tt, s.XR, 7, ident, True)

            load(0); load(1); ldx(0); ldx(1)
            stage1(0)
            for G in range(NGX):
                if G + 1 < NGX:
                    stage1(G + 1)
                    if G + 2 < NGX: load(G + 2)
                stage2(G)
            s.flush(0)
        return s.phase_end(f'{l}X')

    def build(s):
        s.prep()
        s.xt0()
        endL0 = WOFF['0f2w2'][0] + WOFF['0f2w2'][1]
        for l in range(L):
            src = s.x_in if l == 0 else s.XR
            first = (l == 0)
            if first:
                pass
            stop = s._ffn_bg(l, 'f1', 1, src, s.XR, True, 6 if first else 0, endL0 if first else None)
            if stop: break
            if s.mixA(l): break
            if s.mixB(l): break
            if s.mixC1(l): break
            if s.mixC2(l, bgn=1 if first else 0): break
            if s.mixD1(l): break
            if s.mixD2(l, bgn=2 if first else 0): break
            if s.merge1(l): break
            if s.merge2(l): break
            if s.xattn(l): break
            last = (l == L - 1)
            if s._ffn_bg(l, 'f2', 4, s.XR, s.out if (last and not s.dbg) else s.XR, not last, 2 if first else 0, NWB if first else None): break
        s.k.barrier()
        s.k.es.close()

    def _ffn_bg(s, l, f, lni, src, dst, write_xt, bgn, until):
        s._bg_until_col = until
        return s.ffn(l, f, lni, src, dst, write_xt, bgn)


_CACHE = {}


def _get_nc(dbg=False, stop_after=None):
    key = (dbg, stop_after)
    if key not in _CACHE:
        nc = bass.Bass("TRN2", target_bir_lowering=False)
        Prog(nc, dbg, stop_after).build()
        _CACHE[key] = nc
    return _CACHE[key]


def _inmaps(inputs):
    wb, col, row, cf = _pack(inputs)
    x = np.asarray(inputs['x'], dtype=np.float32); mem = np.asarray(inputs['mem'], dtype=np.float32)
    return [{"x": np.ascontiguousarray(x[c]), "mem": np.ascontiguousarray(mem[c]), "wbig": wb, "col": col, "row": row, "cf": cf}
            for c in range(8)]


def kernel(**inputs):
    nc = _get_nc()
    res = run_bass_kernel_spmd(nc, _inmaps(inputs), core_ids=list(range(8)))
    return np.stack([np.asarray(r["out"], dtype=np.float32) for r in res.results], axis=0)
```

```python
import numpy as np
import concourse.bass as bass
import concourse.mybir as mybir
from concourse.bass_utils import run_bass_kernel_spmd
from contextlib import ExitStack

F32 = mybir.dt.float32
BF16 = mybir.dt.bfloat16
AF = mybir.ActivationFunctionType
ALU = mybir.AluOpType

S = 4096; D = 1024; FF = 2816; NT = 32; L = 2; MEM = 256
ALPHA = float(4 ** 0.25)
NEGV = -30000.0
LN_EPS = 1e-5; RMS_EPS = 1e-6
IN_SPLITS = (512, 512, 512, 512, 512, 256, 128, 32, 512) + (128,) * 6 + (24,) + (1024,) * 4
IN_NAMES = ['u', 'v', 'cb', 'cc', 'ch', 'qlat', 'kvlat', 'krope', 'nq', 'nkc', 'nvc', 'nks', 'nvs', 'nkw', 'nvw',
            'ngate', 'ga', 'gb', 'gc', 'gd']


def _offsets(spec):
    off = {}; o = 0
    for n, shp in spec:
        sz = int(np.prod(shp))
        off[n] = (o, sz, tuple(shp)); o += sz
    return off, o


def wspec():
    sp = [('ident', (128,)), ('tril', (128,)), ('negc', (128,)), ('nega', (128,)), ('eexp', (4096,)),
          ('negcmp', (2, 4096)), ('ovl', (2, 64))]
    for l in range(L):
        sp += [(f'{l}f1w1', (8, FF)), (f'{l}f1w3', (8, FF)), (f'{l}f1w2', (22, D)),
               (f'{l}wA', (8, 1024)), (f'{l}wB', (8, 1536)), (f'{l}wC1', (8, 384)), (f'{l}wC2', (8, 64)),
               (f'{l}wDq', (8, 1024)), (f'{l}wDk', (8, 512)), (f'{l}wDc', (8, 256)), (f'{l}wDv', (8, 280)),
               (f'{l}wG', (8, 4096)),
               (f'{l}WsT', (4, 128)), (f'{l}woA', (4, D)), (f'{l}woB', (4, D)), (f'{l}woC', (4, D)), (f'{l}woD', (4, D)),
               (f'{l}wuq', (2, 768)), (f'{l}wuqs', (2, 768)), (f'{l}wukv', (1024,)),
               (f'{l}wck', (32, 64)), (f'{l}wcks', (32, 64)), (f'{l}wcv', (32, 64)),
               (f'{l}wo', (8, D)), (f'{l}xq', (8, 512)), (f'{l}xk', (8, 512)), (f'{l}xv', (8, 512)), (f'{l}xo', (4, D)),
               (f'{l}f2w1', (8, FF)), (f'{l}f2w3', (8, FF)), (f'{l}f2w2', (22, D))]
    return sp


CSPEC = [('bAu', 4), ('bB', 12), ('cw', 12), ('bC2', 2), ('bDq', 16), ('bDk', 8), ('bDc', 4), ('bG', 32),
         ('pek', 32), ('pev', 32)]
RSPEC = [('bAv', 512), ('glng', 512), ('glnb', 512), ('bs', 512), ('bC1', 384), ('qg', 256), ('kvg', 128),
         ('bDv', 280)] + [(f'ln{i}{c}', 1024) for i in (1, 2, 3, 4) for c in 'gb']
FSPEC = [('C64', 4096), ('S64', 4096), ('C96', 4096), ('S96', 4096), ('Ccmp', 256), ('Scmp', 256),
         ('CM', 2048), ('ADD', 2048), ('id32', 128)]

WOFF, NWB = _offsets(wspec())
COFF, NCOL = _offsets([(n, (s,)) for n, s in CSPEC])
ROFF, NROW = _offsets([(n, (s,)) for n, s in RSPEC])
FOFF, NCF = _offsets([(n, (s,)) for n, s in FSPEC])


def _pk(w):
    k, n = w.shape
    return np.ascontiguousarray(w.reshape(k // 128, 128, n).transpose(1, 0, 2))


def _rope_tab(pos, dim):
    inv = (np.float32(10000.0) ** (-(np.arange(0, dim, 2, dtype=np.float32) / np.float32(dim)))).astype(np.float32)
    ang = pos.astype(np.float32)[:, None] * inv[None, :]
    return np.cos(ang).astype(np.float32), np.sin(ang).astype(np.float32)


def _consts_bf():
    j = np.arange(128)[:, None]; i = np.arange(128)[None, :]
    c = {}
    c['ident'] = (j == i).astype(np.float32)
    c['tril'] = (j <= i).astype(np.float32)
    c['negc'] = np.where(j <= i, 0.0, NEGV).astype(np.float32)
    c['nega'] = np.where(j > i, 0.0, NEGV).astype(np.float32)
    ee = np.zeros((128, 4096), np.float32)
    key = np.arange(4096)
    for jj in range(64):
        ee[jj, key // 64 == jj] = -NEGV
    c['eexp'] = ee
    idx = (np.arange(2)[None, :, None] * 128 + np.arange(128)[:, None, None])
    q = np.arange(4096)[None, None, :]
    c['negcmp'] = np.where((idx < 255) & (16 * idx + 31 <= q), 0.0, NEGV).astype(np.float32)
    jj = np.arange(64)[None, None, :]
    ov = np.minimum(16 * idx + 32, 64 * jj + 64) - np.maximum(16 * idx, 64 * jj)
    ov = np.clip(ov, 0, None).astype(np.float32) / 32.0
    c['ovl'] = np.where(idx < 255, ov, 0.0).astype(np.float32)
    return c


def _consts_f32():
    cf = np.zeros((128, NCF), np.float32)
    pos = np.arange(S, dtype=np.float32)
    c64, s64 = _rope_tab(pos, 64)
    c32, s32 = _rope_tab(pos, 32)
    r = np.arange(128)
    C64 = c64.T[r % 32]
    S64 = np.where(((r % 64) < 32)[:, None], -s64.T[r % 32], s64.T[r % 32])
    C96 = np.zeros((128, S), np.float32); S96 = np.zeros((128, S), np.float32)
    C96[0:64] = 1.0
    rr = np.arange(32)
    C96[64:96] = c32.T[rr % 16]
    S96[64:96] = np.where((rr < 16)[:, None], -s32.T[rr % 16], s32.T[rr % 16])
    pc = (np.arange(255) * 16 + 31).astype(np.float32)
    cc, sc = _rope_tab(pc, 64)
    Cc = np.zeros((128, 256), np.float32); Sc = np.zeros((128, 256), np.float32)
    Cc[:, :255] = cc.T[r % 32]
    Sc[:, :255] = np.where(((r % 64) < 32)[:, None], -sc.T[r % 32], sc.T[r % 32])
    qq = (np.arange(32)[None, :, None] * 128 + np.arange(128)[:, None, None])
    jq = qq // 64
    jj = np.arange(64)[None, None, :]
    forced = (jj == 0) | (jj == jq) | (jj == jq - 1)
    causal = jj <= jq
    CM = (causal & ~forced).astype(np.float32)
    ADD = np.where(forced, 1e9, np.where(causal, 0.0, -1.0)).astype(np.float32)
    for n, a in (('C64', C64), ('S64', S64), ('C96', C96), ('S96', S96), ('Ccmp', Cc), ('Scmp', Sc),
                 ('CM', CM.reshape(128, -1)), ('ADD', ADD.reshape(128, -1)), ('id32', np.eye(128, dtype=np.float32))):
        o, sz, _ = FOFF[n]
        cf[:, o:o + sz] = a
    return cf


def _pack(inp):
    wb = np.zeros((128, NWB), np.float32)
    col = np.zeros((L, 128, NCOL), np.float32)
    row = np.zeros((L, 1, NROW), np.float32)

    def put(name, a):
        o, sz, shp = WOFF[name]
        assert a.shape == (128,) + shp, (name, a.shape, shp)
        wb[:, o:o + sz] = a.reshape(128, sz)

    def putc(l, name, a):
        o, sz, _ = COFF[name]
        col[l, :, o:o + sz] = a

    def putr(l, name, a):
        o, sz, _ = ROFF[name]
        row[l, 0, o:o + sz] = a

    offs = np.cumsum((0,) + IN_SPLITS)
    sw32 = np.concatenate([np.arange(16, 32), np.arange(0, 16)])
    sw64 = np.concatenate([np.arange(32, 64), np.arange(0, 32)])
    swq = np.concatenate([h * 64 + sw64 for h in range(8)])
    fm = lambda v: np.ascontiguousarray(v.reshape(-1, 128).T)
    for l in range(L):
        g = lambda n: np.asarray(inp[n][l], dtype=np.float32)
        win = g('w_in'); bin_ = g('b_in')
        cs = {n: win[:, offs[i]:offs[i + 1]] for i, n in enumerate(IN_NAMES)}
        bs_ = {n: bin_[offs[i]:offs[i + 1]] for i, n in enumerate(IN_NAMES)}
        for f, pre in (('f1', 'ffn1'), ('f2', 'ffn2')):
            put(f'{l}{f}w1', _pk(g(pre + '_w1'))); put(f'{l}{f}w3', _pk(g(pre + '_w3'))); put(f'{l}{f}w2', _pk(g(pre + '_w2')))
        put(f'{l}wA', _pk(np.concatenate([cs['u'], cs['v']], 1)))
        put(f'{l}wB', _pk(np.concatenate([cs['cb'], cs['cc'], cs['ch']], 1)))
        put(f'{l}wC1', _pk(np.concatenate([cs['qlat'], cs['kvlat']], 1)))
        put(f'{l}wC2', _pk(np.concatenate([cs['krope'], cs['krope'][:, sw32]], 1)))
        put(f'{l}wDq', _pk(np.concatenate([cs['nq'], cs['nq'][:, swq]], 1)))
        kcols = []; kb = []
        for nm in ('nks', 'nkw'):
            for sw in (False, True):
                idx = np.concatenate([gg * 64 + (sw64 if sw else np.arange(64)) for gg in range(2)])
                kcols.append(cs[nm][:, idx]); kb.append(bs_[nm][idx])
        put(f'{l}wDk', _pk(np.concatenate(kcols, 1)))
        put(f'{l}wDc', _pk(np.concatenate([cs['nkc'], cs['nvc']], 1)))
        put(f'{l}wDv', _pk(np.concatenate([cs['nvs'], cs['nvw'], cs['ngate']], 1)))
        put(f'{l}wG', _pk(np.concatenate([cs['ga'], cs['gb'], cs['gc'], cs['gd']], 1)))
        put(f'{l}WsT', np.ascontiguousarray(g('gmlp_ws').transpose(2, 0, 1)))
        put(f'{l}woA', _pk(g('gmlp_wout'))); put(f'{l}woB', _pk(g('conv_wout')))
        put(f'{l}woC', _pk(g('mla_wout'))); put(f'{l}woD', _pk(g('nsa_wout')))
        wuq = g('mla_wuq')
        swu = np.concatenate([np.concatenate([h * 96 + np.arange(64), h * 96 + 64 + sw32]) for h in range(8)])
        put(f'{l}wuq', _pk(wuq)); put(f'{l}wuqs', _pk(wuq[:, swu]))
        put(f'{l}wukv', g('mla_wukv'))
        wck = g('nsa_wcmp_k').transpose(1, 0, 2)
        wcv = g('nsa_wcmp_v').transpose(1, 0, 2)
        d2 = lambda a: np.concatenate([a, np.zeros_like(a)], 0)
        put(f'{l}wck', d2(wck))
        put(f'{l}wcks', d2(wck[:, :, sw64]))
        put(f'{l}wcv', d2(wcv))
        put(f'{l}wo', _pk(g('w_o')))
        put(f'{l}xq', _pk(g('xattn_wq'))); put(f'{l}xk', _pk(g('xattn_wk'))); put(f'{l}xv', _pk(g('xattn_wv')))
        put(f'{l}xo', _pk(g('xattn_wo')))
        putc(l, 'bAu', fm(bs_['u']))
        putc(l, 'bB', np.concatenate([fm(bs_['cb']), fm(bs_['cc']), fm(bs_['ch'])], 1))
        putc(l, 'cw', np.concatenate([fm(g('conv_w')[k]) for k in range(3)], 1))
        b2 = np.zeros((128, 2), np.float32); b2[64:96, 0] = bs_['krope']; b2[64:96, 1] = bs_['krope'][sw32]
        putc(l, 'bC2', b2)
        fm64 = lambda v: np.concatenate([np.ascontiguousarray(v.reshape(-1, 64).T), np.zeros((64, v.size // 64), np.float32)], 0)
        putc(l, 'bDq', np.concatenate([fm64(bs_['nq']), fm64(bs_['nq'][swq])], 1))
        putc(l, 'bDk', fm64(np.concatenate(kb)))
        putc(l, 'bDc', np.concatenate([fm64(bs_['nkc']), fm64(bs_['nvc'])], 1))
        putc(l, 'bG', np.concatenate([fm(bs_[n]) for n in ('ga', 'gb', 'gc', 'gd')], 1))
        putc(l, 'pek', d2(g('nsa_pe_k').T)); putc(l, 'pev', d2(g('nsa_pe_v').T))
        putr(l, 'bAv', bs_['v']); putr(l, 'glng', g('gmlp_ln_g')); putr(l, 'glnb', g('gmlp_ln_b'))
        putr(l, 'bs', g('gmlp_bs').reshape(-1))
        putr(l, 'bC1', np.concatenate([bs_['qlat'], bs_['kvlat']]))
        putr(l, 'qg', g('mla_qnorm_g')); putr(l, 'kvg', g('mla_kvnorm_g'))
        putr(l, 'bDv', np.concatenate([bs_['nvs'], bs_['nvw'], bs_['ngate']]))
        for i in (1, 2, 3, 4):
            putr(l, f'ln{i}g', g(f'ln{i}_g')); putr(l, f'ln{i}b', g(f'ln{i}_b'))
    for n, a in _consts_bf().items():
        put(n, a)
    return wb, col, row, _consts_f32()


class T:
    __slots__ = ('h', 'w', 'r')

    def __init__(s, h):
        s.h = h; s.w = None; s.r = {}

    def __getitem__(s, i):
        return s.h[i]


def cap(base, *dims):
    return bass.AP(tensor=base.tensor, offset=base.offset, ap=[list(base.ap[0])] + [list(d) for d in dims])


class K:
    def __init__(s, nc):
        s.nc = nc; s.es = ExitStack()
        s.eng = {'pe': nc.tensor, 'act': nc.scalar, 'dve': nc.vector, 'pool': nc.gpsimd, 'sp': nc.sync}
        s.esem = {n: s.es.enter_context(nc.semaphore('s_' + n)) for n in s.eng}
        s.cnt = {n: 0 for n in s.eng}; s.seen = {n: {} for n in s.eng}
        s.ND = 24
        s.dsem = [s.es.enter_context(nc.semaphore(f'd{i}')) for i in range(s.ND)]
        s.dval = [0] * s.ND; s.dnext = 0
        s.ps = [T(s.es.enter_context(nc.psum_tensor(f'ps{i}', [128, 512], F32))) for i in range(8)]
        s.uid = 0
        s.npe = 0; s.marks = []

    def alloc(s, es, shape, dt, name='t'):
        s.uid += 1
        return T(es.enter_context(s.nc.sbuf_tensor(f'{name}_{s.uid}', list(shape), dt)))

    def _need(s, e, ev, raw):
        kind, key, val = ev
        if kind == 'e' and key == e and (e == 'pe' or not raw):
            return
        kk = (kind, key)
        if s.seen[e].get(kk, 0) >= val:
            return
        s.seen[e][kk] = val
        s.eng[e].wait_ge(s.esem[key] if kind == 'e' else s.dsem[key], val)

    def deps(s, e, R, W):
        for t in R:
            if t.w is not None: s._need(e, t.w, True)
        for t in W:
            if t.w is not None: s._need(e, t.w, True)
            for ev in t.r.values(): s._need(e, ev, False)

    def op(s, e, fn, R=(), W=(), inc=True):
        s.deps(e, R, W)
        if e == 'pe': s.npe += 1
        ins = fn(s.eng[e])
        c = s.cnt[e] + 1
        if inc:
            ins.then_inc(s.esem[e], 1); s.cnt[e] = c
        ev = ('e', e, c)
        for t in R: t.r[e] = ev
        for t in W:
            t.w = ev; t.r = {}
        return ins

    def dma(s, out, in_, R=(), W=(), q='sp'):
        s.deps(q, R, W)
        i = s.dnext; s.dnext = (i + 1) % s.ND
        if s.dval[i] > 0: s._need(q, ('d', i, s.dval[i]), True)
        s.dval[i] += 16
        ev = ('d', i, s.dval[i])
        s.eng[q].dma_start(out=out, in_=in_).then_inc(s.dsem[i], 16)
        for t in R: t.r[('d', i)] = ev
        for t in W:
            t.w = ev; t.r = {}

    def barrier(s):
        for e in s.eng:
            for x in s.eng:
                if x != e and s.cnt[x] > 0: s._need(e, ('e', x, s.cnt[x]), True)
            for i in range(s.ND):
                if s.dval[i] > 0: s._need(e, ('d', i, s.dval[i]), True)

    def mm(s, pt, out, lhsT, rhs, R, start=True, stop=True, skip=False, inc=True):
        return s.op('pe', lambda e: e.matmul(out, lhsT=lhsT, rhs=rhs, start=start, stop=stop, skip_group_check=skip),
                    R=R, W=[pt], inc=inc)

    def tr(s, pt, out, in_, ident, R, inc=True):
        return s.op('pe', lambda e: e.transpose(out, in_, ident), R=R, W=[pt], inc=inc)

    def act(s, out, in_, func, R, W, bias=None, scale=None):
        kw = {}
        if bias is not None: kw['bias'] = bias
        if scale is not None: kw['scale'] = scale
        return s.op('act', lambda e: e.activation(out=out, in_=in_, func=func, **kw), R=R, W=W)

    def tt(s, e, out, in0, in1, op, R, W):
        return s.op(e, lambda g: g.tensor_tensor(out=out, in0=in0, in1=in1, op=op), R=R, W=W)

    def ts(s, e, out, in0, s1, s2, op0, op1, R, W):
        if op1 is None:
            return s.op(e, lambda g: g.tensor_scalar(out=out, in0=in0, scalar1=s1, scalar2=None, op0=op0), R=R, W=W)
        return s.op(e, lambda g: g.tensor_scalar(out=out, in0=in0, scalar1=s1, scalar2=s2, op0=op0, op1=op1), R=R, W=W)

    def stt(s, out, in0, sc, in1, op0, op1, R, W):
        return s.op('dve', lambda g: g.scalar_tensor_tensor(out=out, in0=in0, scalar=sc, in1=in1, op0=op0, op1=op1),
                    R=R, W=W)

    def cp(s, e, out, in_, R, W):
        if e == 'act':
            return s.op('act', lambda g: g.copy(out=out, in_=in_), R=R, W=W)
        return s.op(e, lambda g: g.tensor_copy(out=out, in_=in_), R=R, W=W)

    def memset(s, e, ap, v, W):
        return s.op(e, lambda g: g.memset(ap, v), W=W)


class U:
    __slots__ = ('S', 'X', 'P', 'sb', 'eb')

    def __init__(s, S, X, P):
        s.S = S; s.X = X; s.P = P; s.sb = None; s.eb = None


def run_items(k, items, sbanks, ebufs, ctr):
    def emitS(u):
        u.sb = k.ps[sbanks[ctr[0] % len(sbanks)]]; u.eb = ebufs[ctr[0] % len(ebufs)]; ctr[0] += 1
        u.S(u.sb)
    n = len(items)
    for i, it in enumerate(items):
        if isinstance(it, U):
            if it.sb is None: emitS(it)
            j = i + 1
            while j < n and (not isinstance(items[j], U)) and items[j][0] == 'soft': j += 1
            if j < n and isinstance(items[j], U) and items[j].sb is None: emitS(items[j])
            it.X(it.sb, it.eb); it.P(it.eb)
        else:
            it[1]()


class Prog:
    def __init__(s, nc, dbg=False, stop_after=None):
        s.nc = nc; s.k = K(nc); s.dbg = dbg; s.stop_after = stop_after; s.dq = []
        kind_s = "ExternalOutput" if dbg else "Internal"
        dt = nc.dram_tensor
        s.x_in = dt("x", [S, D], F32, kind="ExternalInput").ap()
        s.mem_in = dt("mem", [MEM, D], F32, kind="ExternalInput").ap()
        s.wbig = dt("wbig", [128, NWB], F32, kind="ExternalInput").ap()
        s.col = dt("col", [L, 128, NCOL], F32, kind="ExternalInput").ap()
        s.row = dt("row", [L, 1, NROW], F32, kind="ExternalInput").ap()
        s.cf = dt("cf", [128, NCF], F32, kind="ExternalInput").ap()
        s.out = dt("out", [S, D], F32, kind="ExternalOutput").ap()
        s.WB = dt("WB", [128, NWB], BF16, kind="Internal").ap()
        s.XR = dt("XR", [S, D], F32, kind=kind_s).ap()
        s.XT = dt("XT", [128, 8, S], BF16, kind=kind_s).ap()
        s.PA = dt("PA", [128, 4, S], BF16, kind=kind_s).ap()
        s.PB = dt("PB", [128, 4, S], BF16, kind=kind_s).ap()
        s.PC = dt("PC", [128, 4, S], BF16, kind=kind_s).ap()
        s.PD = dt("PD", [128, 4, S], BF16, kind=kind_s).ap()
        s.QM = dt("QM", [96, 8, S], BF16, kind="Internal").ap()
        s.KM = dt("KM", [96, 8, S], BF16, kind="Internal").ap()
        s.VM = dt("VM", [128, 8, NT, 65], BF16, kind="Internal").ap()
        s.QN = dt("QN", [64, 8, S], BF16, kind="Internal").ap()
        s.KK = dt("KK", [64, 4, S], BF16, kind="Internal").ap()
        s.KCV = dt("KCV", [64, 4, S], BF16, kind="Internal").ap()
        s.VSW = dt("VSW", [128, 4, NT, 65], BF16, kind="Internal").ap()
        s.GATE = dt("GATE", [128, NT, 24], F32, kind="Internal").ap()
        s.YP = dt("YP", [S, D], F32, kind="Internal").ap()
        s.MT = dt("MT", [128, 8, S], BF16, kind="Internal").ap()

    def wsl(s, name):
        o, sz, shp = WOFF[name]
        a = s.WB[:, o:o + sz]
        if len(shp) == 2:
            a = a.rearrange("p (a b) -> p a b", a=shp[0])
        return a

    def loadw(s, es, name):
        o, sz, shp = WOFF[name]
        t = s.k.alloc(es, (128,) + shp, BF16, 'w')
        s.k.dma(t[:], s.wsl(name), W=[t])
        return t

    def loadrow(s, es, l, name, n=None):
        o, sz, _ = ROFF[name]
        n = n or sz
        t = s.k.alloc(es, (128, n), F32, 'r')
        src = s.row[l, 0:1, o:o + n]
        s.k.dma(t[:], bass.AP(tensor=src.tensor, offset=src.offset, ap=[[0, 128], [1, n]]), W=[t])
        return t

    def loadcf(s, es, name, c0=0, n=None, parts=128):
        o, sz, _ = FOFF[name]
        n = n or sz
        t = s.k.alloc(es, (128, n), F32, 'c')
        s.k.dma(t[0:parts, :], s.cf[0:parts, o + c0:o + c0 + n], W=[t])
        return t

    def phase_end(s, name):
        s.k.barrier()
        s.k.marks.append((name, s.k.npe))
        return s.stop_after == name

    def prep(s):
        k = s.k
        CH = 4096
        end0 = WOFF['0f1w2'][0] + WOFF['0f1w2'][1]
        with ExitStack() as es:
            fb = [k.alloc(es, (128, CH), F32, 'pf') for _ in range(4)]
            bb = [k.alloc(es, (128, CH), BF16, 'pb') for _ in range(4)]
            engs = ['dve', 'act', 'dve', 'pool']
            n = 0
            for c0 in range(0, end0, CH):
                w = min(CH, end0 - c0)
                f = fb[n % 4]; b = bb[n % 4]
                k.dma(f[:, 0:w], s.wbig[:, c0:c0 + w], W=[f])
                k.cp(engs[n % 4], b[:, 0:w], f[:, 0:w], R=[f], W=[b])
                k.dma(s.WB[:, c0:c0 + w], b[:, 0:w], R=[b])
                n += 1
        s.k.barrier()
        s.BCH = 1024
        s.bgf = [k.alloc(k.es, (128, s.BCH), F32, 'bgf') for _ in range(4)]
        s.bgb = [k.alloc(k.es, (128, s.BCH), BF16, 'bgb') for _ in range(4)]
        s.bg_pos = end0; s.bg_n = 0

    def bg(s, n=1, act_ok=False):
        k = s.k
        for _ in range(n):
            if s.bg_pos >= NWB: return
            c0 = s.bg_pos; w = min(s.BCH, NWB - c0)
            f = s.bgf[s.bg_n % 4]; b = s.bgb[s.bg_n % 4]
            e = 'pool'
            s.bg_n += 1
            k.dma(f[:, 0:w], s.wbig[:, c0:c0 + w], W=[f])
            k.cp(e, b[:, 0:w], f[:, 0:w], R=[f], W=[b])
            k.dma(s.WB[:, c0:c0 + w], b[:, 0:w], R=[b], q=e)
            s.bg_pos += w

    def bg_until(s, col):
        while s.bg_pos < min(col, NWB):
            s.bg(1)

    def to_fm(s, es_bufs, src_t, src_ap_fn, n, dst_ap, psb, ident):
        k = s.k
        stg = es_bufs
        pv = k.ps[psb][:].bitcast(BF16)
        for j in range(n):
            k.tr(k.ps[psb], pv[:, j * 128:(j + 1) * 128], src_ap_fn(j), ident[:], R=[src_t, ident], inc=(j == n - 1))
        k.cp('act', stg[:, 0:n, :], pv[:, 0:n * 128].rearrange("p (a b) -> p a b", a=n), R=[], W=[stg, k.ps[psb]])
        k.dma(dst_ap, stg[:, 0:n, :], R=[stg])

    def ln_setup(s, es, l, i):
        k = s.k
        o = type('o', (), {})()
        o.g = s.loadrow(es, l, f'ln{i}g'); o.b = s.loadrow(es, l, f'ln{i}b')
        o.st = [k.alloc(es, (128, 2, 6), F32, 'st') for _ in range(4)]
        o.mv = [k.alloc(es, (128, 2), F32, 'mv') for _ in range(4)]
        o.rs = [k.alloc(es, (128, 2), F32, 'rs') for _ in range(4)]
        o.xb = [k.alloc(es, (128, D), BF16, 'xb') for _ in range(2)]
        o.stg = [k.alloc(es, (128, 8, 128), BF16, 'sg') for _ in range(2)]
        o.mh = k.alloc(es, (128, 1), F32, 'mh')
        k.memset('dve', o.mh[:], -0.5, W=[o.mh])
        o.n = 0
        return o

    def ln_tile(s, o, pre, tt, dst, psb, ident, write_xt=True):
        k = s.k
        i = o.n % 2; i4 = o.n % 4; o.n += 1
        st, mv, rs, xb, stg = o.st[i4], o.mv[i4], o.rs[i4], o.xb[i], o.stg[i]
        for hh in range(2):
            k.op('dve', lambda g, hh=hh: g.bn_stats(out=st[:, hh, :], in_=pre[:, hh * 512:(hh + 1) * 512]), R=[pre], W=[st])
        k.op('dve', lambda g: g.bn_aggr(out=mv[:], in_=st[:].rearrange("p a b -> p (a b)")), R=[st], W=[mv])
        k.ts('dve', rs[:, 0:1], mv[:, 1:2], LN_EPS, None, ALU.add, None, R=[mv], W=[rs])
        k.tt('pool', rs[:, 1:2], rs[:, 0:1], o.mh[:], ALU.pow, R=[rs, o.mh], W=[rs])
        k.ts('dve', pre[:], pre[:], mv[:, 0:1], rs[:, 1:2], ALU.subtract, ALU.mult, R=[pre, mv, rs], W=[pre])
        k.tt('pool', pre[:], pre[:], o.g[:], ALU.mult, R=[pre, o.g], W=[pre])
        k.tt('pool', pre[:], pre[:], o.b[:], ALU.add, R=[pre, o.b], W=[pre])
        k.dma(dst[tt * 128:(tt + 1) * 128, :], pre[:], R=[pre])
        if write_xt:
            s.flush(1)
            k.cp('act', xb[:], pre[:], R=[pre], W=[xb])
            s.dq.append(lambda: s.to_fm(stg, xb, lambda j: xb[:, j * 128:(j + 1) * 128], 8,
                                        s.XT[:, :, tt * 128:(tt + 1) * 128], psb, ident))

    def flush(s, keep=0):
        while len(s.dq) > keep:
            s.dq.pop(0)()

    def xt0(s):
        k = s.k
        with ExitStack() as es:
            ident = s.loadw(es, 'ident')
            xf = [k.alloc(es, (128, D), F32, 'xf') for _ in range(2)]
            xb = [k.alloc(es, (128, D), BF16, 'xb') for _ in range(2)]
            stg = [k.alloc(es, (128, 8, 128), BF16, 'sg') for _ in range(2)]
            for tt in range(NT):
                f = xf[tt % 2]; b = xb[tt % 2]
                k.dma(f[:], s.x_in[tt * 128:(tt + 1) * 128, :], W=[f])
                k.cp('dve', b[:], f[:], R=[f], W=[b])
                s.to_fm(stg[tt % 2], b, lambda j, b=b: b[:, j * 128:(j + 1) * 128], 8,
                        s.XT[:, :, tt * 128:(tt + 1) * 128], tt % 2, ident)
        s.k.barrier()

    def ffn(s, l, f, lni, src, dst, write_xt=True, bgn=0):
        k = s.k
        GT = 256; NG = S // GT; HF = FF // 2; NFC = 11
        for half in range(2):
            with ExitStack() as es:
                w1 = k.alloc(es, (128, 8, HF), BF16, 'w1'); w3 = k.alloc(es, (128, 8, HF), BF16, 'w3')
                w2 = k.alloc(es, (128, NFC, D), BF16, 'w2')
                k.dma(w1[:], s.wsl(f'{l}{f}w1')[:, :, half * HF:(half + 1) * HF], W=[w1])
                k.dma(w3[:], s.wsl(f'{l}{f}w3')[:, :, half * HF:(half + 1) * HF], W=[w3])
                k.dma(w2[:], s.wsl(f'{l}{f}w2')[:, half * NFC:(half + 1) * NFC, :], W=[w2])
                ident = s.loadw(es, 'ident')
                ln = s.ln_setup(es, l, lni) if half == 1 else None
                xT = [k.alloc(es, (128, 8, GT), BF16, 'xT') for _ in range(3)]
                gT = [k.alloc(es, (128, NFC, GT), BF16, 'gT') for _ in range(2)]
                sl = [k.alloc(es, (128, GT), F32, 'sl') for _ in range(2)]
                xr = [k.alloc(es, (128, D), F32, 'xr') for _ in range(4)]
                pre = [k.alloc(es, (128, D), F32, 'pre') for _ in range(4)]

                def LDT(gi):
                    k.dma(xT[gi % 3][:], s.XT[:, :, gi * GT:(gi + 1) * GT], W=[xT[gi % 3]])

                def LDX(gi):
                    for t2 in range(2):
                        tt = gi * 2 + t2
                        sr = src if half == 0 else s.YP
                        k.dma(xr[tt % 4][:], sr[tt * 128:(tt + 1) * 128, :], W=[xr[tt % 4]])

                def H(gi):
                    x = xT[gi % 3]; g = gT[gi % 2]
                    for fc in range(NFC):
                        p1 = k.ps[(2 * fc) % 3]; p3 = k.ps[(2 * fc + 1) % 3]
                        for kk in range(8):
                            k.mm(p1, p1[:, 0:GT], w1[:, kk, fc * 128:(fc + 1) * 128], x[:, kk, :], R=[w1, x],
                                 start=(kk == 0), stop=(kk == 7), inc=(kk == 7))
                        for kk in range(8):
                            k.mm(p3, p3[:, 0:GT], w3[:, kk, fc * 128:(fc + 1) * 128], x[:, kk, :], R=[w3, x],
                                 start=(kk == 0), stop=(kk == 7), inc=(kk == 7))
                        sb = sl[fc % 2]
                        k.act(sb[:], p1[:, 0:GT], AF.Silu, R=[], W=[sb, p1])
                        k.tt('dve', g[:, fc, :], sb[:], p3[:, 0:GT], ALU.mult, R=[sb], W=[g, p3])

                def Y(gi):
                    g = gT[gi % 2]
                    for t2 in range(2):
                        tt = gi * 2 + t2
                        xx = xr[tt % 4]; pp = pre[tt % 4]
                        for dh in range(2):
                            pb = k.ps[4 + t2 * 2 + dh]
                            for fc in range(NFC):
                                k.mm(pb, pb[:, :], g[:, fc, t2 * 128:(t2 + 1) * 128], w2[:, fc, dh * 512:(dh + 1) * 512],
                                     R=[g, w2], start=(fc == 0), stop=(fc == NFC - 1), inc=(fc == NFC - 1))
                            if half == 0:
                                k.act(pp[:, dh * 512:(dh + 1) * 512], pb[:, :], AF.Copy, R=[], W=[pp, pb], scale=0.5)
                            else:
                                k.stt(pp[:, dh * 512:(dh + 1) * 512], pb[:, :], 0.5, xx[:, dh * 512:(dh + 1) * 512], ALU.mult, ALU.add,
                                      R=[xx], W=[pp, pb])
                        if half == 0:
                            k.stt(pp[:], xx[:], ALPHA, pp[:], ALU.mult, ALU.add, R=[xx, pp], W=[pp])
                            k.dma(s.YP[tt * 128:(tt + 1) * 128, :], pp[:], R=[pp])
                        else:
                            s.ln_tile(ln, pp, tt, dst, 3, ident, write_xt)

                LDT(0); LDT(1); LDX(0)
                H(0)
                for gi in range(NG):
                    if gi + 2 < NG: LDT(gi + 2)
                    if gi + 1 < NG: LDX(gi + 1)
                    s.bg((bgn * 3 + 1) // 2 if half == 0 else bgn // 2)
                    if gi + 1 < NG: H(gi + 1)
                    Y(gi)
                s.flush(0)
                if half == 1 and getattr(s, '_bg_until_col', None):
                    s.bg_until(s._bg_until_col)
            s.k.barrier()
        return s.phase_end(f'{l}{f}')

    def mixA(s, l):
        k = s.k
        with ExitStack() as es:
            wA = s.loadw(es, f'{l}wA'); ws = s.loadw(es, f'{l}WsT'); tril = s.loadw(es, 'tril')
            bv = s.loadrow(es, l, 'bAv'); lg = s.loadrow(es, l, 'glng'); lb = s.loadrow(es, l, 'glnb')
            bsb = s.loadrow(es, l, 'bs')
            colt = k.alloc(es, (128, NCOL), F32, 'col'); k.dma(colt[:], s.col[l], W=[colt])
            cA = COFF['bAu'][0]
            wsm = k.alloc(es, (128, 4, 128), BF16, 'wsm')
            k.tt('dve', wsm[:], ws[:], cap(tril[:, 0:1], [0, 4], [1, 128]), ALU.mult, R=[ws, tril], W=[wsm])
            mh = k.alloc(es, (128, 1), F32, 'mh'); k.memset('dve', mh[:], -0.5, W=[mh])
            hT = [k.alloc(es, (128, 8, 512), BF16, 'hT') for _ in range(2)]
            uT = [k.alloc(es, (128, 4, 512), F32, 'uT') for _ in range(2)]
            pa = [k.alloc(es, (128, 4, 512), BF16, 'pa') for _ in range(2)]
            vb = [k.alloc(es, (128, 512), F32, 'vb') for _ in range(2)]
            vn = [k.alloc(es, (128, 512), F32, 'vn') for _ in range(2)]
            vl = [k.alloc(es, (128, 512), BF16, 'vl') for _ in range(2)]
            t1 = [k.alloc(es, (128, 512), F32, 't1') for _ in range(2)]
            st = [k.alloc(es, (128, 6), F32, 'st') for _ in range(2)]
            mv = [k.alloc(es, (128, 2), F32, 'mv') for _ in range(2)]
            rs = [k.alloc(es, (128, 2), F32, 'rs') for _ in range(2)]

            def load(G):
                h = hT[G % 2]
                k.dma(h[:], s.XT[:, :, G * 512:(G + 1) * 512], W=[h])

            def Uproj(G):
                h = hT[G % 2]; u = uT[G % 2]
                for fc in range(4):
                    pb = k.ps[fc % 2]
                    for kk in range(8):
                        k.mm(pb, pb[:, :], wA[:, kk, fc * 128:(fc + 1) * 128], h[:, kk, :], R=[wA, h],
                             start=(kk == 0), stop=(kk == 7), inc=(kk == 7))
                    k.act(u[:, fc, :], pb[:, :], AF.Identity, R=[colt], W=[u, pb], bias=colt[:, cA + fc:cA + fc + 1])

            def V(cc):
                G, c = divmod(cc, 4); i = cc % 2
                h = hT[G % 2]
                pv = k.ps[2 + i]
                for kk in range(8):
                    k.mm(pv, pv[:, :], h[:, kk, c * 128:(c + 1) * 128], wA[:, kk, 512:1024], R=[wA, h],
                         start=(kk == 0), stop=(kk == 7), inc=(kk == 7))
                k.tt('dve', vb[i][:], pv[:, :], bv[:], ALU.add, R=[bv], W=[vb[i], pv])
                k.op('dve', lambda g: g.bn_stats(out=st[i][:], in_=vb[i][:]), R=[vb[i]], W=[st[i]])
                k.op('dve', lambda g: g.bn_aggr(out=mv[i][:], in_=st[i][:]), R=[st[i]], W=[mv[i]])
                k.ts('dve', rs[i][:, 0:1], mv[i][:, 1:2], LN_EPS, None, ALU.add, None, R=[mv[i]], W=[rs[i]])
                k.tt('pool', rs[i][:, 1:2], rs[i][:, 0:1], mh[:], ALU.pow, R=[rs[i], mh], W=[rs[i]])
                k.ts('dve', vn[i][:], vb[i][:], mv[i][:, 0:1], rs[i][:, 1:2], ALU.subtract, ALU.mult,
                     R=[vb[i], mv[i], rs[i]], W=[vn[i]])
                k.tt('pool', vn[i][:], vn[i][:], lg[:], ALU.mult, R=[vn[i], lg], W=[vn[i]])
                k.tt('pool', vl[i][:], vn[i][:], lb[:], ALU.add, R=[vn[i], lb], W=[vl[i]])

            def S2(cc):
                G, c = divmod(cc, 4); i = cc % 2
                u = uT[G % 2]; p = pa[G % 2]
                p2 = k.ps[4 + i]
                for g in range(4):
                    k.mm(p2, p2[:, g * 128:(g + 1) * 128], vl[i][:, g * 128:(g + 1) * 128], wsm[:, g, :],
                         R=[vl[i], wsm], start=True, stop=True, inc=(g == 3))
                k.tt('dve', t1[i][:], p2[:, :], bsb[:], ALU.add, R=[bsb], W=[t1[i], p2])
                k.tt('pool', p[:, :, c * 128:(c + 1) * 128], t1[i][:].rearrange("p (a b) -> p a b", a=4),
                     u[:, :, c * 128:(c + 1) * 128], ALU.mult, R=[t1[i], u], W=[p])
                if c == 3:
                    k.dma(s.PA[:, :, G * 512:(G + 1) * 512], p[:], R=[p])

            load(0); load(1); Uproj(0); V(0)
            for cc in range(32):
                G, c = divmod(cc, 4)
                if cc + 1 < 32:
                    if (cc + 1) % 4 == 0: Uproj((cc + 1) // 4)
                    V(cc + 1)
                S2(cc)
                if c == 3 and G + 2 < 8: load(G + 2)
        return s.phase_end(f'{l}A')

    def mixB(s, l):
        k = s.k
        with ExitStack() as es:
            wB = s.loadw(es, f'{l}wB')
            colt = k.alloc(es, (128, NCOL), F32, 'col'); k.dma(colt[:], s.col[l], W=[colt])
            cb0 = COFF['bB'][0]; cw0 = COFF['cw'][0]
            P = k.alloc(es, (128, 4, 514), F32, 'P')
            k.memset('dve', P[:, :, 0:2], 0.0, W=[P])
            hT = [k.alloc(es, (128, 8, 512), BF16, 'hT') for _ in range(2)]
            pbuf = [k.alloc(es, (128, 4, 512), BF16, 'pb') for _ in range(2)]
            ccs = [k.alloc(es, (128, 512), F32, 'cc') for _ in range(2)]
            y1 = [k.alloc(es, (128, 512), F32, 'y1') for _ in range(2)]
            y2 = [k.alloc(es, (128, 512), F32, 'y2') for _ in range(2)]

            def load(G):
                k.dma(hT[G % 2][:], s.XT[:, :, G * 512:(G + 1) * 512], W=[hT[G % 2]])
            load(0)
            n = 0
            for G in range(8):
                if G + 1 < 8: load(G + 1)
                h = hT[G % 2]; pb = pbuf[G % 2]
                for fc in range(4):
                    i = n % 2; n += 1
                    pa_, pb_, pc_ = k.ps[i], k.ps[2 + i], k.ps[4 + i]
                    for which, pp in ((1, pa_), (2, pb_), (0, pc_)):
                        for kk in range(8):
                            c0 = which * 512 + fc * 128
                            k.mm(pp, pp[:, :], wB[:, kk, c0:c0 + 128], h[:, kk, :], R=[wB, h],
                                 start=(kk == 0), stop=(kk == 7), inc=(kk == 7))
                    bcol = lambda which: colt[:, cb0 + which * 4 + fc:cb0 + which * 4 + fc + 1]
                    wcol = lambda kk: colt[:, cw0 + kk * 4 + fc:cw0 + kk * 4 + fc + 1]
                    k.act(ccs[i][:], pa_[:, :], AF.Identity, R=[colt], W=[ccs[i], pa_], bias=bcol(1))
                    k.stt(P[:, fc, 2:514], pb_[:, :], bcol(2), ccs[i][:], ALU.add, ALU.mult, R=[ccs[i], colt], W=[P, pb_])
                    k.ts('dve', y1[i][:], P[:, fc, 2:514], wcol(2), None, ALU.mult, None, R=[P, colt], W=[y1[i]])
                    k.stt(y2[i][:], P[:, fc, 1:513], wcol(1), y1[i][:], ALU.mult, ALU.add, R=[P, y1[i], colt], W=[y2[i]])
                    k.stt(y1[i][:], P[:, fc, 0:512], wcol(0), y2[i][:], ALU.mult, ALU.add, R=[P, y2[i], colt], W=[y1[i]])
                    k.stt(pb[:, fc, :], pc_[:, :], bcol(0), y1[i][:], ALU.add, ALU.mult, R=[y1[i], colt], W=[pb, pc_])
                    k.cp('pool', P[:, fc, 0:2], P[:, fc, 512:514], R=[], W=[P])
                k.dma(s.PB[:, :, G * 512:(G + 1) * 512], pb[:], R=[pb])
        return s.phase_end(f'{l}B')

    def mixC1(s, l):
        k = s.k
        with ExitStack() as es:
            wC1 = s.loadw(es, f'{l}wC1'); wC2 = s.loadw(es, f'{l}wC2')
            wuq = s.loadw(es, f'{l}wuq'); wuqs = s.loadw(es, f'{l}wuqs'); wukv = s.loadw(es, f'{l}wukv')
            ident = s.loadw(es, 'ident')
            bC1 = s.loadrow(es, l, 'bC1'); qg = s.loadrow(es, l, 'qg'); kvg = s.loadrow(es, l, 'kvg')
            colt = k.alloc(es, (128, NCOL), F32, 'col'); k.dma(colt[:], s.col[l], W=[colt])
            c2 = COFF['bC2'][0]
            mh = k.alloc(es, (128, 1), F32, 'mh'); k.memset('dve', mh[:], -0.5, W=[mh])
            hT = [k.alloc(es, (128, 8, 512), BF16, 'hT') for _ in range(2)]
            Ct = [k.alloc(es, (128, 512), F32, 'Ct') for _ in range(2)]
            St = [k.alloc(es, (128, 512), F32, 'St') for _ in range(2)]
            zb = [k.alloc(es, (128, 384), F32, 'zb') for _ in range(2)]
            st = [k.alloc(es, (128, 2, 6), F32, 'st') for _ in range(2)]
            mv = [k.alloc(es, (128, 4), F32, 'mv') for _ in range(2)]
            rs = [k.alloc(es, (128, 4), F32, 'rs') for _ in range(2)]
            lat = [k.alloc(es, (128, 384), BF16, 'lat') for _ in range(2)]
            latT = [k.alloc(es, (128, 3, 512), BF16, 'latT') for _ in range(2)]
            qT = [k.alloc(es, (128, 8, 512), BF16, 'qT') for _ in range(2)]
            kT = [k.alloc(es, (128, 8, 512), BF16, 'kT') for _ in range(2)]
            va = [k.alloc(es, (128, 8, 4, 65), BF16, 'va') for _ in range(2)]
            ta = [k.alloc(es, (128, 512), F32, 'ta') for _ in range(2)]
            tb = [k.alloc(es, (128, 512), F32, 'tb') for _ in range(2)]
            kpe = k.alloc(es, (128, 512), BF16, 'kpe')
            for v in va:
                k.memset('pool', v[:, :, :, 64:65], 1.0, W=[v])
            o96 = FOFF['C96'][0]; s96 = FOFF['S96'][0]

            def load(G):
                i = G % 2
                k.dma(hT[i][:], s.XT[:, :, G * 512:(G + 1) * 512], W=[hT[i]])
                k.dma(Ct[i][0:96, :], s.cf[0:96, o96 + G * 512:o96 + (G + 1) * 512], W=[Ct[i]])
                k.dma(St[i][0:96, :], s.cf[0:96, s96 + G * 512:s96 + (G + 1) * 512], W=[St[i]])
            def Z(cc):
                G, c = divmod(cc, 4); i = cc % 2
                h = hT[G % 2]
                pz = k.ps[i]
                for kk in range(8):
                    k.mm(pz, pz[:, 0:384], h[:, kk, c * 128:(c + 1) * 128], wC1[:, kk, :], R=[h, wC1],
                         start=(kk == 0), stop=(kk == 7), inc=(kk == 7))
                k.tt('dve', zb[i][:], pz[:, 0:384], bC1[:], ALU.add, R=[bC1], W=[zb[i], pz])
                k.op('dve', lambda g, i=i: g.bn_stats(out=st[i][:, 0, :], in_=zb[i][:, 0:256]), R=[zb[i]], W=[st[i]])
                k.op('dve', lambda g, i=i: g.bn_stats(out=st[i][:, 1, :], in_=zb[i][:, 256:384]), R=[zb[i]], W=[st[i]])
                k.op('dve', lambda g, i=i: g.bn_aggr(out=mv[i][:, 0:2], in_=st[i][:, 0, :]), R=[st[i]], W=[mv[i]])
                k.op('dve', lambda g, i=i: g.bn_aggr(out=mv[i][:, 2:4], in_=st[i][:, 1, :]), R=[st[i]], W=[mv[i]])
                for j in range(2):
                    k.stt(rs[i][:, j:j + 1], mv[i][:, 2 * j:2 * j + 1], mv[i][:, 2 * j:2 * j + 1], mv[i][:, 2 * j + 1:2 * j + 2],
                          ALU.mult, ALU.add, R=[mv[i]], W=[rs[i]])
                k.ts('dve', rs[i][:, 2:4], rs[i][:, 0:2], RMS_EPS, None, ALU.add, None, R=[rs[i]], W=[rs[i]])
                k.tt('pool', rs[i][:, 0:2], rs[i][:, 2:4], cap(mh[:, 0:1], [0, 2]), ALU.pow, R=[rs[i], mh], W=[rs[i]])
                k.stt(lat[i][:, 0:256], zb[i][:, 0:256], rs[i][:, 0:1], qg[:], ALU.mult, ALU.mult, R=[zb[i], rs[i], qg], W=[lat[i]])
                k.stt(lat[i][:, 256:384], zb[i][:, 256:384], rs[i][:, 1:2], kvg[:], ALU.mult, ALU.mult, R=[zb[i], rs[i], kvg], W=[lat[i]])

            def Tt(cc):
                G, c = divmod(cc, 4); i = cc % 2
                lt = latT[G % 2]; v = va[G % 2]
                pt = k.ps[2 + i]; ptv = pt[:].bitcast(BF16)
                for j in range(3):
                    k.tr(pt, ptv[:, j * 128:(j + 1) * 128], lat[i][:, j * 128:(j + 1) * 128], ident[:], R=[lat[i], ident], inc=(j == 2))
                k.cp('act', lt[:, :, c * 128:(c + 1) * 128], ptv[:, 0:384].rearrange("p (a b) -> p a b", a=3), R=[], W=[lt, pt])
                pv = k.ps[4 + i]
                vcols = cap(wukv[:, 64:65], [128, 8], [1, 64])
                k.mm(pv, pv[:, :].rearrange("p (a b) -> p a b", a=8), lt[:, 2, c * 128:(c + 1) * 128], vcols, R=[lt, wukv])
                k.cp('act', v[:, :, c, 0:64], pv[:, :].rearrange("p (a b) -> p a b", a=8), R=[], W=[v, pv])

            def UP(G):
                gi = G % 2
                h = hT[gi]; C = Ct[gi]; Sn = St[gi]; lt = latT[gi]; q = qT[gi]; kt_ = kT[gi]; v = va[gi]
                pe1 = k.ps[6]; pe2 = k.ps[7]
                for kk in range(8):
                    k.mm(pe1, pe1[64:96, :], wC2[:, kk, 0:32], h[:, kk, :], R=[wC2, h], start=(kk == 0), stop=(kk == 7), inc=(kk == 7))
                for kk in range(8):
                    k.mm(pe2, pe2[64:96, :], wC2[:, kk, 32:64], h[:, kk, :], R=[wC2, h], start=(kk == 0), stop=(kk == 7), inc=(kk == 7))
                k.stt(ta[0][64:96, :], pe1[64:96, :], colt[64:96, c2:c2 + 1], C[64:96, :], ALU.add, ALU.mult, R=[colt, C], W=[ta[0], pe1])
                k.stt(tb[0][64:96, :], pe2[64:96, :], colt[64:96, c2 + 1:c2 + 2], Sn[64:96, :], ALU.add, ALU.mult, R=[colt, Sn], W=[tb[0], pe2])
                k.tt('pool', kpe[64:96, :], ta[0][64:96, :], tb[0][64:96, :], ALU.add, R=[ta[0], tb[0]], W=[kpe])
                k.cp('pool', kt_[64:96, :, :], cap(kpe[64:96, 0:1], [0, 8], [1, 512]), R=[kpe], W=[kt_])
                for hh in range(8):
                    i = hh % 2
                    px = k.ps[i]; py = k.ps[2 + i]; pk_ = k.ps[4 + i]
                    for kc in range(2):
                        k.mm(px, px[0:96, :], wuq[:, kc, hh * 96:(hh + 1) * 96], lt[:, kc, :], R=[wuq, lt], start=(kc == 0), stop=(kc == 1), inc=(kc == 1))
                    for kc in range(2):
                        k.mm(py, py[0:96, :], wuqs[:, kc, hh * 96:(hh + 1) * 96], lt[:, kc, :], R=[wuqs, lt], start=(kc == 0), stop=(kc == 1), inc=(kc == 1))
                    k.tt('dve', ta[1][0:96, :], px[0:96, :], C[0:96, :], ALU.mult, R=[C], W=[ta[1], px])
                    k.tt('dve', tb[1][0:96, :], py[0:96, :], Sn[0:96, :], ALU.mult, R=[Sn], W=[tb[1], py])
                    k.tt('pool', q[0:96, hh, :], ta[1][0:96, :], tb[1][0:96, :], ALU.add, R=[ta[1], tb[1]], W=[q])
                    k.mm(pk_, pk_[0:64, :], wukv[:, hh * 128:hh * 128 + 64], lt[:, 2, :], R=[wukv, lt])
                    k.cp('act', kt_[0:64, hh, :], pk_[0:64, :], R=[], W=[kt_, pk_])
                k.dma(s.QM[:, :, G * 512:(G + 1) * 512], q[0:96, :, :], R=[q])
                k.dma(s.KM[:, :, G * 512:(G + 1) * 512], kt_[0:96, :, :], R=[kt_])
                k.dma(s.VM[:, :, G * 4:(G + 1) * 4, :], v[:], R=[v])

            load(0); load(1); Z(0)
            for cc in range(32):
                G, c = divmod(cc, 4)
                if cc + 1 < 32: Z(cc + 1)
                Tt(cc)
                if c == 3:
                    UP(G)
                    if G + 2 < 8: load(G + 2)
        return s.phase_end(f'{l}C1')

    def otm_to_fm(s, es, otm, dst, ident, psb0):
        k = s.k
        stg = [k.alloc(es, (128, 4, 512), BF16, 'osg') for _ in range(2)]
        for qg_ in range(8):
            sg = stg[qg_ % 2]
            for q4 in range(4):
                qt = qg_ * 4 + q4
                pt = k.ps[psb0 + qt % 2]; ptv = pt[:].bitcast(BF16)
                for j in range(4):
                    k.tr(pt, ptv[:, j * 128:(j + 1) * 128], otm[:, qt, j * 128:(j + 1) * 128], ident[:], R=[otm, ident], inc=(j == 3))
                k.cp('act' if qt % 2 else 'dve', sg[:, :, q4 * 128:(q4 + 1) * 128], ptv[:, 0:512].rearrange("p (a b) -> p a b", a=4), R=[], W=[sg, pt])
            k.dma(dst[:, :, qg_ * 512:(qg_ + 1) * 512], sg[:], R=[sg])

    def mixC2(s, l, bgn=0):
        k = s.k
        sc = float(96 ** -0.5)
        with ExitStack() as es:
            ident = s.loadw(es, 'ident'); negc = s.loadw(es, 'negc')
            otm = k.alloc(es, (128, NT, 512), BF16, 'otm')
            kT = [k.alloc(es, (128, S), BF16, 'kT') for _ in range(2)]
            qT = [k.alloc(es, (128, S), BF16, 'qT') for _ in range(2)]
            V = [k.alloc(es, (128, NT, 65), BF16, 'V') for _ in range(2)]
            E = [k.alloc(es, (128, 512), BF16, 'E') for _ in range(3)]
            rc = [k.alloc(es, (128, 4), F32, 'rc') for _ in range(2)]

            for t_ in kT + qT:
                k.memset('pool', t_[:, :], 0.0, W=[t_])

            def load(hh):
                i = hh % 2
                k.dma(kT[i][0:96, :], s.KM[:, hh, :], W=[kT[i]])
                k.dma(qT[i][0:96, :], s.QM[:, hh, :], W=[qT[i]])
                k.dma(V[i][:], s.VM[:, hh, :, :], W=[V[i]])
            load(0)
            no = 0; ctr = [0]
            for hh in range(8):
                if hh + 1 < 8: load(hh + 1)
                kt_, q, v = kT[hh % 2], qT[hh % 2], V[hh % 2]
                items = []
                for G in range(8):
                    O = k.ps[4 + no % 2]; r = rc[no % 2]; no += 1
                    items.append(('soft', lambda O=O: (s.bg(bgn), k.memset('dve', O[:, 0:260], 0.0, W=[O]))))
                    for kt in range(4 * G + 4):
                        q0 = max(kt * 128, G * 512) - G * 512
                        nc_ = 512 - q0
                        diag = kt >= 4 * G

                        def fS(Sb, kt=kt, q0=q0, nc_=nc_, diag=diag, G=G):
                            k.mm(Sb, Sb[:, 0:nc_], kt_[:, kt * 128:(kt + 1) * 128], q[:, G * 512 + q0:(G + 1) * 512],
                                 R=[kt_, q], start=True, stop=(not diag), inc=(not diag))
                            if diag:
                                k.mm(Sb, Sb[:, 0:128], ident[:], negc[:], R=[ident, negc], start=False, stop=True)

                        def fX(Sb, Eb, nc_=nc_):
                            k.act(Eb[:, 0:nc_], Sb[:, 0:nc_], AF.Exp, R=[], W=[Eb, Sb], scale=sc)

                        def fP(Eb, kt=kt, q0=q0, O=O):
                            for qs in range(q0 // 128, 4):
                                k.mm(O, O[:, qs * 65:(qs + 1) * 65], Eb[:, (qs * 128 - q0):(qs * 128 - q0) + 128], v[:, kt, :],
                                     R=[Eb, v], start=False, stop=True, skip=True, inc=(qs == 3))
                        items.append(U(fS, fX, fP))

                    def fin(O=O, r=r, G=G, hh=hh):
                        Ov = O[:, 0:260].rearrange("p (a b) -> p a b", a=4)
                        k.op('dve', lambda g: g.reciprocal(out=r[:], in_=Ov[:, :, 64]), R=[], W=[r, O])
                        k.tt('dve', otm[:, G * 4:(G + 1) * 4, hh * 64:(hh + 1) * 64], Ov[:, :, 0:64],
                             cap(r[:, 0:1], [1, 4], [0, 64]), ALU.mult, R=[r], W=[otm, O])
                    items.append(('soft', fin))
                run_items(k, items, [0, 1, 2], E, ctr)
            s.otm_to_fm(es, otm, s.PC, ident, 6)
        return s.phase_end(f'{l}C2')

    def mixD1(s, l):
        k = s.k
        with ExitStack() as es:
            wDq = s.loadw(es, f'{l}wDq'); wDk = s.loadw(es, f'{l}wDk'); wDc = s.loadw(es, f'{l}wDc'); wDv = s.loadw(es, f'{l}wDv')
            bDv = s.loadrow(es, l, 'bDv')
            colt = k.alloc(es, (128, NCOL), F32, 'col'); k.dma(colt[:], s.col[l], W=[colt])
            cq = COFF['bDq'][0]; ck = COFF['bDk'][0]; cc_ = COFF['bDc'][0]
            hT = [k.alloc(es, (128, 8, 512), BF16, 'hT') for _ in range(2)]
            Ct = [k.alloc(es, (128, 512), F32, 'Ct') for _ in range(2)]
            St = [k.alloc(es, (128, 512), F32, 'St') for _ in range(2)]
            qn = [k.alloc(es, (128, 8, 512), BF16, 'qn') for _ in range(2)]
            kk_ = [k.alloc(es, (128, 4, 512), BF16, 'kk') for _ in range(2)]
            kcv = [k.alloc(es, (128, 4, 512), BF16, 'kcv') for _ in range(2)]
            vsw = [k.alloc(es, (128, 4, 4, 65), BF16, 'vsw') for _ in range(2)]
            gate = [k.alloc(es, (128, 4, 24), F32, 'gate') for _ in range(2)]
            ta = [k.alloc(es, (128, 512), F32, 'ta') for _ in range(2)]
            tb = [k.alloc(es, (128, 512), F32, 'tb') for _ in range(2)]
            vb = [k.alloc(es, (128, 280), F32, 'vb') for _ in range(2)]
            for v in vsw:
                k.memset('pool', v[:, :, :, 64:65], 1.0, W=[v])
            o64 = FOFF['C64'][0]; s64 = FOFF['S64'][0]

            def load(G):
                i = G % 2
                k.dma(hT[i][:], s.XT[:, :, G * 512:(G + 1) * 512], W=[hT[i]])
                k.dma(Ct[i][0:64, :], s.cf[0:64, o64 + G * 512:o64 + (G + 1) * 512], W=[Ct[i]])
                k.dma(St[i][0:64, :], s.cf[0:64, s64 + G * 512:s64 + (G + 1) * 512], W=[St[i]])
            load(0)
            n = 0
            for G in range(8):
                if G + 1 < 8: load(G + 1)
                gi = G % 2
                h = hT[gi]; C = Ct[gi]; Sn = St[gi]

                def roped(w, x0, s0, bx, bs_, dst_t, dst_ap):
                    nonlocal n
                    i = n % 2; n += 1
                    px = k.ps[i]; py = k.ps[2 + i]
                    for kk in range(8):
                        k.mm(px, px[0:64, :], w[:, kk, x0:x0 + 64], h[:, kk, :], R=[w, h], start=(kk == 0), stop=(kk == 7), inc=(kk == 7))
                    for kk in range(8):
                        k.mm(py, py[0:64, :], w[:, kk, s0:s0 + 64], h[:, kk, :], R=[w, h], start=(kk == 0), stop=(kk == 7), inc=(kk == 7))
                    k.stt(ta[i][0:64, :], px[0:64, :], colt[0:64, bx:bx + 1], C[0:64, :], ALU.add, ALU.mult, R=[colt, C], W=[ta[i], px])
                    k.stt(tb[i][0:64, :], py[0:64, :], colt[0:64, bs_:bs_ + 1], Sn[0:64, :], ALU.add, ALU.mult, R=[colt, Sn], W=[tb[i], py])
                    k.tt('pool', dst_ap, ta[i][0:64, :], tb[i][0:64, :], ALU.add, R=[ta[i], tb[i]], W=[dst_t])
                for hh in range(8):
                    roped(wDq, hh * 64, 512 + hh * 64, cq + hh, cq + 8 + hh, qn[gi], qn[gi][0:64, hh, :])
                for nm in range(2):
                    for g in range(2):
                        x0 = (nm * 2) * 128 + g * 64
                        roped(wDk, x0, x0 + 128, ck + (nm * 2) * 2 + g, ck + (nm * 2 + 1) * 2 + g, kk_[gi], kk_[gi][0:64, nm * 2 + g, :])
                for j in range(4):
                    pb = k.ps[4 + j % 2]
                    for kk in range(8):
                        k.mm(pb, pb[0:64, :], wDc[:, kk, j * 64:(j + 1) * 64], h[:, kk, :], R=[wDc, h], start=(kk == 0), stop=(kk == 7), inc=(kk == 7))
                    k.act(kcv[gi][0:64, j, :], pb[0:64, :], AF.Identity, R=[colt], W=[kcv[gi], pb], bias=colt[0:64, cc_ + j:cc_ + j + 1])
                for c in range(4):
                    i = c % 2
                    pb = k.ps[6 + i]
                    for kk in range(8):
                        k.mm(pb, pb[:, 0:280], h[:, kk, c * 128:(c + 1) * 128], wDv[:, kk, :], R=[wDv, h], start=(kk == 0), stop=(kk == 7), inc=(kk == 7))
                    k.tt('dve', vb[i][:], pb[:, 0:280], bDv[:], ALU.add, R=[bDv], W=[vb[i], pb])
                    k.cp('pool', vsw[gi][:, :, c, 0:64], vb[i][:, 0:256].rearrange("p (a b) -> p a b", a=4), R=[vb[i]], W=[vsw[gi]])
                    k.act(gate[gi][:, c, :], vb[i][:, 256:280], AF.Sigmoid, R=[vb[i]], W=[gate[gi]])
                k.dma(s.QN[:, :, G * 512:(G + 1) * 512], qn[gi][0:64, :, :], R=[qn[gi]])
                k.dma(s.KK[:, :, G * 512:(G + 1) * 512], kk_[gi][0:64, :, :], R=[kk_[gi]])
                k.dma(s.KCV[:, :, G * 512:(G + 1) * 512], kcv[gi][0:64, :, :], R=[kcv[gi]])
                k.dma(s.VSW[:, :, G * 4:(G + 1) * 4, :], vsw[gi][:], R=[vsw[gi]])
                k.dma(s.GATE[:, G * 4:(G + 1) * 4, :], gate[gi][:], R=[gate[gi]])
        return s.phase_end(f'{l}D1')

    def mixD2(s, l, bgn=0):
        k = s.k
        with ExitStack() as es:
            ident = s.loadw(es, 'ident'); negc = s.loadw(es, 'negc'); nega = s.loadw(es, 'nega')
            negcmp = s.loadw(es, 'negcmp'); ovl = s.loadw(es, 'ovl')
            id32 = s.loadcf(es, 'id32')
            CM = s.loadcf(es, 'CM'); ADD = s.loadcf(es, 'ADD')
            kcmp = k.alloc(es, (128, 2, 256), BF16, 'kcmp')
            vcmp = k.alloc(es, (128, 2, 2, 65), BF16, 'vcmp')
            onsa = k.alloc(es, (128, NT, 512), BF16, 'onsa')
            gate = k.alloc(es, (128, NT, 24), F32, 'gate')
            k.dma(gate[:], s.GATE[:], W=[gate])
            k.memset('dve', kcmp[:], 0.0, W=[kcmp])
            k.memset('dve', vcmp[:], 0.0, W=[vcmp])
            k.memset('dve', vcmp[:, :, :, 64:65], 1.0, W=[vcmp])
            with ExitStack() as e2:
                wck = s.loadw(e2, f'{l}wck'); wcks = s.loadw(e2, f'{l}wcks'); wcv = s.loadw(e2, f'{l}wcv')
                kcv = k.alloc(e2, (128, 4, S), BF16, 'kcv'); k.dma(kcv[0:64, :, :], s.KCV[:], W=[kcv])
                colt = k.alloc(e2, (128, NCOL), F32, 'col'); k.dma(colt[:], s.col[l], W=[colt])
                Cc = s.loadcf(e2, 'Ccmp'); Sc = s.loadcf(e2, 'Scmp')
                pebk = k.alloc(e2, (128, 32, 128), BF16, 'pebk'); pebv = k.alloc(e2, (128, 32, 128), BF16, 'pebv')
                ok_ = COFF['pek'][0]; ov_ = COFF['pev'][0]
                k.cp('dve', pebk[0:64, :, :], cap(colt[0:64, ok_:ok_ + 1], [1, 32], [0, 128]), R=[colt], W=[pebk])
                k.cp('dve', pebv[0:64, :, :], cap(colt[0:64, ov_:ov_ + 1], [1, 32], [0, 128]), R=[colt], W=[pebv])
                ta = k.alloc(e2, (128, 128), F32, 'ta'); tb = k.alloc(e2, (128, 128), F32, 'tb')
                for g in range(2):
                    for nt in range(2):
                        nn = 128 if nt == 0 else 127
                        n0 = nt * 128
                        px = k.ps[0]; py = k.ps[1]; pv = k.ps[2]
                        for (pp, w) in ((px, wck), (py, wcks)):
                            for ll in range(32):
                                rhs = cap(kcv[0:64, g, n0 * 16 + ll:n0 * 16 + ll + 1], [16, nn])
                                k.mm(pp, pp[0:64, 0:nn], w[0:64, ll, :], rhs, R=[w, kcv], start=(ll == 0), stop=False, inc=False)
                            for ll in range(32):
                                k.mm(pp, pp[0:64, 0:nn], w[0:64, ll, :], pebk[0:64, ll, 0:nn], R=[w, pebk],
                                     start=False, stop=(ll == 31), inc=(ll == 31))
                        k.tt('dve', ta[0:64, 0:nn], px[0:64, 0:nn], Cc[0:64, n0:n0 + nn], ALU.mult, R=[Cc], W=[ta, px])
                        k.tt('dve', tb[0:64, 0:nn], py[0:64, 0:nn], Sc[0:64, n0:n0 + nn], ALU.mult, R=[Sc], W=[tb, py])
                        k.tt('pool', kcmp[0:64, g, n0:n0 + nn], ta[0:64, 0:nn], tb[0:64, 0:nn], ALU.add, R=[ta, tb], W=[kcmp])
                        for ll in range(32):
                            lhs = cap(kcv[0:64, 2 + g, n0 * 16 + ll:n0 * 16 + ll + 1], [16, nn])
                            k.mm(pv, pv[0:nn, 0:64], lhs, wcv[0:64, ll, :], R=[wcv, kcv], start=(ll == 0), stop=False, inc=False)
                        for ll in range(32):
                            k.mm(pv, pv[0:nn, 0:64], pebv[0:64, ll, 0:nn], wcv[0:64, ll, :], R=[wcv, pebv],
                                 start=False, stop=(ll == 31), inc=(ll == 31))
                        k.cp('act', vcmp[0:nn, nt, g, 0:64], pv[0:nn, 0:64], R=[], W=[vcmp, pv])
                k.barrier()
            qa = k.alloc(es, (128, 4, S), BF16, 'qa')
            ksa = k.alloc(es, (128, S), BF16, 'ksa')
            kw = k.alloc(es, (128, S), BF16, 'kw')
            vs = k.alloc(es, (128, NT, 65), BF16, 'vs'); vw = k.alloc(es, (128, NT, 65), BF16, 'vw')
            E = [k.alloc(es, (128, 512), BF16, 'E') for _ in range(3)]
            qas = T(qa.h)
            k.dma(ksa[64:128, :], s.wsl('eexp')[0:64, :], W=[ksa])
            k.memset('dve', qa[64:128, :, :], 0.0, W=[qas])
            sm = [type('o', (), {})() for _ in range(2)]
            for o in sm:
                o.dn = k.alloc(es, (128, 4), F32, 'dn'); o.rc = k.alloc(es, (128, 12), F32, 'rc')
                o.imp = k.alloc(es, (128, 64), F32, 'imp'); o.i2 = k.alloc(es, (128, 64), F32, 'i2')
                o.m8 = k.alloc(es, (128, 8), F32, 'm8'); o.thr = k.alloc(es, (128, 1), F32, 'thr')
                o.sel = k.alloc(es, (128, 128), F32, 'sel')
                k.memset('dve', o.sel[:], 0.0, W=[o.sel])
                o.coef = k.alloc(es, (128, 3, 4), F32, 'coef')
                o.o1 = k.alloc(es, (128, 4, 64), F32, 'o1'); o.o2 = k.alloc(es, (128, 4, 64), F32, 'o2'); o.o3 = k.alloc(es, (128, 4, 64), F32, 'o3')
            ctr = [0]
            Oc = k.ps[3]; IMP = k.ps[4]; Os = k.ps[5]; Ow = k.ps[6]; pT = k.ps[7]
            bc4 = lambda t: cap(t[:, 0:1], [0, 4], [1, 128])
            v4 = lambda Sb: Sb[:, :].rearrange("p (a b) -> p a b", a=4)
            for g in range(2):
                k.dma(qa[0:64, :, :], s.QN[:, 4 * g:4 * g + 4, :], W=[qa])
                k.dma(ksa[0:64, :], s.KK[:, g, :], W=[ksa]); k.dma(kw[0:64, :], s.KK[:, 2 + g, :], W=[kw])
                k.dma(vs[:], s.VSW[:, g, :, :], W=[vs]); k.dma(vw[:], s.VSW[:, 2 + g, :, :], W=[vw])
                items = []
                O4 = lambda Ob: Ob[:, 0:260].rearrange("p (a b) -> p a b", a=4)
                gv = lambda qt, b: cap(gate[:, qt, g * 12 + b:g * 12 + b + 1], [3, 4])

                def xexp(Sb, Eb):
                    k.act(Eb[:], Sb[:, :], AF.Exp, R=[], W=[Eb, Sb], scale=0.125)

                def zero_all():
                    k.memset('dve', Oc[:, 0:260], 0.0, W=[Oc]); k.memset('dve', IMP[:, 0:256], 0.0, W=[IMP])
                    k.memset('dve', Os[:, 0:260], 0.0, W=[Os]); k.memset('dve', Ow[:, 0:260], 0.0, W=[Ow])

                def cmp_items(qt):
                    qsl = slice(qt * 128, (qt + 1) * 128)
                    o = sm[qt % 2]
                    out = []
                    for nt in ([0] if qt < 16 else [0, 1]):
                        def fS(Sb, nt=nt):
                            k.mm(Sb, v4(Sb), kcmp[0:64, g, nt * 128:(nt + 1) * 128], qa[0:64, :, qsl], R=[kcmp, qa], start=True, stop=False, inc=False)
                            k.mm(Sb, v4(Sb), ident[:], bc4(negcmp[:, nt, qsl]), R=[ident, negcmp], start=False, stop=True)

                        def fP(Eb, nt=nt):
                            for hh in range(4):
                                k.mm(Oc, Oc[:, hh * 65:(hh + 1) * 65], Eb[:, hh * 128:(hh + 1) * 128], vcmp[:, nt, g, :],
                                     R=[Eb, vcmp], start=False, stop=True, skip=True, inc=(hh == 3))
                            for hh in range(4):
                                k.mm(IMP, IMP[:, hh * 64:(hh + 1) * 64], Eb[:, hh * 128:(hh + 1) * 128], ovl[:, nt, :],
                                     R=[Eb, ovl], start=False, stop=True, skip=True, inc=(hh == 3))
                        out.append(U(fS, xexp, fP))

                    def sel1():
                        s.bg(bgn)
                        Ocv = O4(Oc)
                        k.ts('dve', o.dn[:], Ocv[:, :, 64], 1e-30, None, ALU.max, None, R=[], W=[o.dn, Oc])
                        k.op('dve', lambda g_: g_.reciprocal(out=o.rc[:, 0:4], in_=o.dn[:]), R=[o.dn], W=[o.rc])
                        k.ts('dve', o.imp[:], IMP[:, 0:64], o.rc[:, 0:1], None, ALU.mult, None, R=[o.rc], W=[o.imp, IMP])
                        for hh in range(1, 4):
                            dstb, srcb = (o.i2, o.imp) if hh % 2 == 1 else (o.imp, o.i2)
                            k.stt(dstb[:], IMP[:, hh * 64:(hh + 1) * 64], o.rc[:, hh:hh + 1], srcb[:], ALU.mult, ALU.add,
                                  R=[o.rc, srcb], W=[dstb, IMP])
                        k.tt('dve', o.imp[:], o.i2[:], CM[:, qt * 64:(qt + 1) * 64], ALU.mult, R=[o.i2, CM], W=[o.imp])
                        k.tt('dve', o.i2[:], o.imp[:], ADD[:, qt * 64:(qt + 1) * 64], ALU.add, R=[o.imp, ADD], W=[o.i2])
                        k.op('dve', lambda g_: g_.max(out=o.m8[:], in_=o.i2[:]), R=[o.i2], W=[o.m8])
                        k.ts('dve', o.thr[:], o.m8[:, 7:8], 0.0, None, ALU.max, None, R=[o.m8], W=[o.thr])
                        k.ts('dve', o.sel[:, 64:128], o.i2[:], o.thr[:, 0:1], 1.0, ALU.is_ge, ALU.subtract, R=[o.i2, o.thr], W=[o.sel])
                        k.tt('dve', o.coef[:, 0, :], gv(qt, 0), o.rc[:, 0:4], ALU.mult, R=[gate, o.rc], W=[o.coef])
                        k.tt('dve', o.o1[:], Ocv[:, :, 0:64], cap(o.coef[:, 0, 0:1], [1, 4], [0, 64]), ALU.mult, R=[o.coef], W=[o.o1, Oc])
                        k.memset('dve', Oc[:, 0:260], 0.0, W=[Oc]); k.memset('dve', IMP[:, 0:256], 0.0, W=[IMP])
                    out.append(('soft', sel1))
                    return out

                items.append(('soft', zero_all))
                items += cmp_items(0)
                for qt in range(NT):
                    o = sm[qt % 2]
                    qsl = slice(qt * 128, (qt + 1) * 128)
                    def sel2(o=o, qsl=qsl):
                        k.tr(pT, pT[:, 0:128], o.sel[:], id32[:], R=[o.sel, id32])
                        k.cp('dve', qa[64:128, :, qsl], cap(pT[64:128, 0:1], [0, 4], [1, 128]), R=[], W=[qas, pT])
                    items.append(('hard', sel2))
                    for kt in range(max(0, qt - 4), qt + 1):
                        msk = negc if kt == qt else (nega if kt == qt - 4 else None)

                        def fS(Sb, kt=kt, msk=msk, qsl=qsl):
                            k.mm(Sb, v4(Sb), kw[0:64, kt * 128:(kt + 1) * 128], qa[0:64, :, qsl], R=[kw, qa],
                                 start=True, stop=(msk is None), inc=(msk is None))
                            if msk is not None:
                                k.mm(Sb, v4(Sb), ident[:], bc4(msk), R=[ident, msk], start=False, stop=True)

                        def fP(Eb, kt=kt):
                            for hh in range(4):
                                k.mm(Ow, Ow[:, hh * 65:(hh + 1) * 65], Eb[:, hh * 128:(hh + 1) * 128], vw[:, kt, :],
                                     R=[Eb, vw], start=False, stop=True, skip=True, inc=(hh == 3))
                        items.append(U(fS, xexp, fP))

                    def combW(o=o, qt=qt):
                        Owv = O4(Ow)
                        k.op('dve', lambda g_: g_.reciprocal(out=o.rc[:, 8:12], in_=Owv[:, :, 64]), R=[], W=[o.rc, Ow])
                        k.tt('dve', o.coef[:, 2, :], gv(qt, 2), o.rc[:, 8:12], ALU.mult, R=[gate, o.rc], W=[o.coef])
                        k.tt('dve', o.o3[:], Owv[:, :, 0:64], cap(o.coef[:, 2, 0:1], [1, 4], [0, 64]), ALU.mult, R=[o.coef], W=[o.o3, Ow])
                        k.memset('dve', Ow[:, 0:260], 0.0, W=[Ow])
                    items.append(('soft', combW))

                    for kt in range(qt + 1):
                        def fS(Sb, kt=kt, qt=qt, qsl=qsl):
                            k.mm(Sb, v4(Sb), ksa[:, kt * 128:(kt + 1) * 128], qa[:, :, qsl], R=[ksa, qa, qas],
                                 start=True, stop=(kt != qt), inc=(kt != qt))
                            if kt == qt:
                                k.mm(Sb, v4(Sb), ident[:], bc4(negc), R=[ident, negc], start=False, stop=True)

                        def fP(Eb, kt=kt):
                            for hh in range(4):
                                k.mm(Os, Os[:, hh * 65:(hh + 1) * 65], Eb[:, hh * 128:(hh + 1) * 128], vs[:, kt, :],
                                     R=[Eb, vs], start=False, stop=True, skip=True, inc=(hh == 3))
                        items.append(U(fS, xexp, fP))
                        if kt == 0 and qt + 1 < NT:
                            items += cmp_items(qt + 1)

                    def combS(o=o, qt=qt):
                        Osv = O4(Os)
                        k.op('dve', lambda g_: g_.reciprocal(out=o.rc[:, 4:8], in_=Osv[:, :, 64]), R=[], W=[o.rc, Os])
                        k.tt('dve', o.coef[:, 1, :], gv(qt, 1), o.rc[:, 4:8], ALU.mult, R=[gate, o.rc], W=[o.coef])
                        k.tt('dve', o.o2[:], Osv[:, :, 0:64], cap(o.coef[:, 1, 0:1], [1, 4], [0, 64]), ALU.mult, R=[o.coef], W=[o.o2, Os])
                        k.memset('dve', Os[:, 0:260], 0.0, W=[Os])
                        k.tt('pool', o.o1[:], o.o1[:], o.o2[:], ALU.add, R=[o.o1, o.o2], W=[o.o1])
                        k.tt('pool', onsa[:, qt, g * 256:(g + 1) * 256].rearrange("p (a b) -> p a b", a=4), o.o1[:], o.o3[:], ALU.add,
                             R=[o.o1, o.o3], W=[onsa])
                    items.append(('soft', combS))
                run_items(k, items, [0, 1, 2], E, ctr)
            s.otm_to_fm(es, onsa, s.PD, ident, 0)
        return s.phase_end(f'{l}D2')

    def merge1(s, l):
        k = s.k
        PS = [s.PA, s.PB, s.PC, s.PD]
        for half in range(2):
            with ExitStack() as es:
                wG = []; wos = []
                for b, c in enumerate('ABCD'):
                    t = k.alloc(es, (128, 8, 512), BF16, 'wG')
                    k.dma(t[:], s.wsl(f'{l}wG')[:, :, b * 1024 + half * 512:b * 1024 + (half + 1) * 512], W=[t]); wG.append(t)
                    t = k.alloc(es, (128, 4, 512), BF16, 'wos')
                    k.dma(t[:], s.wsl(f'{l}wo{c}')[:, :, half * 512:(half + 1) * 512], W=[t]); wos.append(t)
                colt = k.alloc(es, (128, NCOL), F32, 'col'); k.dma(colt[:], s.col[l], W=[colt])
                cg = COFF['bG'][0]
                hT = [k.alloc(es, (128, 8, 512), BF16, 'hT') for _ in range(2)]
                Pin = [[k.alloc(es, (128, 4, 512), BF16, 'Pin') for _ in range(4)] for _ in range(2)]
                mT = [k.alloc(es, (128, 4, 512), BF16, 'mT') for _ in range(2)]
                sig = [k.alloc(es, (128, 512), F32, 'sig') for _ in range(2)]
                acc = [k.alloc(es, (128, 512), F32, 'acc') for _ in range(2)]
                tm = [k.alloc(es, (128, 512), F32, 'tm') for _ in range(2)]

                def load(G):
                    i = G % 2
                    k.dma(hT[i][:], s.XT[:, :, G * 512:(G + 1) * 512], W=[hT[i]])
                    for b in range(4):
                        k.dma(Pin[i][b][:], PS[b][:, :, G * 512:(G + 1) * 512], W=[Pin[i][b]])
                load(0)
                n = 0
                for G in range(8):
                    if G + 1 < 8: load(G + 1)
                    gi = G % 2
                    h = hT[gi]; m = mT[gi]
                    for dcl in range(4):
                        dc = half * 4 + dcl
                        a = acc[dcl % 2]
                        for b in range(4):
                            i = n % 2; n += 1
                            pg = k.ps[i]; py = k.ps[2 + i]
                            for kk in range(8):
                                k.mm(pg, pg[:, :], wG[b][:, kk, dcl * 128:(dcl + 1) * 128], h[:, kk, :], R=[wG[b], h], start=(kk == 0), stop=(kk == 7), inc=(kk == 7))
                            for fc in range(4):
                                k.mm(py, py[:, :], wos[b][:, fc, dcl * 128:(dcl + 1) * 128], Pin[gi][b][:, fc, :], R=[wos[b], Pin[gi][b]],
                                     start=(fc == 0), stop=(fc == 3), inc=(fc == 3))
                            k.act(sig[i][:], pg[:, :], AF.Sigmoid, R=[colt], W=[sig[i], pg], bias=colt[:, cg + b * 8 + dc:cg + b * 8 + dc + 1])
                            if b == 0:
                                k.tt('dve', a[:], sig[i][:], py[:, :], ALU.mult, R=[sig[i]], W=[a, py])
                            else:
                                k.tt('dve', tm[i][:], sig[i][:], py[:, :], ALU.mult, R=[sig[i]], W=[tm[i], py])
                                if b < 3:
                                    k.tt('pool', a[:], a[:], tm[i][:], ALU.add, R=[a, tm[i]], W=[a])
                                else:
                                    k.tt('pool', m[:, dcl, :], a[:], tm[i][:], ALU.add, R=[a, tm[i]], W=[m])
                    k.dma(s.MT[:, half * 4:(half + 1) * 4, G * 512:(G + 1) * 512], m[:], R=[m])
            s.k.barrier()
        return s.phase_end(f'{l}M1')

    def merge2(s, l):
        k = s.k
        with ExitStack() as es:
            wo = s.loadw(es, f'{l}wo'); ident = s.loadw(es, 'ident')
            ln = s.ln_setup(es, l, 2)
            mT = [k.alloc(es, (128, 8, 512), BF16, 'mT') for _ in range(2)]
            xr = [k.alloc(es, (128, D), F32, 'xr') for _ in range(4)]
            pre = [k.alloc(es, (128, D), F32, 'pre') for _ in range(4)]

            def load(G):
                k.dma(mT[G % 2][:], s.MT[:, :, G * 512:(G + 1) * 512], W=[mT[G % 2]])

            def ldx(tt):
                if tt < NT: k.dma(xr[tt % 4][:], s.XR[tt * 128:(tt + 1) * 128, :], W=[xr[tt % 4]])
            load(0); ldx(0); ldx(1)
            for G in range(8):
                if G + 1 < 8: load(G + 1)
                m = mT[G % 2]
                for t4 in range(4):
                    tt = G * 4 + t4
                    xx = xr[tt % 4]; pp = pre[tt % 4]
                    ldx(tt + 2)
                    for dh in range(2):
                        pb = k.ps[(tt % 2) * 2 + dh]
                        for kk in range(8):
                            k.mm(pb, pb[:, :], m[:, kk, t4 * 128:(t4 + 1) * 128], wo[:, kk, dh * 512:(dh + 1) * 512], R=[m, wo],
                                 start=(kk == 0), stop=(kk == 7), inc=(kk == 7))
                        k.stt(pp[:, dh * 512:(dh + 1) * 512], xx[:, dh * 512:(dh + 1) * 512], ALPHA, pb[:, :], ALU.mult, ALU.add,
                              R=[xx], W=[pp, pb])
                    s.ln_tile(ln, pp, tt, s.XR, 4 + tt % 2, ident, True)
            s.flush(0)
        return s.phase_end(f'{l}M2')

    def xattn(s, l):
        k = s.k
        sc = float(128 ** -0.5)
        GT = 256
        with ExitStack() as es:
            xq = s.loadw(es, f'{l}xq'); xk = s.loadw(es, f'{l}xk'); xv = s.loadw(es, f'{l}xv'); xo = s.loadw(es, f'{l}xo')
            ident = s.loadw(es, 'ident')
            ln = s.ln_setup(es, l, 3)
            memT = k.alloc(es, (128, 8, MEM), BF16, 'memT')
            kT = k.alloc(es, (128, 4, MEM), BF16, 'kT')
            va = k.alloc(es, (128, 2, 4, 129), BF16, 'va')
            k.memset('dve', va[:, :, :, 128:129], 1.0, W=[va])
            with ExitStack() as e2:
                mf = k.alloc(e2, (128, D), F32, 'mf'); mb = k.alloc(e2, (128, D), BF16, 'mb')
                for mt in range(2):
                    k.dma(mf[:], s.mem_in[mt * 128:(mt + 1) * 128, :], W=[mf])
                    k.cp('dve', mb[:], mf[:], R=[mf], W=[mb])
                    pt = k.ps[mt]; ptv = pt[:].bitcast(BF16)
                    for j in range(8):
                        k.tr(pt, ptv[:, j * 128:(j + 1) * 128], mb[:, j * 128:(j + 1) * 128], ident[:], R=[mb, ident], inc=(j == 7))
                    k.cp('act', memT[:, :, mt * 128:(mt + 1) * 128], ptv[:, :].rearrange("p (a b) -> p a b", a=8), R=[], W=[memT, pt])
                for hh in range(4):
                    pb = k.ps[2 + hh % 2]
                    for kk in range(8):
                        k.mm(pb, pb[:, 0:MEM], xk[:, kk, hh * 128:(hh + 1) * 128], memT[:, kk, :], R=[xk, memT], start=(kk == 0), stop=(kk == 7), inc=(kk == 7))
                    k.cp('act', kT[:, hh, :], pb[:, 0:MEM], R=[], W=[kT, pb])
                for mt in range(2):
                    pb = k.ps[4 + mt]
                    for kk in range(8):
                        k.mm(pb, pb[:, :], memT[:, kk, mt * 128:(mt + 1) * 128], xv[:, kk, :], R=[xv, memT], start=(kk == 0), stop=(kk == 7), inc=(kk == 7))
                    k.cp('act', va[:, mt, :, 0:128], pb[:, :].rearrange("p (a b) -> p a b", a=4), R=[], W=[va, pb])
                k.barrier()
            xT = [k.alloc(es, (128, 8, GT), BF16, 'xT') for _ in range(2)]
            qT = [k.alloc(es, (128, 4, GT), BF16, 'qT') for _ in range(2)]
            E = [k.alloc(es, (128, GT), BF16, 'E') for _ in range(3)]
            otm = [k.alloc(es, (128, 2, 512), BF16, 'otm') for _ in range(2)]
            oT = [k.alloc(es, (128, 4, GT), BF16, 'oT') for _ in range(2)]
            rc = [k.alloc(es, (128, 2), F32, 'rc') for _ in range(2)]
            xr = [k.alloc(es, (128, D), F32, 'xr') for _ in range(4)]
            pre = [k.alloc(es, (128, D), F32, 'pre') for _ in range(4)]

            def load(G):
                k.dma(xT[G % 2][:], s.XT[:, :, G * GT:(G + 1) * GT], W=[xT[G % 2]])

            def ldx(tt):
                if tt < NT: k.dma(xr[tt % 4][:], s.XR[tt * 128:(tt + 1) * 128, :], W=[xr[tt % 4]])
            cn = {'ne': 0, 'nh': 0}; xctr = [0]
            NGX = S // GT

            def stage1(G):
                gi = G % 2
                x = xT[gi]; q = qT[gi]; ot = otm[gi]
                for hh in range(4):
                    pb = k.ps[hh % 2]
                    for kk in range(8):
                        k.mm(pb, pb[:, 0:GT], xq[:, kk, hh * 128:(hh + 1) * 128], x[:, kk, :], R=[xq, x], start=(kk == 0), stop=(kk == 7), inc=(kk == 7))
                    k.cp('dve', q[:, hh, :], pb[:, 0:GT], R=[], W=[q, pb])
                items = []
                for hh in range(4):
                    O = k.ps[2 + cn['nh'] % 2]; r = rc[cn['nh'] % 2]; cn['nh'] += 1
                    items.append(('soft', lambda O=O: k.memset('dve', O[:, 0:258], 0.0, W=[O])))
                    for mt in range(2):
                        def fS(Sb, hh=hh, mt=mt):
                            k.mm(Sb, Sb[:, 0:GT], kT[:, hh, mt * 128:(mt + 1) * 128], q[:, hh, :], R=[kT, q])

                        def fX(Sb, Eb):
                            k.act(Eb[:], Sb[:, 0:GT], AF.Exp, R=[], W=[Eb, Sb], scale=sc)

                        def fP(Eb, hh=hh, mt=mt, O=O):
                            for qs in range(2):
                                k.mm(O, O[:, qs * 129:(qs + 1) * 129], Eb[:, qs * 128:(qs + 1) * 128], va[:, mt, hh, :], R=[Eb, va],
                                     start=False, stop=True, skip=True, inc=(qs == 1))
                        items.append(U(fS, fX, fP))

                    def fin(O=O, r=r, hh=hh):
                        Ov = O[:, 0:258].rearrange("p (a b) -> p a b", a=2)
                        k.op('dve', lambda g_: g_.reciprocal(out=r[:], in_=Ov[:, :, 128]), R=[], W=[r, O])
                        k.tt('dve', ot[:, :, hh * 128:(hh + 1) * 128], Ov[:, :, 0:128], cap(r[:, 0:1], [1, 2], [0, 128]), ALU.mult, R=[r], W=[ot, O])
                    items.append(('soft', fin))
                run_items(k, items, [4, 5], E, xctr)

            def stage2(G):
                gi = G % 2
                ot = otm[gi]; o_T = oT[gi]
                for qs in range(2):
                    pt = k.ps[6]; ptv = pt[:].bitcast(BF16)
                    for j in range(4):
                        k.tr(pt, ptv[:, j * 128:(j + 1) * 128], ot[:, qs, j * 128:(j + 1) * 128], ident[:], R=[ot, ident], inc=(j == 3))
                    k.cp('act', o_T[:, :, qs * 128:(qs + 1) * 128], ptv[:, 0:512].rearrange("p (a b) -> p a b", a=4), R=[], W=[o_T, pt])
                for qs in range(2):
                    tt = G * 2 + qs
                    xx = xr[tt % 4]; pp = pre[tt % 4]
                    ldx(tt + 2)
                    for dh in range(2):
                        pb = k.ps[dh]
                        for kk in range(4):
                            k.mm(pb, pb[:, :], o_T[:, kk, qs * 128:(qs + 1) * 128], xo[:, kk, dh * 512:(dh + 1) * 512], R=[o_T, xo],
                                 start=(kk == 0), stop=(kk == 3), inc=(kk == 3))
                        k.stt(pp[:, dh * 512:(dh + 1) * 512], xx[:, dh * 512:(dh + 1) * 512], ALPHA, pb[:, :], ALU.mult, ALU.add,
                              R=[xx], W=[pp, pb])
                    s.ln_tile(ln, pp, tt, s.XR, 7, ident, True)

            load(0); load(1); ldx(0); ldx(1)
            stage1(0)
            for G in range(NGX):
                if G + 1 < NGX:
                    stage1(G + 1)
                    if G + 2 < NGX: load(G + 2)
                stage2(G)
            s.flush(0)
        return s.phase_end(f'{l}X')

    def build(s):
        s.prep()
        s.xt0()
        endL0 = WOFF['0f2w2'][0] + WOFF['0f2w2'][1]
        for l in range(L):
            src = s.x_in if l == 0 else s.XR
            first = (l == 0)
            if first:
                pass
            stop = s._ffn_bg(l, 'f1', 1, src, s.XR, True, 6 if first else 0, endL0 if first else None)
            if stop: break
            if s.mixA(l): break
            if s.mixB(l): break
            if s.mixC1(l): break
            if s.mixC2(l, bgn=1 if first else 0): break
            if s.mixD1(l): break
            if s.mixD2(l, bgn=2 if first else 0): break
            if s.merge1(l): break
            if s.merge2(l): break
            if s.xattn(l): break
            last = (l == L - 1)
            if s._ffn_bg(l, 'f2', 4, s.XR, s.out if (last and not s.dbg) else s.XR, not last, 2 if first else 0, NWB if first else None): break
        s.k.barrier()
        s.k.es.close()

    def _ffn_bg(s, l, f, lni, src, dst, write_xt, bgn, until):
        s._bg_until_col = until
        return s.ffn(l, f, lni, src, dst, write_xt, bgn)


_CACHE = {}


def _get_nc(dbg=False, stop_after=None):
    key = (dbg, stop_after)
    if key not in _CACHE:
        nc = bass.Bass("TRN2", target_bir_lowering=False)
        Prog(nc, dbg, stop_after).build()
        _CACHE[key] = nc
    return _CACHE[key]


def _inmaps(inputs):
    wb, col, row, cf = _pack(inputs)
    x = np.asarray(inputs['x'], dtype=np.float32); mem = np.asarray(inputs['mem'], dtype=np.float32)
    return [{"x": np.ascontiguousarray(x[c]), "mem": np.ascontiguousarray(mem[c]), "wbig": wb, "col": col, "row": row, "cf": cf}
            for c in range(8)]


def kernel(**inputs):
    nc = _get_nc()
    res = run_bass_kernel_spmd(nc, _inmaps(inputs), core_ids=list(range(8)))
    return np.stack([np.asarray(r["out"], dtype=np.float32) for r in res.results], axis=0)
```

```python
import numpy as np
import concourse.bass as bass
import concourse.mybir as mybir
from concourse.bass_utils import run_bass_kernel_spmd
from contextlib import ExitStack

F32 = mybir.dt.float32
BF16 = mybir.dt.bfloat16
AF = mybir.ActivationFunctionType
ALU = mybir.AluOpType

S = 4096; D = 1024; FF = 2816; NT = 32; L = 2; MEM = 256
ALPHA = float(4 ** 0.25)
NEGV = -30000.0
LN_EPS = 1e-5; RMS_EPS = 1e-6
IN_SPLITS = (512, 512, 512, 512, 512, 256, 128, 32, 512) + (128,) * 6 + (24,) + (1024,) * 4
IN_NAMES = ['u', 'v', 'cb', 'cc', 'ch', 'qlat', 'kvlat', 'krope', 'nq', 'nkc', 'nvc', 'nks', 'nvs', 'nkw', 'nvw',
            'ngate', 'ga', 'gb', 'gc', 'gd']


def _offsets(spec):
    off = {}; o = 0
    for n, shp in spec:
        sz = int(np.prod(shp))
        off[n] = (o, sz, tuple(shp)); o += sz
    return off, o


def wspec():
    sp = [('ident', (128,)), ('tril', (128,)), ('negc', (128,)), ('nega', (128,)), ('eexp', (4096,)),
          ('negcmp', (2, 4096)), ('ovl', (2, 64))]
    for l in range(L):
        sp += [(f'{l}f1w1', (8, FF)), (f'{l}f1w3', (8, FF)), (f'{l}f1w2', (22, D)),
               (f'{l}wA', (8, 1024)), (f'{l}wB', (8, 1536)), (f'{l}wC1', (8, 384)), (f'{l}wC2', (8, 64)),
               (f'{l}wDq', (8, 1024)), (f'{l}wDk', (8, 512)), (f'{l}wDc', (8, 256)), (f'{l}wDv', (8, 280)),
               (f'{l}wG', (8, 4096)),
               (f'{l}WsT', (4, 128)), (f'{l}woA', (4, D)), (f'{l}woB', (4, D)), (f'{l}woC', (4, D)), (f'{l}woD', (4, D)),
               (f'{l}wuq', (2, 768)), (f'{l}wuqs', (2, 768)), (f'{l}wukv', (1024,)),
               (f'{l}wck', (32, 64)), (f'{l}wcks', (32, 64)), (f'{l}wcv', (32, 64)),
               (f'{l}wo', (8, D)), (f'{l}xq', (8, 512)), (f'{l}xk', (8, 512)), (f'{l}xv', (8, 512)), (f'{l}xo', (4, D)),
               (f'{l}f2w1', (8, FF)), (f'{l}f2w3', (8, FF)), (f'{l}f2w2', (22, D))]
    return sp


CSPEC = [('bAu', 4), ('bB', 12), ('cw', 12), ('bC2', 2), ('bDq', 16), ('bDk', 8), ('bDc', 4), ('bG', 32),
         ('pek', 32), ('pev', 32)]
RSPEC = [('bAv', 512), ('glng', 512), ('glnb', 512), ('bs', 512), ('bC1', 384), ('qg', 256), ('kvg', 128),
         ('bDv', 280)] + [(f'ln{i}{c}', 1024) for i in (1, 2, 3, 4) for c in 'gb']
FSPEC = [('C64', 4096), ('S64', 4096), ('C96', 4096), ('S96', 4096), ('Ccmp', 256), ('Scmp', 256),
         ('CM', 2048), ('ADD', 2048), ('id32', 128)]

WOFF, NWB = _offsets(wspec())
COFF, NCOL = _offsets([(n, (s,)) for n, s in CSPEC])
ROFF, NROW = _offsets([(n, (s,)) for n, s in RSPEC])
FOFF, NCF = _offsets([(n, (s,)) for n, s in FSPEC])


def _pk(w):
    k, n = w.shape
    return np.ascontiguousarray(w.reshape(k // 128, 128, n).transpose(1, 0, 2))


def _rope_tab(pos, dim):
    inv = (np.float32(10000.0) ** (-(np.arange(0, dim, 2, dtype=np.float32) / np.float32(dim)))).astype(np.float32)
    ang = pos.astype(np.float32)[:, None] * inv[None, :]
    return np.cos(ang).astype(np.float32), np.sin(ang).astype(np.float32)


def _consts_bf():
    j = np.arange(128)[:, None]; i = np.arange(128)[None, :]
    c = {}
    c['ident'] = (j == i).astype(np.float32)
    c['tril'] = (j <= i).astype(np.float32)
    c['negc'] = np.where(j <= i, 0.0, NEGV).astype(np.float32)
    c['nega'] = np.where(j > i, 0.0, NEGV).astype(np.float32)
    ee = np.zeros((128, 4096), np.float32)
    key = np.arange(4096)
    for jj in range(64):
        ee[jj, key // 64 == jj] = -NEGV
    c['eexp'] = ee
    idx = (np.arange(2)[None, :, None] * 128 + np.arange(128)[:, None, None])
    q = np.arange(4096)[None, None, :]
    c['negcmp'] = np.where((idx < 255) & (16 * idx + 31 <= q), 0.0, NEGV).astype(np.float32)
    jj = np.arange(64)[None, None, :]
    ov = np.minimum(16 * idx + 32, 64 * jj + 64) - np.maximum(16 * idx, 64 * jj)
    ov = np.clip(ov, 0, None).astype(np.float32) / 32.0
    c['ovl'] = np.where(idx < 255, ov, 0.0).astype(np.float32)
    return c


def _consts_f32():
    cf = np.zeros((128, NCF), np.float32)
    pos = np.arange(S, dtype=np.float32)
    c64, s64 = _rope_tab(pos, 64)
    c32, s32 = _rope_tab(pos, 32)
    r = np.arange(128)
    C64 = c64.T[r % 32]
    S64 = np.where(((r % 64) < 32)[:, None], -s64.T[r % 32], s64.T[r % 32])
    C96 = np.zeros((128, S), np.float32); S96 = np.zeros((128, S), np.float32)
    C96[0:64] = 1.0
    rr = np.arange(32)
    C96[64:96] = c32.T[rr % 16]
    S96[64:96] = np.where((rr < 16)[:, None], -s32.T[rr % 16], s32.T[rr % 16])
    pc = (np.arange(255) * 16 + 31).astype(np.float32)
    cc, sc = _rope_tab(pc, 64)
    Cc = np.zeros((128, 256), np.float32); Sc = np.zeros((128, 256), np.float32)
    Cc[:, :255] = cc.T[r % 32]
    Sc[:, :255] = np.where(((r % 64) < 32)[:, None], -sc.T[r % 32], sc.T[r % 32])
    qq = (np.arange(32)[None, :, None] * 128 + np.arange(128)[:, None, None])
    jq = qq // 64
    jj = np.arange(64)[None, None, :]
    forced = (jj == 0) | (jj == jq) | (jj == jq - 1)
    causal = jj <= jq
    CM = (causal & ~forced).astype(np.float32)
    ADD = np.where(forced, 1e9, np.where(causal, 0.0, -1.0)).astype(np.float32)
    for n, a in (('C64', C64), ('S64', S64), ('C96', C96), ('S96', S96), ('Ccmp', Cc), ('Scmp', Sc),
                 ('CM', CM.reshape(128, -1)), ('ADD', ADD.reshape(128, -1)), ('id32', np.eye(128, dtype=np.float32))):
        o, sz, _ = FOFF[n]
        cf[:, o:o + sz] = a
    return cf


def _pack(inp):
    wb = np.zeros((128, NWB), np.float32)
    col = np.zeros((L, 128, NCOL), np.float32)
    row = np.zeros((L, 1, NROW), np.float32)

    def put(name, a):
        o, sz, shp = WOFF[name]
        assert a.shape == (128,) + shp, (name, a.shape, shp)
        wb[:, o:o + sz] = a.reshape(128, sz)

    def putc(l, name, a):
        o, sz, _ = COFF[name]
        col[l, :, o:o + sz] = a

    def putr(l, name, a):
        o, sz, _ = ROFF[name]
        row[l, 0, o:o + sz] = a

    offs = np.cumsum((0,) + IN_SPLITS)
    sw32 = np.concatenate([np.arange(16, 32), np.arange(0, 16)])
    sw64 = np.concatenate([np.arange(32, 64), np.arange(0, 32)])
    swq = np.concatenate([h * 64 + sw64 for h in range(8)])
    fm = lambda v: np.ascontiguousarray(v.reshape(-1, 128).T)
    for l in range(L):
        g = lambda n: np.asarray(inp[n][l], dtype=np.float32)
        win = g('w_in'); bin_ = g('b_in')
        cs = {n: win[:, offs[i]:offs[i + 1]] for i, n in enumerate(IN_NAMES)}
        bs_ = {n: bin_[offs[i]:offs[i + 1]] for i, n in enumerate(IN_NAMES)}
        for f, pre in (('f1', 'ffn1'), ('f2', 'ffn2')):
            put(f'{l}{f}w1', _pk(g(pre + '_w1'))); put(f'{l}{f}w3', _pk(g(pre + '_w3'))); put(f'{l}{f}w2', _pk(g(pre + '_w2')))
        put(f'{l}wA', _pk(np.concatenate([cs['u'], cs['v']], 1)))
        put(f'{l}wB', _pk(np.concatenate([cs['cb'], cs['cc'], cs['ch']], 1)))
        put(f'{l}wC1', _pk(np.concatenate([cs['qlat'], cs['kvlat']], 1)))
        put(f'{l}wC2', _pk(np.concatenate([cs['krope'], cs['krope'][:, sw32]], 1)))
        put(f'{l}wDq', _pk(np.concatenate([cs['nq'], cs['nq'][:, swq]], 1)))
        kcols = []; kb = []
        for nm in ('nks', 'nkw'):
            for sw in (False, True):
                idx = np.concatenate([gg * 64 + (sw64 if sw else np.arange(64)) for gg in range(2)])
                kcols.append(cs[nm][:, idx]); kb.append(bs_[nm][idx])
        put(f'{l}wDk', _pk(np.concatenate(kcols, 1)))
        put(f'{l}wDc', _pk(np.concatenate([cs['nkc'], cs['nvc']], 1)))
        put(f'{l}wDv', _pk(np.concatenate([cs['nvs'], cs['nvw'], cs['ngate']], 1)))
        put(f'{l}wG', _pk(np.concatenate([cs['ga'], cs['gb'], cs['gc'], cs['gd']], 1)))
        put(f'{l}WsT', np.ascontiguousarray(g('gmlp_ws').transpose(2, 0, 1)))
        put(f'{l}woA', _pk(g('gmlp_wout'))); put(f'{l}woB', _pk(g('conv_wout')))
        put(f'{l}woC', _pk(g('mla_wout'))); put(f'{l}woD', _pk(g('nsa_wout')))
        wuq = g('mla_wuq')
        swu = np.concatenate([np.concatenate([h * 96 + np.arange(64), h * 96 + 64 + sw32]) for h in range(8)])
        put(f'{l}wuq', _pk(wuq)); put(f'{l}wuqs', _pk(wuq[:, swu]))
        put(f'{l}wukv', g('mla_wukv'))
        wck = g('nsa_wcmp_k').transpose(1, 0, 2)
        wcv = g('nsa_wcmp_v').transpose(1, 0, 2)
        d2 = lambda a: np.concatenate([a, np.zeros_like(a)], 0)
        put(f'{l}wck', d2(wck))
        put(f'{l}wcks', d2(wck[:, :, sw64]))
        put(f'{l}wcv', d2(wcv))
        put(f'{l}wo', _pk(g('w_o')))
        put(f'{l}xq', _pk(g('xattn_wq'))); put(f'{l}xk', _pk(g('xattn_wk'))); put(f'{l}xv', _pk(g('xattn_wv')))
        put(f'{l}xo', _pk(g('xattn_wo')))
        putc(l, 'bAu', fm(bs_['u']))
        putc(l, 'bB', np.concatenate([fm(bs_['cb']), fm(bs_['cc']), fm(bs_['ch'])], 1))
        putc(l, 'cw', np.concatenate([fm(g('conv_w')[k]) for k in range(3)], 1))
        b2 = np.zeros((128, 2), np.float32); b2[64:96, 0] = bs_['krope']; b2[64:96, 1] = bs_['krope'][sw32]
        putc(l, 'bC2', b2)
        fm64 = lambda v: np.concatenate([np.ascontiguousarray(v.reshape(-1, 64).T), np.zeros((64, v.size // 64), np.float32)], 0)
        putc(l, 'bDq', np.concatenate([fm64(bs_['nq']), fm64(bs_['nq'][swq])], 1))
        putc(l, 'bDk', fm64(np.concatenate(kb)))
        putc(l, 'bDc', np.concatenate([fm64(bs_['nkc']), fm64(bs_['nvc'])], 1))
        putc(l, 'bG', np.concatenate([fm(bs_[n]) for n in ('ga', 'gb', 'gc', 'gd')], 1))
        putc(l, 'pek', d2(g('nsa_pe_k').T)); putc(l, 'pev', d2(g('nsa_pe_v').T))
        putr(l, 'bAv', bs_['v']); putr(l, 'glng', g('gmlp_ln_g')); putr(l, 'glnb', g('gmlp_ln_b'))
        putr(l, 'bs', g('gmlp_bs').reshape(-1))
        putr(l, 'bC1', np.concatenate([bs_['qlat'], bs_['kvlat']]))
        putr(l, 'qg', g('mla_qnorm_g')); putr(l, 'kvg', g('mla_kvnorm_g'))
        putr(l, 'bDv', np.concatenate([bs_['nvs'], bs_['nvw'], bs_['ngate']]))
        for i in (1, 2, 3, 4):
            putr(l, f'ln{i}g', g(f'ln{i}_g')); putr(l, f'ln{i}b', g(f'ln{i}_b'))
    for n, a in _consts_bf().items():
        put(n, a)
    return wb, col, row, _consts_f32()


class T:
    __slots__ = ('h', 'w', 'r')

    def __init__(s, h):
        s.h = h; s.w = None; s.r = {}

    def __getitem__(s, i):
        return s.h[i]


def cap(base, *dims):
    return bass.AP(tensor=base.tensor, offset=base.offset, ap=[list(base.ap[0])] + [list(d) for d in dims])


class K:
    def __init__(s, nc):
        s.nc = nc; s.es = ExitStack()
        s.eng = {'pe': nc.tensor, 'act': nc.scalar, 'dve': nc.vector, 'pool': nc.gpsimd, 'sp': nc.sync}
        s.esem = {n: s.es.enter_context(nc.semaphore('s_' + n)) for n in s.eng}
        s.cnt = {n: 0 for n in s.eng}; s.seen = {n: {} for n in s.eng}
        s.ND = 24
        s.dsem = [s.es.enter_context(nc.semaphore(f'd{i}')) for i in range(s.ND)]
        s.dval = [0] * s.ND; s.dnext = 0
        s.ps = [T(s.es.enter_context(nc.psum_tensor(f'ps{i}', [128, 512], F32))) for i in range(8)]
        s.uid = 0
        s.npe = 0; s.marks = []

    def alloc(s, es, shape, dt, name='t'):
        s.uid += 1
        return T(es.enter_context(s.nc.sbuf_tensor(f'{name}_{s.uid}', list(shape), dt)))

    def _need(s, e, ev, raw):
        kind, key, val = ev
        if kind == 'e' and key == e and (e == 'pe' or not raw):
            return
        kk = (kind, key)
        if s.seen[e].get(kk, 0) >= val:
            return
        s.seen[e][kk] = val
        s.eng[e].wait_ge(s.esem[key] if kind == 'e' else s.dsem[key], val)

    def deps(s, e, R, W):
        for t in R:
            if t.w is not None: s._need(e, t.w, True)
        for t in W:
            if t.w is not None: s._need(e, t.w, True)
            for ev in t.r.values(): s._need(e, ev, False)

    def op(s, e, fn, R=(), W=(), inc=True):
        s.deps(e, R, W)
        if e == 'pe': s.npe += 1
        ins = fn(s.eng[e])
        c = s.cnt[e] + 1
        if inc:
            ins.then_inc(s.esem[e], 1); s.cnt[e] = c
        ev = ('e', e, c)
        for t in R: t.r[e] = ev
        for t in W:
            t.w = ev; t.r = {}
        return ins

    def dma(s, out, in_, R=(), W=(), q='sp'):
        s.deps(q, R, W)
        i = s.dnext; s.dnext = (i + 1) % s.ND
        if s.dval[i] > 0: s._need(q, ('d', i, s.dval[i]), True)
        s.dval[i] += 16
        ev = ('d', i, s.dval[i])
        s.eng[q].dma_start(out=out, in_=in_).then_inc(s.dsem[i], 16)
        for t in R: t.r[('d', i)] = ev
        for t in W:
            t.w = ev; t.r = {}

    def barrier(s):
        for e in s.eng:
            for x in s.eng:
                if x != e and s.cnt[x] > 0: s._need(e, ('e', x, s.cnt[x]), True)
            for i in range(s.ND):
                if s.dval[i] > 0: s._need(e, ('d', i, s.dval[i]), True)

    def mm(s, pt, out, lhsT, rhs, R, start=True, stop=True, skip=False, inc=True):
        return s.op('pe', lambda e: e.matmul(out, lhsT=lhsT, rhs=rhs, start=start, stop=stop, skip_group_check=skip),
                    R=R, W=[pt], inc=inc)

    def tr(s, pt, out, in_, ident, R, inc=True):
        return s.op('pe', lambda e: e.transpose(out, in_, ident), R=R, W=[pt], inc=inc)

    def act(s, out, in_, func, R, W, bias=None, scale=None):
        kw = {}
        if bias is not None: kw['bias'] = bias
        if scale is not None: kw['scale'] = scale
        return s.op('act', lambda e: e.activation(out=out, in_=in_, func=func, **kw), R=R, W=W)

    def tt(s, e, out, in0, in1, op, R, W):
        return s.op(e, lambda g: g.tensor_tensor(out=out, in0=in0, in1=in1, op=op), R=R, W=W)

    def ts(s, e, out, in0, s1, s2, op0, op1, R, W):
        if op1 is None:
            return s.op(e, lambda g: g.tensor_scalar(out=out, in0=in0, scalar1=s1, scalar2=None, op0=op0), R=R, W=W)
        return s.op(e, lambda g: g.tensor_scalar(out=out, in0=in0, scalar1=s1, scalar2=s2, op0=op0, op1=op1), R=R, W=W)

    def stt(s, out, in0, sc, in1, op0, op1, R, W):
        return s.op('dve', lambda g: g.scalar_tensor_tensor(out=out, in0=in0, scalar=sc, in1=in1, op0=op0, op1=op1),
                    R=R, W=W)

    def cp(s, e, out, in_, R, W):
        if e == 'act':
            return s.op('act', lambda g: g.copy(out=out, in_=in_), R=R, W=W)
        return s.op(e, lambda g: g.tensor_copy(out=out, in_=in_), R=R, W=W)

    def memset(s, e, ap, v, W):
        return s.op(e, lambda g: g.memset(ap, v), W=W)


class U:
    __slots__ = ('S', 'X', 'P', 'sb', 'eb')

    def __init__(s, S, X, P):
        s.S = S; s.X = X; s.P = P; s.sb = None; s.eb = None


def run_items(k, items, sbanks, ebufs, ctr):
    def emitS(u):
        u.sb = k.ps[sbanks[ctr[0] % len(sbanks)]]; u.eb = ebufs[ctr[0] % len(ebufs)]; ctr[0] += 1
        u.S(u.sb)
    n = len(items)
    for i, it in enumerate(items):
        if isinstance(it, U):
            if it.sb is None: emitS(it)
            j = i + 1
            while j < n and (not isinstance(items[j], U)) and items[j][0] == 'soft': j += 1
            if j < n and isinstance(items[j], U) and items[j].sb is None: emitS(items[j])
            it.X(it.sb, it.eb); it.P(it.eb)
        else:
            it[1]()


class Prog:
    def __init__(s, nc, dbg=False, stop_after=None):
        s.nc = nc; s.k = K(nc); s.dbg = dbg; s.stop_after = stop_after; s.dq = []
        kind_s = "ExternalOutput" if dbg else "Internal"
        dt = nc.dram_tensor
        s.x_in = dt("x", [S, D], F32, kind="ExternalInput").ap()
        s.mem_in = dt("mem", [MEM, D], F32, kind="ExternalInput").ap()
        s.wbig = dt("wbig", [128, NWB], F32, kind="ExternalInput").ap()
        s.col = dt("col", [L, 128, NCOL], F32, kind="ExternalInput").ap()
        s.row = dt("row", [L, 1, NROW], F32, kind="ExternalInput").ap()
        s.cf = dt("cf", [128, NCF], F32, kind="ExternalInput").ap()
        s.out = dt("out", [S, D], F32, kind="ExternalOutput").ap()
        s.WB = dt("WB", [128, NWB], BF16, kind="Internal").ap()
        s.XR = dt("XR", [S, D], F32, kind=kind_s).ap()
        s.XT = dt("XT", [128, 8, S], BF16, kind=kind_s).ap()
        s.PA = dt("PA", [128, 4, S], BF16, kind=kind_s).ap()
        s.PB = dt("PB", [128, 4, S], BF16, kind=kind_s).ap()
        s.PC = dt("PC", [128, 4, S], BF16, kind=kind_s).ap()
        s.PD = dt("PD", [128, 4, S], BF16, kind=kind_s).ap()
        s.QM = dt("QM", [96, 8, S], BF16, kind="Internal").ap()
        s.KM = dt("KM", [96, 8, S], BF16, kind="Internal").ap()
        s.VM = dt("VM", [128, 8, NT, 65], BF16, kind="Internal").ap()
        s.QN = dt("QN", [64, 8, S], BF16, kind="Internal").ap()
        s.KK = dt("KK", [64, 4, S], BF16, kind="Internal").ap()
        s.KCV = dt("KCV", [64, 4, S], BF16, kind="Internal").ap()
        s.VSW = dt("VSW", [128, 4, NT, 65], BF16, kind="Internal").ap()
        s.GATE = dt("GATE", [128, NT, 24], F32, kind="Internal").ap()
        s.YP = dt("YP", [S, D], F32, kind="Internal").ap()
        s.MT = dt("MT", [128, 8, S], BF16, kind="Internal").ap()

    def wsl(s, name):
        o, sz, shp = WOFF[name]
        a = s.WB[:, o:o + sz]
        if len(shp) == 2:
            a = a.rearrange("p (a b) -> p a b", a=shp[0])
        return a

    def loadw(s, es, name):
        o, sz, shp = WOFF[name]
        t = s.k.alloc(es, (128,) + shp, BF16, 'w')
        s.k.dma(t[:], s.wsl(name), W=[t])
        return t

    def loadrow(s, es, l, name, n=None):
        o, sz, _ = ROFF[name]
        n = n or sz
        t = s.k.alloc(es, (128, n), F32, 'r')
        src = s.row[l, 0:1, o:o + n]
        s.k.dma(t[:], bass.AP(tensor=src.tensor, offset=src.offset, ap=[[0, 128], [1, n]]), W=[t])
        return t

    def loadcf(s, es, name, c0=0, n=None, parts=128):
        o, sz, _ = FOFF[name]
        n = n or sz
        t = s.k.alloc(es, (128, n), F32, 'c')
        s.k.dma(t[0:parts, :], s.cf[0:parts, o + c0:o + c0 + n], W=[t])
        return t

    def phase_end(s, name):
        s.k.barrier()
        s.k.marks.append((name, s.k.npe))
        return s.stop_after == name

    def prep(s):
        k = s.k
        CH = 4096
        end0 = WOFF['0f1w2'][0] + WOFF['0f1w2'][1]
        with ExitStack() as es:
            fb = [k.alloc(es, (128, CH), F32, 'pf') for _ in range(4)]
            bb = [k.alloc(es, (128, CH), BF16, 'pb') for _ in range(4)]
            engs = ['dve', 'act', 'dve', 'pool']
            n = 0
            for c0 in range(0, end0, CH):
                w = min(CH, end0 - c0)
                f = fb[n % 4]; b = bb[n % 4]
                k.dma(f[:, 0:w], s.wbig[:, c0:c0 + w], W=[f])
                k.cp(engs[n % 4], b[:, 0:w], f[:, 0:w], R=[f], W=[b])
                k.dma(s.WB[:, c0:c0 + w], b[:, 0:w], R=[b])
                n += 1
        s.k.barrier()
        s.BCH = 1024
        s.bgf = [k.alloc(k.es, (128, s.BCH), F32, 'bgf') for _ in range(4)]
        s.bgb = [k.alloc(k.es, (128, s.BCH), BF16, 'bgb') for _ in range(4)]
        s.bg_pos = end0; s.bg_n = 0

    def bg(s, n=1, act_ok=False):
        k = s.k
        for _ in range(n):
            if s.bg_pos >= NWB: return
            c0 = s.bg_pos; w = min(s.BCH, NWB - c0)
            f = s.bgf[s.bg_n % 4]; b = s.bgb[s.bg_n % 4]
            e = 'pool'
            s.bg_n += 1
            k.dma(f[:, 0:w], s.wbig[:, c0:c0 + w], W=[f])
            k.cp(e, b[:, 0:w], f[:, 0:w], R=[f], W=[b])
            k.dma(s.WB[:, c0:c0 + w], b[:, 0:w], R=[b], q=e)
            s.bg_pos += w

    def bg_until(s, col):
        while s.bg_pos < min(col, NWB):
            s.bg(1)

    def to_fm(s, es_bufs, src_t, src_ap_fn, n, dst_ap, psb, ident):
        k = s.k
        stg = es_bufs
        pv = k.ps[psb][:].bitcast(BF16)
        for j in range(n):
            k.tr(k.ps[psb], pv[:, j * 128:(j + 1) * 128], src_ap_fn(j), ident[:], R=[src_t, ident], inc=(j == n - 1))
        k.cp('act', stg[:, 0:n, :], pv[:, 0:n * 128].rearrange("p (a b) -> p a b", a=n), R=[], W=[stg, k.ps[psb]])
        k.dma(dst_ap, stg[:, 0:n, :], R=[stg])

    def ln_setup(s, es, l, i):
        k = s.k
        o = type('o', (), {})()
        o.g = s.loadrow(es, l, f'ln{i}g'); o.b = s.loadrow(es, l, f'ln{i}b')
        o.st = [k.alloc(es, (128, 2, 6), F32, 'st') for _ in range(4)]
        o.mv = [k.alloc(es, (128, 2), F32, 'mv') for _ in range(4)]
        o.rs = [k.alloc(es, (128, 2), F32, 'rs') for _ in range(4)]
        o.xb = [k.alloc(es, (128, D), BF16, 'xb') for _ in range(2)]
        o.stg = [k.alloc(es, (128, 8, 128), BF16, 'sg') for _ in range(2)]
        o.mh = k.alloc(es, (128, 1), F32, 'mh')
        k.memset('dve', o.mh[:], -0.5, W=[o.mh])
        o.n = 0
        return o

    def ln_tile(s, o, pre, tt, dst, psb, ident, write_xt=True):
        k = s.k
        i = o.n % 2; i4 = o.n % 4; o.n += 1
        st, mv, rs, xb, stg = o.st[i4], o.mv[i4], o.rs[i4], o.xb[i], o.stg[i]
        for hh in range(2):
            k.op('dve', lambda g, hh=hh: g.bn_stats(out=st[:, hh, :], in_=pre[:, hh * 512:(hh + 1) * 512]), R=[pre], W=[st])
        k.op('dve', lambda g: g.bn_aggr(out=mv[:], in_=st[:].rearrange("p a b -> p (a b)")), R=[st], W=[mv])
        k.ts('dve', rs[:, 0:1], mv[:, 1:2], LN_EPS, None, ALU.add, None, R=[mv], W=[rs])
        k.tt('pool', rs[:, 1:2], rs[:, 0:1], o.mh[:], ALU.pow, R=[rs, o.mh], W=[rs])
        k.ts('dve', pre[:], pre[:], mv[:, 0:1], rs[:, 1:2], ALU.subtract, ALU.mult, R=[pre, mv, rs], W=[pre])
        k.tt('pool', pre[:], pre[:], o.g[:], ALU.mult, R=[pre, o.g], W=[pre])
        k.tt('pool', pre[:], pre[:], o.b[:], ALU.add, R=[pre, o.b], W=[pre])
        k.dma(dst[tt * 128:(tt + 1) * 128, :], pre[:], R=[pre])
        if write_xt:
            s.flush(1)
            k.cp('act', xb[:], pre[:], R=[pre], W=[xb])
            s.dq.append(lambda: s.to_fm(stg, xb, lambda j: xb[:, j * 128:(j + 1) * 128], 8,
                                        s.XT[:, :, tt * 128:(tt + 1) * 128], psb, ident))

    def flush(s, keep=0):
        while len(s.dq) > keep:
            s.dq.pop(0)()

    def xt0(s):
        k = s.k
        with ExitStack() as es:
            ident = s.loadw(es, 'ident')
            xf = [k.alloc(es, (128, D), F32, 'xf') for _ in range(2)]
            xb = [k.alloc(es, (128, D), BF16, 'xb') for _ in range(2)]
            stg = [k.alloc(es, (128, 8, 128), BF16, 'sg') for _ in range(2)]
            for tt in range(NT):
                f = xf[tt % 2]; b = xb[tt % 2]
                k.dma(f[:], s.x_in[tt * 128:(tt + 1) * 128, :], W=[f])
                k.cp('dve', b[:], f[:], R=[f], W=[b])
                s.to_fm(stg[tt % 2], b, lambda j, b=b: b[:, j * 128:(j + 1) * 128], 8,
                        s.XT[:, :, tt * 128:(tt + 1) * 128], tt % 2, ident)
        s.k.barrier()

    def ffn(s, l, f, lni, src, dst, write_xt=True, bgn=0):
        k = s.k
        GT = 256; NG = S // GT; HF = FF // 2; NFC = 11
        for half in range(2):
            with ExitStack() as es:
                w1 = k.alloc(es, (128, 8, HF), BF16, 'w1'); w3 = k.alloc(es, (128, 8, HF), BF16, 'w3')
                w2 = k.alloc(es, (128, NFC, D), BF16, 'w2')
                k.dma(w1[:], s.wsl(f'{l}{f}w1')[:, :, half * HF:(half + 1) * HF], W=[w1])
                k.dma(w3[:], s.wsl(f'{l}{f}w3')[:, :, half * HF:(half + 1) * HF], W=[w3])
                k.dma(w2[:], s.wsl(f'{l}{f}w2')[:, half * NFC:(half + 1) * NFC, :], W=[w2])
                ident = s.loadw(es, 'ident')
                ln = s.ln_setup(es, l, lni) if half == 1 else None
                xT = [k.alloc(es, (128, 8, GT), BF16, 'xT') for _ in range(3)]
                gT = [k.alloc(es, (128, NFC, GT), BF16, 'gT') for _ in range(2)]
                sl = [k.alloc(es, (128, GT), F32, 'sl') for _ in range(2)]
                xr = [k.alloc(es, (128, D), F32, 'xr') for _ in range(4)]
                pre = [k.alloc(es, (128, D), F32, 'pre') for _ in range(4)]

                def LDT(gi):
                    k.dma(xT[gi % 3][:], s.XT[:, :, gi * GT:(gi + 1) * GT], W=[xT[gi % 3]])

                def LDX(gi):
                    for t2 in range(2):
                        tt = gi * 2 + t2
                        sr = src if half == 0 else s.YP
                        k.dma(xr[tt % 4][:], sr[tt * 128:(tt + 1) * 128, :], W=[xr[tt % 4]])

                def H(gi):
                    x = xT[gi % 3]; g = gT[gi % 2]
                    for fc in range(NFC):
                        p1 = k.ps[(2 * fc) % 3]; p3 = k.ps[(2 * fc + 1) % 3]
                        for kk in range(8):
                            k.mm(p1, p1[:, 0:GT], w1[:, kk, fc * 128:(fc + 1) * 128], x[:, kk, :], R=[w1, x],
                                 start=(kk == 0), stop=(kk == 7), inc=(kk == 7))
                        for kk in range(8):
                            k.mm(p3, p3[:, 0:GT], w3[:, kk, fc * 128:(fc + 1) * 128], x[:, kk, :], R=[w3, x],
                                 start=(kk == 0), stop=(kk == 7), inc=(kk == 7))
                        sb = sl[fc % 2]
                        k.act(sb[:], p1[:, 0:GT], AF.Silu, R=[], W=[sb, p1])
                        k.tt('dve', g[:, fc, :], sb[:], p3[:, 0:GT], ALU.mult, R=[sb], W=[g, p3])

                def Y(gi):
                    g = gT[gi % 2]
                    for t2 in range(2):
                        tt = gi * 2 + t2
                        xx = xr[tt % 4]; pp = pre[tt % 4]
                        for dh in range(2):
                            pb = k.ps[4 + t2 * 2 + dh]
                            for fc in range(NFC):
                                k.mm(pb, pb[:, :], g[:, fc, t2 * 128:(t2 + 1) * 128], w2[:, fc, dh * 512:(dh + 1) * 512],
                                     R=[g, w2], start=(fc == 0), stop=(fc == NFC - 1), inc=(fc == NFC - 1))
                            if half == 0:
                                k.act(pp[:, dh * 512:(dh + 1) * 512], pb[:, :], AF.Copy, R=[], W=[pp, pb], scale=0.5)
                            else:
                                k.stt(pp[:, dh * 512:(dh + 1) * 512], pb[:, :], 0.5, xx[:, dh * 512:(dh + 1) * 512], ALU.mult, ALU.add,
                                      R=[xx], W=[pp, pb])
                        if half == 0:
                            k.stt(pp[:], xx[:], ALPHA, pp[:], ALU.mult, ALU.add, R=[xx, pp], W=[pp])
                            k.dma(s.YP[tt * 128:(tt + 1) * 128, :], pp[:], R=[pp])
                        else:
                            s.ln_tile(ln, pp, tt, dst, 3, ident, write_xt)

                LDT(0); LDT(1); LDX(0)
                H(0)
                for gi in range(NG):
                    if gi + 2 < NG: LDT(gi + 2)
                    if gi + 1 < NG: LDX(gi + 1)
                    s.bg((bgn * 3 + 1) // 2 if half == 0 else bgn // 2)
                    if gi + 1 < NG: H(gi + 1)
                    Y(gi)
                s.flush(0)
                if half == 1 and getattr(s, '_bg_until_col', None):
                    s.bg_until(s._bg_until_col)
            s.k.barrier()
        return s.phase_end(f'{l}{f}')

    def mixA(s, l):
        k = s.k
        with ExitStack() as es:
            wA = s.loadw(es, f'{l}wA'); ws = s.loadw(es, f'{l}WsT'); tril = s.loadw(es, 'tril')
            bv = s.loadrow(es, l, 'bAv'); lg = s.loadrow(es, l, 'glng'); lb = s.loadrow(es, l, 'glnb')
            bsb = s.loadrow(es, l, 'bs')
            colt = k.alloc(es, (128, NCOL), F32, 'col'); k.dma(colt[:], s.col[l], W=[colt])
            cA = COFF['bAu'][0]
            wsm = k.alloc(es, (128, 4, 128), BF16, 'wsm')
            k.tt('dve', wsm[:], ws[:], cap(tril[:, 0:1], [0, 4], [1, 128]), ALU.mult, R=[ws, tril], W=[wsm])
            mh = k.alloc(es, (128, 1), F32, 'mh'); k.memset('dve', mh[:], -0.5, W=[mh])
            hT = [k.alloc(es, (128, 8, 512), BF16, 'hT') for _ in range(2)]
            uT = [k.alloc(es, (128, 4, 512), F32, 'uT') for _ in range(2)]
            pa = [k.alloc(es, (128, 4, 512), BF16, 'pa') for _ in range(2)]
            vb = [k.alloc(es, (128, 512), F32, 'vb') for _ in range(2)]
            vn = [k.alloc(es, (128, 512), F32, 'vn') for _ in range(2)]
            vl = [k.alloc(es, (128, 512), BF16, 'vl') for _ in range(2)]
            t1 = [k.alloc(es, (128, 512), F32, 't1') for _ in range(2)]
            st = [k.alloc(es, (128, 6), F32, 'st') for _ in range(2)]
            mv = [k.alloc(es, (128, 2), F32, 'mv') for _ in range(2)]
            rs = [k.alloc(es, (128, 2), F32, 'rs') for _ in range(2)]

            def load(G):
                h = hT[G % 2]
                k.dma(h[:], s.XT[:, :, G * 512:(G + 1) * 512], W=[h])

            def Uproj(G):
                h = hT[G % 2]; u = uT[G % 2]
                for fc in range(4):
                    pb = k.ps[fc % 2]
                    for kk in range(8):
                        k.mm(pb, pb[:, :], wA[:, kk, fc * 128:(fc + 1) * 128], h[:, kk, :], R=[wA, h],
                             start=(kk == 0), stop=(kk == 7), inc=(kk == 7))
                    k.act(u[:, fc, :], pb[:, :], AF.Identity, R=[colt], W=[u, pb], bias=colt[:, cA + fc:cA + fc + 1])

            def V(cc):
                G, c = divmod(cc, 4); i = cc % 2
                h = hT[G % 2]
                pv = k.ps[2 + i]
                for kk in range(8):
                    k.mm(pv, pv[:, :], h[:, kk, c * 128:(c + 1) * 128], wA[:, kk, 512:1024], R=[wA, h],
                         start=(kk == 0), stop=(kk == 7), inc=(kk == 7))
                k.tt('dve', vb[i][:], pv[:, :], bv[:], ALU.add, R=[bv], W=[vb[i], pv])
                k.op('dve', lambda g: g.bn_stats(out=st[i][:], in_=vb[i][:]), R=[vb[i]], W=[st[i]])
                k.op('dve', lambda g: g.bn_aggr(out=mv[i][:], in_=st[i][:]), R=[st[i]], W=[mv[i]])
                k.ts('dve', rs[i][:, 0:1], mv[i][:, 1:2], LN_EPS, None, ALU.add, None, R=[mv[i]], W=[rs[i]])
                k.tt('pool', rs[i][:, 1:2], rs[i][:, 0:1], mh[:], ALU.pow, R=[rs[i], mh], W=[rs[i]])
                k.ts('dve', vn[i][:], vb[i][:], mv[i][:, 0:1], rs[i][:, 1:2], ALU.subtract, ALU.mult,
                     R=[vb[i], mv[i], rs[i]], W=[vn[i]])
                k.tt('pool', vn[i][:], vn[i][:], lg[:], ALU.mult, R=[vn[i], lg], W=[vn[i]])
                k.tt('pool', vl[i][:], vn[i][:], lb[:], ALU.add, R=[vn[i], lb], W=[vl[i]])

            def S2(cc):
                G, c = divmod(cc, 4); i = cc % 2
                u = uT[G % 2]; p = pa[G % 2]
                p2 = k.ps[4 + i]
                for g in range(4):
                    k.mm(p2, p2[:, g * 128:(g + 1) * 128], vl[i][:, g * 128:(g + 1) * 128], wsm[:, g, :],
                         R=[vl[i], wsm], start=True, stop=True, inc=(g == 3))
                k.tt('dve', t1[i][:], p2[:, :], bsb[:], ALU.add, R=[bsb], W=[t1[i], p2])
                k.tt('pool', p[:, :, c * 128:(c + 1) * 128], t1[i][:].rearrange("p (a b) -> p a b", a=4),
                     u[:, :, c * 128:(c + 1) * 128], ALU.mult, R=[t1[i], u], W=[p])
                if c == 3:
                    k.dma(s.PA[:, :, G * 512:(G + 1) * 512], p[:], R=[p])

            load(0); load(1); Uproj(0); V(0)
            for cc in range(32):
                G, c = divmod(cc, 4)
                if cc + 1 < 32:
                    if (cc + 1) % 4 == 0: Uproj((cc + 1) // 4)
                    V(cc + 1)
                S2(cc)
                if c == 3 and G + 2 < 8: load(G + 2)
        return s.phase_end(f'{l}A')

    def mixB(s, l):
        k = s.k
        with ExitStack() as es:
            wB = s.loadw(es, f'{l}wB')
            colt = k.alloc(es, (128, NCOL), F32, 'col'); k.dma(colt[:], s.col[l], W=[colt])
            cb0 = COFF['bB'][0]; cw0 = COFF['cw'][0]
            P = k.alloc(es, (128, 4, 514), F32, 'P')
            k.memset('dve', P[:, :, 0:2], 0.0, W=[P])
            hT = [k.alloc(es, (128, 8, 512), BF16, 'hT') for _ in range(2)]
            pbuf = [k.alloc(es, (128, 4, 512), BF16, 'pb') for _ in range(2)]
            ccs = [k.alloc(es, (128, 512), F32, 'cc') for _ in range(2)]
            y1 = [k.alloc(es, (128, 512), F32, 'y1') for _ in range(2)]
            y2 = [k.alloc(es, (128, 512), F32, 'y2') for _ in range(2)]

            def load(G):
                k.dma(hT[G % 2][:], s.XT[:, :, G * 512:(G + 1) * 512], W=[hT[G % 2]])
            load(0)
            n = 0
            for G in range(8):
                if G + 1 < 8: load(G + 1)
                h = hT[G % 2]; pb = pbuf[G % 2]
                for fc in range(4):
                    i = n % 2; n += 1
                    pa_, pb_, pc_ = k.ps[i], k.ps[2 + i], k.ps[4 + i]
                    for which, pp in ((1, pa_), (2, pb_), (0, pc_)):
                        for kk in range(8):
                            c0 = which * 512 + fc * 128
                            k.mm(pp, pp[:, :], wB[:, kk, c0:c0 + 128], h[:, kk, :], R=[wB, h],
                                 start=(kk == 0), stop=(kk == 7), inc=(kk == 7))
                    bcol = lambda which: colt[:, cb0 + which * 4 + fc:cb0 + which * 4 + fc + 1]
                    wcol = lambda kk: colt[:, cw0 + kk * 4 + fc:cw0 + kk * 4 + fc + 1]
                    k.act(ccs[i][:], pa_[:, :], AF.Identity, R=[colt], W=[ccs[i], pa_], bias=bcol(1))
                    k.stt(P[:, fc, 2:514], pb_[:, :], bcol(2), ccs[i][:], ALU.add, ALU.mult, R=[ccs[i], colt], W=[P, pb_])
                    k.ts('dve', y1[i][:], P[:, fc, 2:514], wcol(2), None, ALU.mult, None, R=[P, colt], W=[y1[i]])
                    k.stt(y2[i][:], P[:, fc, 1:513], wcol(1), y1[i][:], ALU.mult, ALU.add, R=[P, y1[i], colt], W=[y2[i]])
                    k.stt(y1[i][:], P[:, fc, 0:512], wcol(0), y2[i][:], ALU.mult, ALU.add, R=[P, y2[i], colt], W=[y1[i]])
                    k.stt(pb[:, fc, :], pc_[:, :], bcol(0), y1[i][:], ALU.add, ALU.mult, R=[y1[i], colt], W=[pb, pc_])
                    k.cp('pool', P[:, fc, 0:2], P[:, fc, 512:514], R=[], W=[P])
                k.dma(s.PB[:, :, G * 512:(G + 1) * 512], pb[:], R=[pb])
        return s.phase_end(f'{l}B')

    def mixC1(s, l):
        k = s.k
        with ExitStack() as es:
            wC1 = s.loadw(es, f'{l}wC1'); wC2 = s.loadw(es, f'{l}wC2')
            wuq = s.loadw(es, f'{l}wuq'); wuqs = s.loadw(es, f'{l}wuqs'); wukv = s.loadw(es, f'{l}wukv')
            ident = s.loadw(es, 'ident')
            bC1 = s.loadrow(es, l, 'bC1'); qg = s.loadrow(es, l, 'qg'); kvg = s.loadrow(es, l, 'kvg')
            colt = k.alloc(es, (128, NCOL), F32, 'col'); k.dma(colt[:], s.col[l], W=[colt])
            c2 = COFF['bC2'][0]
            mh = k.alloc(es, (128, 1), F32, 'mh'); k.memset('dve', mh[:], -0.5, W=[mh])
            hT = [k.alloc(es, (128, 8, 512), BF16, 'hT') for _ in range(2)]
            Ct = [k.alloc(es, (128, 512), F32, 'Ct') for _ in range(2)]
            St = [k.alloc(es, (128, 512), F32, 'St') for _ in range(2)]
            zb = [k.alloc(es, (128, 384), F32, 'zb') for _ in range(2)]
            st = [k.alloc(es, (128, 2, 6), F32, 'st') for _ in range(2)]
            mv = [k.alloc(es, (128, 4), F32, 'mv') for _ in range(2)]
            rs = [k.alloc(es, (128, 4), F32, 'rs') for _ in range(2)]
            lat = [k.alloc(es, (128, 384), BF16, 'lat') for _ in range(2)]
            latT = [k.alloc(es, (128, 3, 512), BF16, 'latT') for _ in range(2)]
            qT = [k.alloc(es, (128, 8, 512), BF16, 'qT') for _ in range(2)]
            kT = [k.alloc(es, (128, 8, 512), BF16, 'kT') for _ in range(2)]
            va = [k.alloc(es, (128, 8, 4, 65), BF16, 'va') for _ in range(2)]
            ta = [k.alloc(es, (128, 512), F32, 'ta') for _ in range(2)]
            tb = [k.alloc(es, (128, 512), F32, 'tb') for _ in range(2)]
            kpe = k.alloc(es, (128, 512), BF16, 'kpe')
            for v in va:
                k.memset('pool', v[:, :, :, 64:65], 1.0, W=[v])
            o96 = FOFF['C96'][0]; s96 = FOFF['S96'][0]

            def load(G):
                i = G % 2
                k.dma(hT[i][:], s.XT[:, :, G * 512:(G + 1) * 512], W=[hT[i]])
                k.dma(Ct[i][0:96, :], s.cf[0:96, o96 + G * 512:o96 + (G + 1) * 512], W=[Ct[i]])
                k.dma(St[i][0:96, :], s.cf[0:96, s96 + G * 512:s96 + (G + 1) * 512], W=[St[i]])
            def Z(cc):
                G, c = divmod(cc, 4); i = cc % 2
                h = hT[G % 2]
                pz = k.ps[i]
                for kk in range(8):
                    k.mm(pz, pz[:, 0:384], h[:, kk, c * 128:(c + 1) * 128], wC1[:, kk, :], R=[h, wC1],
                         start=(kk == 0), stop=(kk == 7), inc=(kk == 7))
                k.tt('dve', zb[i][:], pz[:, 0:384], bC1[:], ALU.add, R=[bC1], W=[zb[i], pz])
                k.op('dve', lambda g, i=i: g.bn_stats(out=st[i][:, 0, :], in_=zb[i][:, 0:256]), R=[zb[i]], W=[st[i]])
                k.op('dve', lambda g, i=i: g.bn_stats(out=st[i][:, 1, :], in_=zb[i][:, 256:384]), R=[zb[i]], W=[st[i]])
                k.op('dve', lambda g, i=i: g.bn_aggr(out=mv[i][:, 0:2], in_=st[i][:, 0, :]), R=[st[i]], W=[mv[i]])
                k.op('dve', lambda g, i=i: g.bn_aggr(out=mv[i][:, 2:4], in_=st[i][:, 1, :]), R=[st[i]], W=[mv[i]])
                for j in range(2):
                    k.stt(rs[i][:, j:j + 1], mv[i][:, 2 * j:2 * j + 1], mv[i][:, 2 * j:2 * j + 1], mv[i][:, 2 * j + 1:2 * j + 2],
                          ALU.mult, ALU.add, R=[mv[i]], W=[rs[i]])
                k.ts('dve', rs[i][:, 2:4], rs[i][:, 0:2], RMS_EPS, None, ALU.add, None, R=[rs[i]], W=[rs[i]])
                k.tt('pool', rs[i][:, 0:2], rs[i][:, 2:4], cap(mh[:, 0:1], [0, 2]), ALU.pow, R=[rs[i], mh], W=[rs[i]])
                k.stt(lat[i][:, 0:256], zb[i][:, 0:256], rs[i][:, 0:1], qg[:], ALU.mult, ALU.mult, R=[zb[i], rs[i], qg], W=[lat[i]])
                k.stt(lat[i][:, 256:384], zb[i][:, 256:384], rs[i][:, 1:2], kvg[:], ALU.mult, ALU.mult, R=[zb[i], rs[i], kvg], W=[lat[i]])

            def Tt(cc):
                G, c = divmod(cc, 4); i = cc % 2
                lt = latT[G % 2]; v = va[G % 2]
                pt = k.ps[2 + i]; ptv = pt[:].bitcast(BF16)
                for j in range(3):
                    k.tr(pt, ptv[:, j * 128:(j + 1) * 128], lat[i][:, j * 128:(j + 1) * 128], ident[:], R=[lat[i], ident], inc=(j == 2))
                k.cp('act', lt[:, :, c * 128:(c + 1) * 128], ptv[:, 0:384].rearrange("p (a b) -> p a b", a=3), R=[], W=[lt, pt])
                pv = k.ps[4 + i]
                vcols = cap(wukv[:, 64:65], [128, 8], [1, 64])
                k.mm(pv, pv[:, :].rearrange("p (a b) -> p a b", a=8), lt[:, 2, c * 128:(c + 1) * 128], vcols, R=[lt, wukv])
                k.cp('act', v[:, :, c, 0:64], pv[:, :].rearrange("p (a b) -> p a b", a=8), R=[], W=[v, pv])

            def UP(G):
                gi = G % 2
                h = hT[gi]; C = Ct[gi]; Sn = St[gi]; lt = latT[gi]; q = qT[gi]; kt_ = kT[gi]; v = va[gi]
                pe1 = k.ps[6]; pe2 = k.ps[7]
                for kk in range(8):
                    k.mm(pe1, pe1[64:96, :], wC2[:, kk, 0:32], h[:, kk, :], R=[wC2, h], start=(kk == 0), stop=(kk == 7), inc=(kk == 7))
                for kk in range(8):
                    k.mm(pe2, pe2[64:96, :], wC2[:, kk, 32:64], h[:, kk, :], R=[wC2, h], start=(kk == 0), stop=(kk == 7), inc=(kk == 7))
                k.stt(ta[0][64:96, :], pe1[64:96, :], colt[64:96, c2:c2 + 1], C[64:96, :], ALU.add, ALU.mult, R=[colt, C], W=[ta[0], pe1])
                k.stt(tb[0][64:96, :], pe2[64:96, :], colt[64:96, c2 + 1:c2 + 2], Sn[64:96, :], ALU.add, ALU.mult, R=[colt, Sn], W=[tb[0], pe2])
                k.tt('pool', kpe[64:96, :], ta[0][64:96, :], tb[0][64:96, :], ALU.add, R=[ta[0], tb[0]], W=[kpe])
                k.cp('pool', kt_[64:96, :, :], cap(kpe[64:96, 0:1], [0, 8], [1, 512]), R=[kpe], W=[kt_])
                for hh in range(8):
                    i = hh % 2
                    px = k.ps[i]; py = k.ps[2 + i]; pk_ = k.ps[4 + i]
                    for kc in range(2):
                        k.mm(px, px[0:96, :], wuq[:, kc, hh * 96:(hh + 1) * 96], lt[:, kc, :], R=[wuq, lt], start=(kc == 0), stop=(kc == 1), inc=(kc == 1))
                    for kc in range(2):
                        k.mm(py, py[0:96, :], wuqs[:, kc, hh * 96:(hh + 1) * 96], lt[:, kc, :], R=[wuqs, lt], start=(kc == 0), stop=(kc == 1), inc=(kc == 1))
                    k.tt('dve', ta[1][0:96, :], px[0:96, :], C[0:96, :], ALU.mult, R=[C], W=[ta[1], px])
                    k.tt('dve', tb[1][0:96, :], py[0:96, :], Sn[0:96, :], ALU.mult, R=[Sn], W=[tb[1], py])
                    k.tt('pool', q[0:96, hh, :], ta[1][0:96, :], tb[1][0:96, :], ALU.add, R=[ta[1], tb[1]], W=[q])
                    k.mm(pk_, pk_[0:64, :], wukv[:, hh * 128:hh * 128 + 64], lt[:, 2, :], R=[wukv, lt])
                    k.cp('act', kt_[0:64, hh, :], pk_[0:64, :], R=[], W=[kt_, pk_])
                k.dma(s.QM[:, :, G * 512:(G + 1) * 512], q[0:96, :, :], R=[q])
                k.dma(s.KM[:, :, G * 512:(G + 1) * 512], kt_[0:96, :, :], R=[kt_])
                k.dma(s.VM[:, :, G * 4:(G + 1) * 4, :], v[:], R=[v])

            load(0); load(1); Z(0)
            for cc in range(32):
                G, c = divmod(cc, 4)
                if cc + 1 < 32: Z(cc + 1)
                Tt(cc)
                if c == 3:
                    UP(G)
                    if G + 2 < 8: load(G + 2)
        return s.phase_end(f'{l}C1')

    def otm_to_fm(s, es, otm, dst, ident, psb0):
        k = s.k
        stg = [k.alloc(es, (128, 4, 512), BF16, 'osg') for _ in range(2)]
        for qg_ in range(8):
            sg = stg[qg_ % 2]
            for q4 in range(4):
                qt = qg_ * 4 + q4
                pt = k.ps[psb0 + qt % 2]; ptv = pt[:].bitcast(BF16)
                for j in range(4):
                    k.tr(pt, ptv[:, j * 128:(j + 1) * 128], otm[:, qt, j * 128:(j + 1) * 128], ident[:], R=[otm, ident], inc=(j == 3))
                k.cp('act' if qt % 2 else 'dve', sg[:, :, q4 * 128:(q4 + 1) * 128], ptv[:, 0:512].rearrange("p (a b) -> p a b", a=4), R=[], W=[sg, pt])
            k.dma(dst[:, :, qg_ * 512:(qg_ + 1) * 512], sg[:], R=[sg])

    def fm_tasks(s, stg, otm, otm_t, dst, ident, psb0, j):
        k = s.k
        tasks = []
        for qg_ in range(8):
            def task(qg_=qg_):
                sg = stg[qg_ % 2]
                pt = k.ps[psb0 + qg_ % 2]; ptv = pt[:].bitcast(BF16)
                for q4 in range(4):
                    qt = qg_ * 4 + q4
                    k.tr(pt, ptv[:, q4 * 128:(q4 + 1) * 128], otm[:, qt, j * 128:(j + 1) * 128], ident[:], R=[otm_t, ident], inc=(q4 == 3))
                k.cp('dve', sg[:, 0, :], ptv[:, 0:512], R=[], W=[sg, pt])
                k.dma(dst[:, j, qg_ * 512:(qg_ + 1) * 512], sg[:, 0, :], R=[sg])
            tasks.append(task)
        return tasks

    def mixC2(s, l, bgn=0):
        k = s.k
        sc = float(96 ** -0.5)
        with ExitStack() as es:
            ident = s.loadw(es, 'ident'); negc = s.loadw(es, 'negc')
            otm = k.alloc(es, (128, NT, 512), BF16, 'otm')
            kT = [k.alloc(es, (128, S), BF16, 'kT') for _ in range(2)]
            qT = [k.alloc(es, (128, S), BF16, 'qT') for _ in range(2)]
            V = [k.alloc(es, (128, NT, 65), BF16, 'V') for _ in range(2)]
            E = [k.alloc(es, (128, 512), BF16, 'E') for _ in range(3)]
            rc = [k.alloc(es, (128, 4), F32, 'rc') for _ in range(2)]
            stg = [k.alloc(es, (128, 4, 512), BF16, 'osg') for _ in range(2)]
            otm_t = [T(otm.h) for _ in range(4)]
            pend = []

            for t_ in kT + qT:
                k.memset('pool', t_[:, :], 0.0, W=[t_])

            def load(hh):
                i = hh % 2
                k.dma(kT[i][0:96, :], s.KM[:, hh, :], W=[kT[i]])
                k.dma(qT[i][0:96, :], s.QM[:, hh, :], W=[qT[i]])
                k.dma(V[i][:], s.VM[:, hh, :, :], W=[V[i]])
            load(0)
            no = 0; ctr = [0]
            for hh in range(8):
                if hh + 1 < 8: load(hh + 1)
                kt_, q, v = kT[hh % 2], qT[hh % 2], V[hh % 2]
                items = []
                for G in range(8):
                    O = k.ps[4 + no % 2]; r = rc[no % 2]; no += 1
                    items.append(('soft', lambda O=O: (s.bg(bgn), k.memset('dve', O[:, 0:260], 0.0, W=[O]))))
                    for kt in range(4 * G + 4):
                        q0 = max(kt * 128, G * 512) - G * 512
                        nc_ = 512 - q0
                        diag = kt >= 4 * G

                        def fS(Sb, kt=kt, q0=q0, nc_=nc_, diag=diag, G=G):
                            k.mm(Sb, Sb[:, 0:nc_], kt_[:, kt * 128:(kt + 1) * 128], q[:, G * 512 + q0:(G + 1) * 512],
                                 R=[kt_, q], start=True, stop=(not diag), inc=(not diag))
                            if diag:
                                k.mm(Sb, Sb[:, 0:128], ident[:], negc[:], R=[ident, negc], start=False, stop=True)

                        def fX(Sb, Eb, nc_=nc_):
                            k.act(Eb[:, 0:nc_], Sb[:, 0:nc_], AF.Exp, R=[], W=[Eb, Sb], scale=sc)

                        def fP(Eb, kt=kt, q0=q0, O=O):
                            for qs in range(q0 // 128, 4):
                                k.mm(O, O[:, qs * 65:(qs + 1) * 65], Eb[:, (qs * 128 - q0):(qs * 128 - q0) + 128], v[:, kt, :],
                                     R=[Eb, v], start=False, stop=True, skip=True, inc=(qs == 3))
                        items.append(U(fS, fX, fP))

                    def fin(O=O, r=r, G=G, hh=hh):
                        Ov = O[:, 0:260].rearrange("p (a b) -> p a b", a=4)
                        k.op('dve', lambda g: g.reciprocal(out=r[:], in_=Ov[:, :, 64]), R=[], W=[r, O])
                        k.tt('dve', otm[:, G * 4:(G + 1) * 4, hh * 64:(hh + 1) * 64], Ov[:, :, 0:64],
                             cap(r[:, 0:1], [1, 4], [0, 64]), ALU.mult, R=[r], W=[otm_t[hh // 2], O])
                    items.append(('soft', fin))
                    if pend:
                        items.append(('soft', pend.pop(0)))
                run_items(k, items, [0, 1, 2], E, ctr)
                if hh % 2 == 1:
                    pend += s.fm_tasks(stg, otm, otm_t[hh // 2], s.PC, ident, 6, hh // 2)
            for t_ in pend:
                t_()
        return s.phase_end(f'{l}C2')

    def mixD1(s, l):
        k = s.k
        with ExitStack() as es:
            wDq = s.loadw(es, f'{l}wDq'); wDk = s.loadw(es, f'{l}wDk'); wDc = s.loadw(es, f'{l}wDc'); wDv = s.loadw(es, f'{l}wDv')
            bDv = s.loadrow(es, l, 'bDv')
            colt = k.alloc(es, (128, NCOL), F32, 'col'); k.dma(colt[:], s.col[l], W=[colt])
            cq = COFF['bDq'][0]; ck = COFF['bDk'][0]; cc_ = COFF['bDc'][0]
            hT = [k.alloc(es, (128, 8, 512), BF16, 'hT') for _ in range(2)]
            Ct = [k.alloc(es, (128, 512), F32, 'Ct') for _ in range(2)]
            St = [k.alloc(es, (128, 512), F32, 'St') for _ in range(2)]
            qn = [k.alloc(es, (128, 8, 512), BF16, 'qn') for _ in range(2)]
            kk_ = [k.alloc(es, (128, 4, 512), BF16, 'kk') for _ in range(2)]
            kcv = [k.alloc(es, (128, 4, 512), BF16, 'kcv') for _ in range(2)]
            vsw = [k.alloc(es, (128, 4, 4, 65), BF16, 'vsw') for _ in range(2)]
            gate = [k.alloc(es, (128, 4, 24), F32, 'gate') for _ in range(2)]
            ta = [k.alloc(es, (128, 512), F32, 'ta') for _ in range(2)]
            tb = [k.alloc(es, (128, 512), F32, 'tb') for _ in range(2)]
            vb = [k.alloc(es, (128, 280), F32, 'vb') for _ in range(2)]
            for v in vsw:
                k.memset('pool', v[:, :, :, 64:65], 1.0, W=[v])
            o64 = FOFF['C64'][0]; s64 = FOFF['S64'][0]

            def load(G):
                i = G % 2
                k.dma(hT[i][:], s.XT[:, :, G * 512:(G + 1) * 512], W=[hT[i]])
                k.dma(Ct[i][0:64, :], s.cf[0:64, o64 + G * 512:o64 + (G + 1) * 512], W=[Ct[i]])
                k.dma(St[i][0:64, :], s.cf[0:64, s64 + G * 512:s64 + (G + 1) * 512], W=[St[i]])
            load(0)
            n = 0
            for G in range(8):
                if G + 1 < 8: load(G + 1)
                gi = G % 2
                h = hT[gi]; C = Ct[gi]; Sn = St[gi]

                def roped(w, x0, s0, bx, bs_, dst_t, dst_ap):
                    nonlocal n
                    i = n % 2; n += 1
                    px = k.ps[i]; py = k.ps[2 + i]
                    for kk in range(8):
                        k.mm(px, px[0:64, :], w[:, kk, x0:x0 + 64], h[:, kk, :], R=[w, h], start=(kk == 0), stop=(kk == 7), inc=(kk == 7))
                    for kk in range(8):
                        k.mm(py, py[0:64, :], w[:, kk, s0:s0 + 64], h[:, kk, :], R=[w, h], start=(kk == 0), stop=(kk == 7), inc=(kk == 7))
                    k.stt(ta[i][0:64, :], px[0:64, :], colt[0:64, bx:bx + 1], C[0:64, :], ALU.add, ALU.mult, R=[colt, C], W=[ta[i], px])
                    k.stt(tb[i][0:64, :], py[0:64, :], colt[0:64, bs_:bs_ + 1], Sn[0:64, :], ALU.add, ALU.mult, R=[colt, Sn], W=[tb[i], py])
                    k.tt('pool', dst_ap, ta[i][0:64, :], tb[i][0:64, :], ALU.add, R=[ta[i], tb[i]], W=[dst_t])
                for hh in range(8):
                    roped(wDq, hh * 64, 512 + hh * 64, cq + hh, cq + 8 + hh, qn[gi], qn[gi][0:64, hh, :])
                for nm in range(2):
                    for g in range(2):
                        x0 = (nm * 2) * 128 + g * 64
                        roped(wDk, x0, x0 + 128, ck + (nm * 2) * 2 + g, ck + (nm * 2 + 1) * 2 + g, kk_[gi], kk_[gi][0:64, nm * 2 + g, :])
                for j in range(4):
                    pb = k.ps[4 + j % 2]
                    for kk in range(8):
                        k.mm(pb, pb[0:64, :], wDc[:, kk, j * 64:(j + 1) * 64], h[:, kk, :], R=[wDc, h], start=(kk == 0), stop=(kk == 7), inc=(kk == 7))
                    k.act(kcv[gi][0:64, j, :], pb[0:64, :], AF.Identity, R=[colt], W=[kcv[gi], pb], bias=colt[0:64, cc_ + j:cc_ + j + 1])
                for c in range(4):
                    i = c % 2
                    pb = k.ps[6 + i]
                    for kk in range(8):
                        k.mm(pb, pb[:, 0:280], h[:, kk, c * 128:(c + 1) * 128], wDv[:, kk, :], R=[wDv, h], start=(kk == 0), stop=(kk == 7), inc=(kk == 7))
                    k.tt('dve', vb[i][:], pb[:, 0:280], bDv[:], ALU.add, R=[bDv], W=[vb[i], pb])
                    k.cp('pool', vsw[gi][:, :, c, 0:64], vb[i][:, 0:256].rearrange("p (a b) -> p a b", a=4), R=[vb[i]], W=[vsw[gi]])
                    k.act(gate[gi][:, c, :], vb[i][:, 256:280], AF.Sigmoid, R=[vb[i]], W=[gate[gi]])
                k.dma(s.QN[:, :, G * 512:(G + 1) * 512], qn[gi][0:64, :, :], R=[qn[gi]])
                k.dma(s.KK[:, :, G * 512:(G + 1) * 512], kk_[gi][0:64, :, :], R=[kk_[gi]])
                k.dma(s.KCV[:, :, G * 512:(G + 1) * 512], kcv[gi][0:64, :, :], R=[kcv[gi]])
                k.dma(s.VSW[:, :, G * 4:(G + 1) * 4, :], vsw[gi][:], R=[vsw[gi]])
                k.dma(s.GATE[:, G * 4:(G + 1) * 4, :], gate[gi][:], R=[gate[gi]])
        return s.phase_end(f'{l}D1')

    def mixD2(s, l, bgn=0):
        k = s.k
        with ExitStack() as es:
            ident = s.loadw(es, 'ident'); negc = s.loadw(es, 'negc'); nega = s.loadw(es, 'nega')
            negcmp = s.loadw(es, 'negcmp'); ovl = s.loadw(es, 'ovl')
            id32 = s.loadcf(es, 'id32')
            CM = s.loadcf(es, 'CM'); ADD = s.loadcf(es, 'ADD')
            kcmp = k.alloc(es, (128, 2, 256), BF16, 'kcmp')
            vcmp = k.alloc(es, (128, 2, 2, 65), BF16, 'vcmp')
            onsa = k.alloc(es, (128, NT, 512), BF16, 'onsa')
            gate = k.alloc(es, (128, NT, 24), F32, 'gate')
            k.dma(gate[:], s.GATE[:], W=[gate])
            k.memset('dve', kcmp[:], 0.0, W=[kcmp])
            k.memset('dve', vcmp[:], 0.0, W=[vcmp])
            k.memset('dve', vcmp[:, :, :, 64:65], 1.0, W=[vcmp])
            with ExitStack() as e2:
                wck = s.loadw(e2, f'{l}wck'); wcks = s.loadw(e2, f'{l}wcks'); wcv = s.loadw(e2, f'{l}wcv')
                kcv = k.alloc(e2, (128, 4, S), BF16, 'kcv'); k.dma(kcv[0:64, :, :], s.KCV[:], W=[kcv])
                colt = k.alloc(e2, (128, NCOL), F32, 'col'); k.dma(colt[:], s.col[l], W=[colt])
                Cc = s.loadcf(e2, 'Ccmp'); Sc = s.loadcf(e2, 'Scmp')
                pebk = k.alloc(e2, (128, 32, 128), BF16, 'pebk'); pebv = k.alloc(e2, (128, 32, 128), BF16, 'pebv')
                ok_ = COFF['pek'][0]; ov_ = COFF['pev'][0]
                k.cp('dve', pebk[0:64, :, :], cap(colt[0:64, ok_:ok_ + 1], [1, 32], [0, 128]), R=[colt], W=[pebk])
                k.cp('dve', pebv[0:64, :, :], cap(colt[0:64, ov_:ov_ + 1], [1, 32], [0, 128]), R=[colt], W=[pebv])
                ta = k.alloc(e2, (128, 128), F32, 'ta'); tb = k.alloc(e2, (128, 128), F32, 'tb')
                for g in range(2):
                    for nt in range(2):
                        nn = 128 if nt == 0 else 127
                        n0 = nt * 128
                        px = k.ps[0]; py = k.ps[1]; pv = k.ps[2]
                        for (pp, w) in ((px, wck), (py, wcks)):
                            for ll in range(32):
                                rhs = cap(kcv[0:64, g, n0 * 16 + ll:n0 * 16 + ll + 1], [16, nn])
                                k.mm(pp, pp[0:64, 0:nn], w[0:64, ll, :], rhs, R=[w, kcv], start=(ll == 0), stop=False, inc=False)
                            for ll in range(32):
                                k.mm(pp, pp[0:64, 0:nn], w[0:64, ll, :], pebk[0:64, ll, 0:nn], R=[w, pebk],
                                     start=False, stop=(ll == 31), inc=(ll == 31))
                        k.tt('dve', ta[0:64, 0:nn], px[0:64, 0:nn], Cc[0:64, n0:n0 + nn], ALU.mult, R=[Cc], W=[ta, px])
                        k.tt('dve', tb[0:64, 0:nn], py[0:64, 0:nn], Sc[0:64, n0:n0 + nn], ALU.mult, R=[Sc], W=[tb, py])
                        k.tt('pool', kcmp[0:64, g, n0:n0 + nn], ta[0:64, 0:nn], tb[0:64, 0:nn], ALU.add, R=[ta, tb], W=[kcmp])
                        for ll in range(32):
                            lhs = cap(kcv[0:64, 2 + g, n0 * 16 + ll:n0 * 16 + ll + 1], [16, nn])
                            k.mm(pv, pv[0:nn, 0:64], lhs, wcv[0:64, ll, :], R=[wcv, kcv], start=(ll == 0), stop=False, inc=False)
                        for ll in range(32):
                            k.mm(pv, pv[0:nn, 0:64], pebv[0:64, ll, 0:nn], wcv[0:64, ll, :], R=[wcv, pebv],
                                 start=False, stop=(ll == 31), inc=(ll == 31))
                        k.cp('act', vcmp[0:nn, nt, g, 0:64], pv[0:nn, 0:64], R=[], W=[vcmp, pv])
                k.barrier()
            qa = k.alloc(es, (128, 4, S), BF16, 'qa')
            ksa = k.alloc(es, (128, S), BF16, 'ksa')
            kw = k.alloc(es, (128, S), BF16, 'kw')
            vs = k.alloc(es, (128, NT, 65), BF16, 'vs'); vw = k.alloc(es, (128, NT, 65), BF16, 'vw')
            E = [k.alloc(es, (128, 512), BF16, 'E') for _ in range(3)]
            qas = T(qa.h)
            k.dma(ksa[64:128, :], s.wsl('eexp')[0:64, :], W=[ksa])
            k.memset('dve', qa[64:128, :, :], 0.0, W=[qas])
            sm = [type('o', (), {})() for _ in range(2)]
            for o in sm:
                o.dn = k.alloc(es, (128, 4), F32, 'dn'); o.rc = k.alloc(es, (128, 12), F32, 'rc')
                o.imp = k.alloc(es, (128, 64), F32, 'imp'); o.i2 = k.alloc(es, (128, 64), F32, 'i2')
                o.m8 = k.alloc(es, (128, 8), F32, 'm8'); o.thr = k.alloc(es, (128, 1), F32, 'thr')
                o.sel = k.alloc(es, (128, 128), F32, 'sel')
                k.memset('dve', o.sel[:], 0.0, W=[o.sel])
                o.coef = k.alloc(es, (128, 3, 4), F32, 'coef')
                o.o1 = k.alloc(es, (128, 4, 64), F32, 'o1'); o.o2 = k.alloc(es, (128, 4, 64), F32, 'o2'); o.o3 = k.alloc(es, (128, 4, 64), F32, 'o3')
            ctr = [0]
            Oc = k.ps[3]; IMP = k.ps[4]; Os = k.ps[5]; Ow = k.ps[6]; pT = k.ps[7]
            bc4 = lambda t: cap(t[:, 0:1], [0, 4], [1, 128])
            v4 = lambda Sb: Sb[:, :].rearrange("p (a b) -> p a b", a=4)
            for g in range(2):
                k.dma(qa[0:64, :, :], s.QN[:, 4 * g:4 * g + 4, :], W=[qa])
                k.dma(ksa[0:64, :], s.KK[:, g, :], W=[ksa]); k.dma(kw[0:64, :], s.KK[:, 2 + g, :], W=[kw])
                k.dma(vs[:], s.VSW[:, g, :, :], W=[vs]); k.dma(vw[:], s.VSW[:, 2 + g, :, :], W=[vw])
                items = []
                O4 = lambda Ob: Ob[:, 0:260].rearrange("p (a b) -> p a b", a=4)
                gv = lambda qt, b: cap(gate[:, qt, g * 12 + b:g * 12 + b + 1], [3, 4])

                def xexp(Sb, Eb):
                    k.act(Eb[:], Sb[:, :], AF.Exp, R=[], W=[Eb, Sb], scale=0.125)

                def zero_all():
                    k.memset('dve', Oc[:, 0:260], 0.0, W=[Oc]); k.memset('dve', IMP[:, 0:256], 0.0, W=[IMP])
                    k.memset('dve', Os[:, 0:260], 0.0, W=[Os]); k.memset('dve', Ow[:, 0:260], 0.0, W=[Ow])

                def cmp_items(qt):
                    qsl = slice(qt * 128, (qt + 1) * 128)
                    o = sm[qt % 2]
                    out = []
                    for nt in ([0] if qt < 16 else [0, 1]):
                        def fS(Sb, nt=nt):
                            k.mm(Sb, v4(Sb), kcmp[0:64, g, nt * 128:(nt + 1) * 128], qa[0:64, :, qsl], R=[kcmp, qa], start=True, stop=False, inc=False)
                            k.mm(Sb, v4(Sb), ident[:], bc4(negcmp[:, nt, qsl]), R=[ident, negcmp], start=False, stop=True)

                        def fP(Eb, nt=nt):
                            for hh in range(4):
                                k.mm(Oc, Oc[:, hh * 65:(hh + 1) * 65], Eb[:, hh * 128:(hh + 1) * 128], vcmp[:, nt, g, :],
                                     R=[Eb, vcmp], start=False, stop=True, skip=True, inc=(hh == 3))
                            for hh in range(4):
                                k.mm(IMP, IMP[:, hh * 64:(hh + 1) * 64], Eb[:, hh * 128:(hh + 1) * 128], ovl[:, nt, :],
                                     R=[Eb, ovl], start=False, stop=True, skip=True, inc=(hh == 3))
                        out.append(U(fS, xexp, fP))

                    def sel1():
                        s.bg(bgn)
                        Ocv = O4(Oc)
                        k.ts('dve', o.dn[:], Ocv[:, :, 64], 1e-30, None, ALU.max, None, R=[], W=[o.dn, Oc])
                        k.op('dve', lambda g_: g_.reciprocal(out=o.rc[:, 0:4], in_=o.dn[:]), R=[o.dn], W=[o.rc])
                        k.ts('dve', o.imp[:], IMP[:, 0:64], o.rc[:, 0:1], None, ALU.mult, None, R=[o.rc], W=[o.imp, IMP])
                        for hh in range(1, 4):
                            dstb, srcb = (o.i2, o.imp) if hh % 2 == 1 else (o.imp, o.i2)
                            k.stt(dstb[:], IMP[:, hh * 64:(hh + 1) * 64], o.rc[:, hh:hh + 1], srcb[:], ALU.mult, ALU.add,
                                  R=[o.rc, srcb], W=[dstb, IMP])
                        k.tt('dve', o.imp[:], o.i2[:], CM[:, qt * 64:(qt + 1) * 64], ALU.mult, R=[o.i2, CM], W=[o.imp])
                        k.tt('dve', o.i2[:], o.imp[:], ADD[:, qt * 64:(qt + 1) * 64], ALU.add, R=[o.imp, ADD], W=[o.i2])
                        k.op('dve', lambda g_: g_.max(out=o.m8[:], in_=o.i2[:]), R=[o.i2], W=[o.m8])
                        k.ts('dve', o.thr[:], o.m8[:, 7:8], 0.0, None, ALU.max, None, R=[o.m8], W=[o.thr])
                        k.ts('dve', o.sel[:, 64:128], o.i2[:], o.thr[:, 0:1], 1.0, ALU.is_ge, ALU.subtract, R=[o.i2, o.thr], W=[o.sel])
                        k.tt('dve', o.coef[:, 0, :], gv(qt, 0), o.rc[:, 0:4], ALU.mult, R=[gate, o.rc], W=[o.coef])
                        k.tt('dve', o.o1[:], Ocv[:, :, 0:64], cap(o.coef[:, 0, 0:1], [1, 4], [0, 64]), ALU.mult, R=[o.coef], W=[o.o1, Oc])
                        k.memset('dve', Oc[:, 0:260], 0.0, W=[Oc]); k.memset('dve', IMP[:, 0:256], 0.0, W=[IMP])
                    out.append(('soft', sel1))
                    return out

                items.append(('soft', zero_all))
                items += cmp_items(0)
                for qt in range(NT):
                    o = sm[qt % 2]
                    qsl = slice(qt * 128, (qt + 1) * 128)
                    def sel2(o=o, qsl=qsl):
                        k.tr(pT, pT[:, 0:128], o.sel[:], id32[:], R=[o.sel, id32])
                        k.cp('dve', qa[64:128, :, qsl], cap(pT[64:128, 0:1], [0, 4], [1, 128]), R=[], W=[qas, pT])
                    items.append(('hard', sel2))
                    for kt in range(max(0, qt - 4), qt + 1):
                        msk = negc if kt == qt else (nega if kt == qt - 4 else None)

                        def fS(Sb, kt=kt, msk=msk, qsl=qsl):
                            k.mm(Sb, v4(Sb), kw[0:64, kt * 128:(kt + 1) * 128], qa[0:64, :, qsl], R=[kw, qa],
                                 start=True, stop=(msk is None), inc=(msk is None))
                            if msk is not None:
                                k.mm(Sb, v4(Sb), ident[:], bc4(msk), R=[ident, msk], start=False, stop=True)

                        def fP(Eb, kt=kt):
                            for hh in range(4):
                                k.mm(Ow, Ow[:, hh * 65:(hh + 1) * 65], Eb[:, hh * 128:(hh + 1) * 128], vw[:, kt, :],
                                     R=[Eb, vw], start=False, stop=True, skip=True, inc=(hh == 3))
                        items.append(U(fS, xexp, fP))

                    def combW(o=o, qt=qt):
                        Owv = O4(Ow)
                        k.op('dve', lambda g_: g_.reciprocal(out=o.rc[:, 8:12], in_=Owv[:, :, 64]), R=[], W=[o.rc, Ow])
                        k.tt('dve', o.coef[:, 2, :], gv(qt, 2), o.rc[:, 8:12], ALU.mult, R=[gate, o.rc], W=[o.coef])
                        k.tt('dve', o.o3[:], Owv[:, :, 0:64], cap(o.coef[:, 2, 0:1], [1, 4], [0, 64]), ALU.mult, R=[o.coef], W=[o.o3, Ow])
                        k.memset('dve', Ow[:, 0:260], 0.0, W=[Ow])
                    items.append(('soft', combW))

                    for kt in range(qt + 1):
                        def fS(Sb, kt=kt, qt=qt, qsl=qsl):
                            k.mm(Sb, v4(Sb), ksa[:, kt * 128:(kt + 1) * 128], qa[:, :, qsl], R=[ksa, qa, qas],
                                 start=True, stop=(kt != qt), inc=(kt != qt))
                            if kt == qt:
                                k.mm(Sb, v4(Sb), ident[:], bc4(negc), R=[ident, negc], start=False, stop=True)

                        def fP(Eb, kt=kt):
                            for hh in range(4):
                                k.mm(Os, Os[:, hh * 65:(hh + 1) * 65], Eb[:, hh * 128:(hh + 1) * 128], vs[:, kt, :],
                                     R=[Eb, vs], start=False, stop=True, skip=True, inc=(hh == 3))
                        items.append(U(fS, xexp, fP))
                        if kt == 0 and qt + 1 < NT:
                            items += cmp_items(qt + 1)

                    def combS(o=o, qt=qt):
                        Osv = O4(Os)
                        k.op('dve', lambda g_: g_.reciprocal(out=o.rc[:, 4:8], in_=Osv[:, :, 64]), R=[], W=[o.rc, Os])
                        k.tt('dve', o.coef[:, 1, :], gv(qt, 1), o.rc[:, 4:8], ALU.mult, R=[gate, o.rc], W=[o.coef])
                        k.tt('dve', o.o2[:], Osv[:, :, 0:64], cap(o.coef[:, 1, 0:1], [1, 4], [0, 64]), ALU.mult, R=[o.coef], W=[o.o2, Os])
                        k.memset('dve', Os[:, 0:260], 0.0, W=[Os])
                        k.tt('pool', o.o1[:], o.o1[:], o.o2[:], ALU.add, R=[o.o1, o.o2], W=[o.o1])
                        k.tt('pool', onsa[:, qt, g * 256:(g + 1) * 256].rearrange("p (a b) -> p a b", a=4), o.o1[:], o.o3[:], ALU.add,
                             R=[o.o1, o.o3], W=[onsa])
                    items.append(('soft', combS))
                run_items(k, items, [0, 1, 2], E, ctr)
            s.otm_to_fm(es, onsa, s.PD, ident, 0)
        return s.phase_end(f'{l}D2')

    def merge1(s, l):
        k = s.k
        PS = [s.PA, s.PB, s.PC, s.PD]
        for half in range(2):
            with ExitStack() as es:
                wG = []; wos = []
                for b, c in enumerate('ABCD'):
                    t = k.alloc(es, (128, 8, 512), BF16, 'wG')
                    k.dma(t[:], s.wsl(f'{l}wG')[:, :, b * 1024 + half * 512:b * 1024 + (half + 1) * 512], W=[t]); wG.append(t)
                    t = k.alloc(es, (128, 4, 512), BF16, 'wos')
                    k.dma(t[:], s.wsl(f'{l}wo{c}')[:, :, half * 512:(half + 1) * 512], W=[t]); wos.append(t)
                colt = k.alloc(es, (128, NCOL), F32, 'col'); k.dma(colt[:], s.col[l], W=[colt])
                cg = COFF['bG'][0]
                hT = [k.alloc(es, (128, 8, 512), BF16, 'hT') for _ in range(2)]
                Pin = [[k.alloc(es, (128, 4, 512), BF16, 'Pin') for _ in range(4)] for _ in range(2)]
                mT = [k.alloc(es, (128, 4, 512), BF16, 'mT') for _ in range(2)]
                sig = [k.alloc(es, (128, 512), F32, 'sig') for _ in range(2)]
                acc = [k.alloc(es, (128, 512), F32, 'acc') for _ in range(2)]
                tm = [k.alloc(es, (128, 512), F32, 'tm') for _ in range(2)]

                def load(G):
                    i = G % 2
                    k.dma(hT[i][:], s.XT[:, :, G * 512:(G + 1) * 512], W=[hT[i]])
                    for b in range(4):
                        k.dma(Pin[i][b][:], PS[b][:, :, G * 512:(G + 1) * 512], W=[Pin[i][b]])
                load(0)
                n = 0
                for G in range(8):
                    if G + 1 < 8: load(G + 1)
                    gi = G % 2
                    h = hT[gi]; m = mT[gi]
                    for dcl in range(4):
                        dc = half * 4 + dcl
                        a = acc[dcl % 2]
                        for b in range(4):
                            i = n % 2; n += 1
                            pg = k.ps[i]; py = k.ps[2 + i]
                            for kk in range(8):
                                k.mm(pg, pg[:, :], wG[b][:, kk, dcl * 128:(dcl + 1) * 128], h[:, kk, :], R=[wG[b], h], start=(kk == 0), stop=(kk == 7), inc=(kk == 7))
                            for fc in range(4):
                                k.mm(py, py[:, :], wos[b][:, fc, dcl * 128:(dcl + 1) * 128], Pin[gi][b][:, fc, :], R=[wos[b], Pin[gi][b]],
                                     start=(fc == 0), stop=(fc == 3), inc=(fc == 3))
                            k.act(sig[i][:], pg[:, :], AF.Sigmoid, R=[colt], W=[sig[i], pg], bias=colt[:, cg + b * 8 + dc:cg + b * 8 + dc + 1])
                            if b == 0:
                                k.tt('dve', a[:], sig[i][:], py[:, :], ALU.mult, R=[sig[i]], W=[a, py])
                            else:
                                k.tt('dve', tm[i][:], sig[i][:], py[:, :], ALU.mult, R=[sig[i]], W=[tm[i], py])
                                if b < 3:
                                    k.tt('pool', a[:], a[:], tm[i][:], ALU.add, R=[a, tm[i]], W=[a])
                                else:
                                    k.tt('pool', m[:, dcl, :], a[:], tm[i][:], ALU.add, R=[a, tm[i]], W=[m])
                    k.dma(s.MT[:, half * 4:(half + 1) * 4, G * 512:(G + 1) * 512], m[:], R=[m])
            s.k.barrier()
        return s.phase_end(f'{l}M1')

    def merge2(s, l):
        k = s.k
        with ExitStack() as es:
            wo = s.loadw(es, f'{l}wo'); ident = s.loadw(es, 'ident')
            ln = s.ln_setup(es, l, 2)
            mT = [k.alloc(es, (128, 8, 512), BF16, 'mT') for _ in range(2)]
            xr = [k.alloc(es, (128, D), F32, 'xr') for _ in range(4)]
            pre = [k.alloc(es, (128, D), F32, 'pre') for _ in range(4)]

            def load(G):
                k.dma(mT[G % 2][:], s.MT[:, :, G * 512:(G + 1) * 512], W=[mT[G % 2]])

            def ldx(tt):
                if tt < NT: k.dma(xr[tt % 4][:], s.XR[tt * 128:(tt + 1) * 128, :], W=[xr[tt % 4]])
            load(0); ldx(0); ldx(1)
            for G in range(8):
                if G + 1 < 8: load(G + 1)
                m = mT[G % 2]
                for t4 in range(4):
                    tt = G * 4 + t4
                    xx = xr[tt % 4]; pp = pre[tt % 4]
                    ldx(tt + 2)
                    for dh in range(2):
                        pb = k.ps[(tt % 2) * 2 + dh]
                        for kk in range(8):
                            k.mm(pb, pb[:, :], m[:, kk, t4 * 128:(t4 + 1) * 128], wo[:, kk, dh * 512:(dh + 1) * 512], R=[m, wo],
                                 start=(kk == 0), stop=(kk == 7), inc=(kk == 7))
                        k.stt(pp[:, dh * 512:(dh + 1) * 512], xx[:, dh * 512:(dh + 1) * 512], ALPHA, pb[:, :], ALU.mult, ALU.add,
                              R=[xx], W=[pp, pb])
                    s.ln_tile(ln, pp, tt, s.XR, 4 + tt % 2, ident, True)
            s.flush(0)
        return s.phase_end(f'{l}M2')

    def xattn(s, l):
        k = s.k
        sc = float(128 ** -0.5)
        GT = 256
        with ExitStack() as es:
            xq = s.loadw(es, f'{l}xq'); xk = s.loadw(es, f'{l}xk'); xv = s.loadw(es, f'{l}xv'); xo = s.loadw(es, f'{l}xo')
            ident = s.loadw(es, 'ident')
            ln = s.ln_setup(es, l, 3)
            memT = k.alloc(es, (128, 8, MEM), BF16, 'memT')
            kT = k.alloc(es, (128, 4, MEM), BF16, 'kT')
            va = k.alloc(es, (128, 2, 4, 129), BF16, 'va')
            k.memset('dve', va[:, :, :, 128:129], 1.0, W=[va])
            with ExitStack() as e2:
                mf = k.alloc(e2, (128, D), F32, 'mf'); mb = k.alloc(e2, (128, D), BF16, 'mb')
                for mt in range(2):
                    k.dma(mf[:], s.mem_in[mt * 128:(mt + 1) * 128, :], W=[mf])
                    k.cp('dve', mb[:], mf[:], R=[mf], W=[mb])
                    pt = k.ps[mt]; ptv = pt[:].bitcast(BF16)
                    for j in range(8):
                        k.tr(pt, ptv[:, j * 128:(j + 1) * 128], mb[:, j * 128:(j + 1) * 128], ident[:], R=[mb, ident], inc=(j == 7))
                    k.cp('act', memT[:, :, mt * 128:(mt + 1) * 128], ptv[:, :].rearrange("p (a b) -> p a b", a=8), R=[], W=[memT, pt])
                for hh in range(4):
                    pb = k.ps[2 + hh % 2]
                    for kk in range(8):
                        k.mm(pb, pb[:, 0:MEM], xk[:, kk, hh * 128:(hh + 1) * 128], memT[:, kk, :], R=[xk, memT], start=(kk == 0), stop=(kk == 7), inc=(kk == 7))
                    k.cp('act', kT[:, hh, :], pb[:, 0:MEM], R=[], W=[kT, pb])
                for mt in range(2):
                    pb = k.ps[4 + mt]
                    for kk in range(8):
                        k.mm(pb, pb[:, :], memT[:, kk, mt * 128:(mt + 1) * 128], xv[:, kk, :], R=[xv, memT], start=(kk == 0), stop=(kk == 7), inc=(kk == 7))
                    k.cp('act', va[:, mt, :, 0:128], pb[:, :].rearrange("p (a b) -> p a b", a=4), R=[], W=[va, pb])
                k.barrier()
            xT = [k.alloc(es, (128, 8, GT), BF16, 'xT') for _ in range(2)]
            qT = [k.alloc(es, (128, 4, GT), BF16, 'qT') for _ in range(2)]
            E = [k.alloc(es, (128, GT), BF16, 'E') for _ in range(3)]
            otm = [k.alloc(es, (128, 2, 512), BF16, 'otm') for _ in range(2)]
            oT = [k.alloc(es, (128, 4, GT), BF16, 'oT') for _ in range(2)]
            rc = [k.alloc(es, (128, 2), F32, 'rc') for _ in range(2)]
            xr = [k.alloc(es, (128, D), F32, 'xr') for _ in range(4)]
            pre = [k.alloc(es, (128, D), F32, 'pre') for _ in range(4)]

            def load(G):
                k.dma(xT[G % 2][:], s.XT[:, :, G * GT:(G + 1) * GT], W=[xT[G % 2]])

            def ldx(tt):
                if tt < NT: k.dma(xr[tt % 4][:], s.XR[tt * 128:(tt + 1) * 128, :], W=[xr[tt % 4]])
            cn = {'ne': 0, 'nh': 0}; xctr = [0]
            NGX = S // GT

            def stage1(G):
                gi = G % 2
                x = xT[gi]; q = qT[gi]; ot = otm[gi]
                for hh in range(4):
                    pb = k.ps[hh % 2]
                    for kk in range(8):
                        k.mm(pb, pb[:, 0:GT], xq[:, kk, hh * 128:(hh + 1) * 128], x[:, kk, :], R=[xq, x], start=(kk == 0), stop=(kk == 7), inc=(kk == 7))
                    k.cp('dve', q[:, hh, :], pb[:, 0:GT], R=[], W=[q, pb])
                items = []
                for hh in range(4):
                    O = k.ps[2 + cn['nh'] % 2]; r = rc[cn['nh'] % 2]; cn['nh'] += 1
                    items.append(('soft', lambda O=O: k.memset('dve', O[:, 0:258], 0.0, W=[O])))
                    for mt in range(2):
                        def fS(Sb, hh=hh, mt=mt):
                            k.mm(Sb, Sb[:, 0:GT], kT[:, hh, mt * 128:(mt + 1) * 128], q[:, hh, :], R=[kT, q])

                        def fX(Sb, Eb):
                            k.act(Eb[:], Sb[:, 0:GT], AF.Exp, R=[], W=[Eb, Sb], scale=sc)

                        def fP(Eb, hh=hh, mt=mt, O=O):
                            for qs in range(2):
                                k.mm(O, O[:, qs * 129:(qs + 1) * 129], Eb[:, qs * 128:(qs + 1) * 128], va[:, mt, hh, :], R=[Eb, va],
                                     start=False, stop=True, skip=True, inc=(qs == 1))
                        items.append(U(fS, fX, fP))

                    def fin(O=O, r=r, hh=hh):
                        Ov = O[:, 0:258].rearrange("p (a b) -> p a b", a=2)
                        k.op('dve', lambda g_: g_.reciprocal(out=r[:], in_=Ov[:, :, 128]), R=[], W=[r, O])
                        k.tt('dve', ot[:, :, hh * 128:(hh + 1) * 128], Ov[:, :, 0:128], cap(r[:, 0:1], [1, 2], [0, 128]), ALU.mult, R=[r], W=[ot, O])
                    items.append(('soft', fin))
                run_items(k, items, [4, 5], E, xctr)

            def stage2(G):
                gi = G % 2
                ot = otm[gi]; o_T = oT[gi]
                for qs in range(2):
                    pt = k.ps[6]; ptv = pt[:].bitcast(BF16)
                    for j in range(4):
                        k.tr(pt, ptv[:, j * 128:(j + 1) * 128], ot[:, qs, j * 128:(j + 1) * 128], ident[:], R=[ot, ident], inc=(j == 3))
                    k.cp('act', o_T[:, :, qs * 128:(qs + 1) * 128], ptv[:, 0:512].rearrange("p (a b) -> p a b", a=4), R=[], W=[o_T, pt])
                for qs in range(2):
                    tt = G * 2 + qs
                    xx = xr[tt % 4]; pp = pre[tt % 4]
                    ldx(tt + 2)
                    for dh in range(2):
                        pb = k.ps[dh]
                        for kk in range(4):
                            k.mm(pb, pb[:, :], o_T[:, kk, qs * 128:(qs + 1) * 128], xo[:, kk, dh * 512:(dh + 1) * 512], R=[o_T, xo],
                                 start=(kk == 0), stop=(kk == 3), inc=(kk == 3))
                        k.stt(pp[:, dh * 512:(dh + 1) * 512], xx[:, dh * 512:(dh + 1) * 512], ALPHA, pb[:, :], ALU.mult, ALU.add,
                              R=[xx], W=[pp, pb])
                    s.ln_tile(ln, pp, tt, s.XR, 7, ident, True)

            load(0); load(1); ldx(0); ldx(1)
            stage1(0)
            for G in range(NGX):
                if G + 1 < NGX:
                    stage1(G + 1)
                    if G + 2 < NGX: load(G + 2)
                stage2(G)
            s.flush(0)
        return s.phase_end(f'{l}X')

    def build(s):
        s.prep()
        s.xt0()
        endL0 = WOFF['0f2w2'][0] + WOFF['0f2w2'][1]
        for l in range(L):
            src = s.x_in if l == 0 else s.XR
            first = (l == 0)
            if first:
                pass
            stop = s._ffn_bg(l, 'f1', 1, src, s.XR, True, 6 if first else 0, endL0 if first else None)
            if stop: break
            if s.mixA(l): break
            if s.mixB(l): break
            if s.mixC1(l): break
            if s.mixC2(l, bgn=1 if first else 0): break
            if s.mixD1(l): break
            if s.mixD2(l, bgn=2 if first else 0): break
            if s.merge1(l): break
            if s.merge2(l): break
            if s.xattn(l): break
            last = (l == L - 1)
            if s._ffn_bg(l, 'f2', 4, s.XR, s.out if (last and not s.dbg) else s.XR, not last, 2 if first else 0, NWB if first else None): break
        s.k.barrier()
        s.k.es.close()

    def _ffn_bg(s, l, f, lni, src, dst, write_xt, bgn, until):
        s._bg_until_col = until
        return s.ffn(l, f, lni, src, dst, write_xt, bgn)


_CACHE = {}


def _get_nc(dbg=False, stop_after=None):
    key = (dbg, stop_after)
    if key not in _CACHE:
        nc = bass.Bass("TRN2", target_bir_lowering=False)
        Prog(nc, dbg, stop_after).build()
        _CACHE[key] = nc
    return _CACHE[key]


def _inmaps(inputs):
    wb, col, row, cf = _pack(inputs)
    x = np.asarray(inputs['x'], dtype=np.float32); mem = np.asarray(inputs['mem'], dtype=np.float32)
    return [{"x": np.ascontiguousarray(x[c]), "mem": np.ascontiguousarray(mem[c]), "wbig": wb, "col": col, "row": row, "cf": cf}
            for c in range(8)]


def kernel(**inputs):
    nc = _get_nc()
    res = run_bass_kernel_spmd(nc, _inmaps(inputs), core_ids=list(range(8)))
    return np.stack([np.asarray(r["out"], dtype=np.float32) for r in res.results], axis=0)
```
